# Optimizing a Trainium2 kernel written in Bass

```python
import math
import jax
import jax.numpy as jnp
from jax import lax
import numpy as np

D_MODEL = 1024
BATCH = 32
SEQ = 256
DEPTH = 4
DEC_BATCH = 8
DEC_SEQ = 2048
PAST_LEN = 512

GRID_W = 64
D_MIX = D_MODEL
GROUP_W = D_MIX // 4
N_MOD = 9
D_FF = 2816
EPS = 1e-6
ROPE_BASE = 10000.0
Q_BLOCK = 128
H_A = 4
DK_A = GROUP_W // H_A
DV_A = GROUP_W // H_A
A_WK = H_A * DK_A
A_WV = H_A * DV_A
GLA_CHUNK = 16
MAX_INPUT_KEY = 1.0 - 1e-6
H_B = 4
NOPE_B = 64
ROPE_B = 32
V_B = GROUP_W // H_B
Q_LORA = 256
KV_LORA = 128
W_C = GROUP_W
HY_ORDER = 2
HY_BANDS = 8
HY_EMB = 1 + 2 * HY_BANDS
HY_FH = 64
SHORT_K = 3
HY_MIN_DECAY = 3.07
HY_MAX_DECAY = 15.35
H_D = 4
DV_D = GROUP_W // H_D
DH_D = DV_D // 2
D_WQK = H_D * 2 * DH_D
D_WV = H_D * DV_D
IN_W = 3 * A_WK + 2 * A_WV + Q_LORA + KV_LORA + ROPE_B + 3 * W_C + 2 * D_WQK + D_WV

kernel_name = 'hybrid_dit_hgrn2_mla_hyena_diffattn_step'


def rmsnorm(x, g):
    xf = x.astype(jnp.float32)
    y = xf * lax.rsqrt(jnp.mean(xf * xf, axis=-1, keepdims=True) + EPS)
    return (y * g.astype(jnp.float32)).astype(x.dtype)


def swiglu(h, w_gu, w_down):
    a, u = jnp.split(h @ w_gu, 2, axis=-1)
    return (jax.nn.silu(a) * u) @ w_down


def axial_angles(n_tok, rot_dim):
    rows = n_tok // GRID_W
    r, col = jnp.meshgrid(jnp.arange(rows), jnp.arange(GRID_W), indexing='ij')
    n_f = rot_dim // 4
    inv = ROPE_BASE ** (-jnp.arange(n_f, dtype=jnp.float32) / n_f)
    ang_r = r.reshape(-1).astype(jnp.float32)[:, None] * inv
    ang_c = col.reshape(-1).astype(jnp.float32)[:, None] * inv
    return (ang_r, ang_c)


def rope_1d(x, ang):
    x1, x2 = jnp.split(x, 2, axis=-1)
    cos = jnp.cos(ang)[None, :, None, :].astype(x.dtype)
    sin = jnp.sin(ang)[None, :, None, :].astype(x.dtype)
    return jnp.concatenate([x1 * cos - x2 * sin, x1 * sin + x2 * cos], axis=-1)


def axial_rope(x, angs):
    xr, xc = jnp.split(x, 2, axis=-1)
    return jnp.concatenate([rope_1d(xr, angs[0]), rope_1d(xc, angs[1])], axis=-1)


def block_map(fn, q):
    b, lq = q.shape[:2]
    nb = lq // Q_BLOCK
    qb = jnp.moveaxis(q.reshape((b, nb, Q_BLOCK) + q.shape[2:]), 1, 0)
    out = lax.map(fn, qb)
    return jnp.moveaxis(out, 0, 1).reshape((b, lq) + out.shape[3:])


def gla_chunk(q, k, v, log_f, s0):
    b, n, h, dk = q.shape
    dv = v.shape[-1]
    nc = n // GLA_CHUNK
    r = lambda t: t.reshape(b, nc, GLA_CHUNK, h, t.shape[-1])
    q, k, v, g = r(q), r(k), r(v), r(log_f)
    cum = jnp.cumsum(g, axis=2)
    causal = jnp.tril(jnp.ones((GLA_CHUNK, GLA_CHUNK), dtype=bool))[None, None, :, :, None, None]
    diff = cum[:, :, :, None] - cum[:, :, None, :]
    decay = jnp.where(causal, jnp.exp(jnp.minimum(diff, 0.0)), 0.0)
    attn = jnp.einsum('bnthd,bnshd,bntshd->bnhts', q, k, decay)
    o_intra = jnp.einsum('bnhts,bnshv->bnthv', attn, v)
    last = cum[:, :, -1]
    upd = jnp.einsum('bnchk,bnchv->bnhkv', k * jnp.exp(last[:, :, None] - cum), v)

    def step(s, xs):
        a, u = xs
        return a[..., None] * s + u, s

    s_fin, s_prev = lax.scan(step, s0, (jnp.moveaxis(jnp.exp(last), 1, 0), jnp.moveaxis(upd, 1, 0)))
    o_inter = jnp.einsum('bnchk,nbhkv->bnchv', q * jnp.exp(cum), s_prev)
    return (o_intra + o_inter).reshape(b, n, h, dv), s_fin


def hgrn_forget(logit, lb):
    x = logit.astype(jnp.float32)
    lbf = lb.astype(jnp.float32).reshape(H_A, DK_A)
    k = jnp.minimum((1.0 - lbf) * jax.nn.sigmoid(-x), MAX_INPUT_KEY)
    log_f = jnp.log1p(-k)
    return log_f, k


def hgrn_mixer(pq, pi, pff, pfb, pog, lb_f, lb_b, onorm_g, s0):
    b, n, _ = pq.shape
    hv = lambda t, d: t.reshape(b, n, H_A, d)
    q = hv(jax.nn.silu(pq), DK_A).astype(jnp.float32) * DK_A ** -0.5
    v = hv(pi, DV_A).astype(jnp.float32)
    logf_f, k_f = hgrn_forget(hv(pff, DK_A), lb_f)
    logf_b, k_b = hgrn_forget(hv(pfb, DK_A), lb_b)
    s0 = s0.astype(jnp.float32)
    o_f, s_f = gla_chunk(q, k_f, v, logf_f, s0[:, 0])
    rev = lambda t: jnp.flip(t, axis=1)
    o_b, s_b = gla_chunk(rev(q), rev(k_b), rev(v), rev(logf_b), s0[:, 1])
    o = rmsnorm(o_f + rev(o_b), onorm_g) * jax.nn.silu(hv(pog, DV_A).astype(jnp.float32))
    return o.reshape(b, n, A_WV).astype(pq.dtype), jnp.stack([s_f, s_b], axis=1)


def mla_queries(c_q, w_uq, qn_g, angs):
    b, n, _ = c_q.shape
    q = rmsnorm((c_q @ w_uq).reshape(b, n, H_B, NOPE_B + ROPE_B), qn_g)
    if angs is not None:
        q = jnp.concatenate([q[..., :NOPE_B], axial_rope(q[..., NOPE_B:], angs)], axis=-1)
    return q


def mla_keys(c_kv, k_rope, w_ukv, kn_g, angs):
    b, n, _ = c_kv.shape
    kv = (c_kv @ w_ukv).reshape(b, n, H_B, NOPE_B + V_B)
    k = jnp.concatenate([kv[..., :NOPE_B], jnp.broadcast_to(k_rope[:, :, None, :], (b, n, H_B, ROPE_B))], axis=-1)
    k = rmsnorm(k, kn_g)
    if angs is not None:
        k = jnp.concatenate([k[..., :NOPE_B], axial_rope(k[..., NOPE_B:], angs)], axis=-1)
    return k, kv[..., NOPE_B:]


def mla_attend(q, k, v):
    scale = (NOPE_B + ROPE_B) ** -0.5

    def blk(qb):
        s = jnp.einsum('bqhd,bkhd->bhqk', qb, k).astype(jnp.float32) * scale
        p = jax.nn.softmax(s, axis=-1).astype(v.dtype)
        return jnp.einsum('bhqk,bkhd->bqhd', p, v)

    return block_map(blk, q)


def short_conv(x, w):
    pad = SHORT_K // 2
    n = x.shape[1]
    xp = jnp.pad(x, ((0, 0), (pad, pad), (0, 0)))
    return sum(xp[:, j:j + n] * w[j] for j in range(SHORT_K))


def hyena_filters(n, w1, b1, w2, b2, w3, log_decay):
    f32 = jnp.float32
    tn = jnp.arange(n, dtype=f32) / n
    ang = 2.0 * math.pi * tn[:, None] * jnp.arange(1, HY_BANDS + 1, dtype=f32)
    feats = jnp.concatenate([tn[:, None], jnp.cos(ang), jnp.sin(ang)], axis=-1)
    h = jnp.sin(feats @ w1.astype(f32) + b1.astype(f32))
    h = jnp.sin(h @ w2.astype(f32) + b2.astype(f32))
    h = (h @ w3.astype(f32)).reshape(n, 2, HY_ORDER, W_C)
    h = h * jnp.exp(-jnp.exp(log_decay.astype(f32))[None] * tn[:, None, None, None])
    h_f, h_b = h[:, 0], h[:, 1]
    filt = jnp.concatenate([h_f, jnp.zeros((1, HY_ORDER, W_C), f32), jnp.flip(h_b[1:], axis=0)], axis=0)
    return filt / (jnp.sum(jnp.abs(filt), axis=0, keepdims=True) + EPS)


def long_conv(u, filt, bias):
    n = u.shape[1]
    uf = jnp.fft.rfft(u, n=2 * n, axis=1)
    ff = jnp.fft.rfft(filt, n=2 * n, axis=0)
    y = jnp.fft.irfft(uf * ff[None], n=2 * n, axis=1)[:, :n]
    return y + u * bias.astype(jnp.float32)


def hyena_mixer(pv, px1, px2, w_short, w1, b1, w2, b2, w3, log_decay, bias):
    n = pv.shape[1]
    u = short_conv(jnp.concatenate([pv, px1, px2], axis=-1), w_short).astype(jnp.float32)
    v, x1, x2 = jnp.split(u, 3, axis=-1)
    filt = hyena_filters(n, w1, b1, w2, b2, w3, log_decay)
    z = x1 * long_conv(v, filt[:, 0], bias[0])
    z = x2 * long_conv(z, filt[:, 1], bias[1])
    return z.astype(pv.dtype)


def diff_heads(p, g, angs):
    b, n, _ = p.shape
    x = rmsnorm(p.reshape(b, n, H_D, 2, DH_D), g)
    if angs is not None:
        x = axial_rope(x.reshape(b, n, H_D * 2, DH_D), angs).reshape(b, n, H_D, 2, DH_D)
    return x


def diff_attend(q, k, v, lam, lam_init, sub_g):
    scale = DH_D ** -0.5

    def blk(qb):
        s = jnp.einsum('bqhmd,bkhmd->bmhqk', qb, k).astype(jnp.float32) * scale
        p = jax.nn.softmax(s, axis=-1)
        w = (p[:, 0] - lam * p[:, 1]).astype(v.dtype)
        return jnp.einsum('bhqk,bkhv->bqhv', w, v)

    return rmsnorm(block_map(blk, q), sub_g) * (1.0 - lam_init)


def mixing(h, l, P, lb, angs, ctx):
    b, n, _ = h.shape
    sizes = [A_WK, A_WV, A_WK, A_WK, A_WV, Q_LORA, KV_LORA, ROPE_B, W_C, W_C, W_C, D_WQK, D_WQK, D_WV]
    idx = np.cumsum(sizes)[:-1].tolist()
    (a_q, a_i, a_ff, a_fb, a_og, b_cq, b_ckv, b_kr,
     c_v, c_x1, c_x2, d_q, d_k, d_v) = jnp.split(h @ P['w_in'][l], idx, axis=-1)
    angs_b, angs_d = (None, None) if angs is None else angs
    s0 = jnp.zeros((b, 2, H_A, DK_A, DV_A), jnp.float32) if ctx is None else ctx['hgrn']
    o_a, s_hgrn = hgrn_mixer(a_q, a_i, a_ff, a_fb, a_og, lb[0, l], lb[1, l], P['hgrn_onorm'][l], s0)
    c_q = rmsnorm(b_cq, P['mla_q_norm'][l])
    c_kv = rmsnorm(b_ckv, P['mla_kv_norm'][l])
    q_b = mla_queries(c_q, P['mla_w_uq'][l], P['mla_qk_norm'][l, 0], angs_b)
    k_b, v_b = mla_keys(c_kv, b_kr, P['mla_w_ukv'][l], P['mla_qk_norm'][l, 1], angs_b)
    if ctx is not None:
        k_c, v_c = mla_keys(ctx['mla'][..., :KV_LORA], ctx['mla'][..., KV_LORA:], P['mla_w_ukv'][l], P['mla_qk_norm'][l, 1], None)
        k_b = jnp.concatenate([k_c, k_b], axis=1)
        v_b = jnp.concatenate([v_c, v_b], axis=1)
    o_b = mla_attend(q_b, k_b, v_b).reshape(b, n, H_B * V_B)
    o_c = hyena_mixer(c_v, c_x1, c_x2, P['hy_short'][l], P['hy_w1'][l], P['hy_b1'][l], P['hy_w2'][l],
                      P['hy_b2'][l], P['hy_w3'][l], P['hy_log_decay'][l], P['hy_bias'][l])
    lam_init = 0.8 - 0.6 * math.exp(-0.3 * l)
    dl = P['diff_lambda'][l].astype(jnp.float32)
    lam = jnp.exp(jnp.sum(dl[0] * dl[1])) - jnp.exp(jnp.sum(dl[2] * dl[3])) + lam_init
    q_d = diff_heads(d_q, P['diff_qk_norm'][l, 0], angs_d)
    k_d = diff_heads(d_k, P['diff_qk_norm'][l, 1], angs_d)
    v_d = d_v.reshape(b, n, H_D, DV_D)
    k_all, v_all = k_d, v_d
    if ctx is not None:
        k_all = jnp.concatenate([ctx['dk'], k_d], axis=1)
        v_all = jnp.concatenate([ctx['dv'], v_d], axis=1)
    o_d = diff_attend(q_d, k_all, v_all, lam, lam_init, P['diff_subln'][l]).reshape(b, n, D_WV)
    out = jnp.concatenate([o_a, o_b, o_c, o_d], axis=-1) @ P['w_out'][l]
    ctx_out = (jnp.concatenate([c_kv, b_kr], axis=-1), k_d, v_d, s_hgrn)
    return out, ctx_out


def layer(x, cvec, l, P, lb, angs, ctx):
    m = jax.nn.silu(cvec) @ P['w_mod'][l] + P['b_mod'][l]
    sh1, sc1, g1, sh2, sc2, g2, sh3, sc3, g3 = jnp.split(m, N_MOD, axis=-1)
    h = rmsnorm(x, P['norm_g'][l, 0]) * (1 + sc1) + sh1
    x = x + 0.5 * g1 * swiglu(h, P['ffn_w_gu'][l, 0], P['ffn_w_down'][l, 0])
    h = rmsnorm(x, P['norm_g'][l, 1]) * (1 + sc2) + sh2
    o, ctx_out = mixing(h, l, P, lb, angs, ctx)
    x = x + g2 * o
    h = rmsnorm(x, P['norm_g'][l, 2]) * (1 + sc3) + sh3
    x = x + 0.5 * g3 * swiglu(h, P['ffn_w_gu'][l, 1], P['ffn_w_down'][l, 1])
    return x, ctx_out


def setup_inputs(seed: int = 0) -> dict:
    key = jax.random.key(seed)
    ks = iter(jax.random.split(key, 40))
    f32 = jnp.float32
    nrm = lambda shape, s: s * jax.random.normal(next(ks), shape, f32)
    return {
        'x_prompt': nrm((BATCH, SEQ, D_MODEL), 1.0),
        'x_sample': nrm((DEC_BATCH, DEC_SEQ, D_MODEL), 1.0),
        'c': nrm((DEC_BATCH, D_MODEL), 1.0),
        'cache_mla': nrm((DEC_BATCH, DEPTH, PAST_LEN, KV_LORA + ROPE_B), 1.0),
        'cache_diff_k': nrm((DEC_BATCH, DEPTH, PAST_LEN, H_D, 2, DH_D), 1.0),
        'cache_diff_v': nrm((DEC_BATCH, DEPTH, PAST_LEN, H_D, DV_D), 1.0),
        'state_hgrn': nrm((DEC_BATCH, DEPTH, 2, H_A, DK_A, DV_A), 0.5),
        'c_ctx': nrm((D_MODEL,), 1.0),
        'w_mod': nrm((DEPTH, D_MODEL, N_MOD * D_MODEL), 0.5 * D_MODEL ** -0.5),
        'b_mod': nrm((DEPTH, N_MOD * D_MODEL), 0.02),
        'norm_g': 1.0 + nrm((DEPTH, 3, D_MODEL), 0.02),
        'ffn_w_gu': nrm((DEPTH, 2, D_MODEL, 2 * D_FF), D_MODEL ** -0.5),
        'ffn_w_down': nrm((DEPTH, 2, D_FF, D_MODEL), D_FF ** -0.5),
        'w_in': nrm((DEPTH, D_MODEL, IN_W), D_MODEL ** -0.5),
        'w_out': nrm((DEPTH, D_MIX, D_MODEL), D_MIX ** -0.5),
        'hgrn_lb_logits': nrm((2, DEPTH, A_WK), 0.1),
        'hgrn_onorm': 1.0 + nrm((DEPTH, DV_A), 0.02),
        'mla_q_norm': 1.0 + nrm((DEPTH, Q_LORA), 0.02),
        'mla_kv_norm': 1.0 + nrm((DEPTH, KV_LORA), 0.02),
        'mla_w_uq': nrm((DEPTH, Q_LORA, H_B * (NOPE_B + ROPE_B)), Q_LORA ** -0.5),
        'mla_w_ukv': nrm((DEPTH, KV_LORA, H_B * (NOPE_B + V_B)), KV_LORA ** -0.5),
        'mla_qk_norm': 1.0 + nrm((DEPTH, 2, NOPE_B + ROPE_B), 0.02),
        'hy_short': nrm((DEPTH, SHORT_K, 3 * W_C), SHORT_K ** -0.5),
        'hy_w1': nrm((DEPTH, HY_EMB, HY_FH), 1.0),
        'hy_b1': nrm((DEPTH, HY_FH), 0.1),
        'hy_w2': nrm((DEPTH, HY_FH, HY_FH), HY_FH ** -0.5),
        'hy_b2': nrm((DEPTH, HY_FH), 0.1),
        'hy_w3': nrm((DEPTH, HY_FH, 2 * HY_ORDER * W_C), HY_FH ** -0.5),
        'hy_log_decay': jnp.log(jnp.linspace(HY_MIN_DECAY, HY_MAX_DECAY, W_C, dtype=f32)) + nrm((DEPTH, 2, HY_ORDER, W_C), 0.05),
        'hy_bias': nrm((DEPTH, HY_ORDER, W_C), 0.1),
        'diff_qk_norm': 1.0 + nrm((DEPTH, 2, DH_D), 0.02),
        'diff_lambda': nrm((DEPTH, 4, DH_D), 0.1),
        'diff_subln': 1.0 + nrm((DEPTH, DV_D), 0.02),
    }


def reference(x_prompt, x_sample, c, cache_mla, cache_diff_k, cache_diff_v, state_hgrn, c_ctx,
              w_mod, b_mod, norm_g, ffn_w_gu, ffn_w_down, w_in, w_out, hgrn_lb_logits, hgrn_onorm,
              mla_q_norm, mla_kv_norm, mla_w_uq, mla_w_ukv, mla_qk_norm, hy_short, hy_w1, hy_b1,
              hy_w2, hy_b2, hy_w3, hy_log_decay, hy_bias, diff_qk_norm, diff_lambda, diff_subln):
    P = dict(w_mod=w_mod, b_mod=b_mod, norm_g=norm_g, ffn_w_gu=ffn_w_gu, ffn_w_down=ffn_w_down,
             w_in=w_in, w_out=w_out, hgrn_onorm=hgrn_onorm, mla_q_norm=mla_q_norm,
             mla_kv_norm=mla_kv_norm, mla_w_uq=mla_w_uq, mla_w_ukv=mla_w_ukv, mla_qk_norm=mla_qk_norm,
             hy_short=hy_short, hy_w1=hy_w1, hy_b1=hy_b1, hy_w2=hy_w2, hy_b2=hy_b2, hy_w3=hy_w3,
             hy_log_decay=hy_log_decay, hy_bias=hy_bias, diff_qk_norm=diff_qk_norm,
             diff_lambda=diff_lambda, diff_subln=diff_subln)
    lb_p = jax.nn.softmax(hgrn_lb_logits.astype(jnp.float32), axis=1)
    lb = jnp.cumsum(lb_p, axis=1) - lb_p[:, :1]
    n_lat = x_sample.shape[1]
    angs = (axial_angles(n_lat, ROPE_B), axial_angles(n_lat, DH_D))
    c_lat = c[:, None, :]
    x_p, x_s = x_prompt, x_sample
    mla_l, dk_l, dv_l, hg_l = [], [], [], []
    for l in range(DEPTH):
        x_p, (m_c, k_c, v_c, s_c) = layer(x_p, c_ctx, l, P, lb, None, None)
        mla_l.append(m_c.astype(x_prompt.dtype))
        dk_l.append(k_c.astype(x_prompt.dtype))
        dv_l.append(v_c.astype(x_prompt.dtype))
        hg_l.append(s_c.astype(x_prompt.dtype))
        ctx = dict(mla=cache_mla[:, l], dk=cache_diff_k[:, l], dv=cache_diff_v[:, l], hgrn=state_hgrn[:, l])
        x_s, _ = layer(x_s, c_lat, l, P, lb, angs, ctx)
    new_cache_mla = jnp.stack(mla_l, axis=1)
    new_cache_diff_k = jnp.stack(dk_l, axis=1)
    new_cache_diff_v = jnp.stack(dv_l, axis=1)
    new_state_hgrn = jnp.stack(hg_l, axis=1)
    return (x_p, x_s, new_cache_mla, new_cache_diff_k, new_cache_diff_v, new_state_hgrn)
```

```python
from concourse.bass_utils import run_bass_kernel_spmd
import ml_dtypes
import numpy as np
from contextlib import ExitStack
import concourse.bass as bass
import concourse.mybir as mybir

F32 = mybir.dt.float32
BF16 = mybir.dt.bfloat16
I32 = mybir.dt.int32
AF = mybir.ActivationFunctionType
ALU = mybir.AluOpType
AX = mybir.AxisListType


class Res:
    __slots__ = ("w", "rs")

    def __init__(self):
        self.w = None
        self.rs = {}


class T:
    __slots__ = ("t", "res")

    def __init__(self, t, res=None):
        self.t = t
        self.res = res if res is not None else Res()

    def __getitem__(self, idx):
        return self.t[idx]


class Eng:
    def __init__(self, k, eng, name, inorder):
        self.k = k
        self.eng = eng
        self.name = name
        self.inorder = inorder
        self.sem = k.new_sem("s_" + name)
        self.semid = k.semid(self.sem)
        self.cnt = 0
        self.pending = False
        self.seen = {}
        self.dsems = None
        self.dvals = None
        self.di = 0

    def init_dma(self, ns):
        self.dsems = [self.k.new_sem("d_%s_%d" % (self.name, i)) for i in range(ns)]
        self.dids = [self.k.semid(s) for s in self.dsems]
        self.dvals = [0] * ns


class K:
    def __init__(self, nc, es):
        self.nc = nc
        self.es = es
        self._sems = {}
        self._nsem = 0
        self.pe = Eng(self, nc.tensor, "pe", True)
        self.act = Eng(self, nc.scalar, "act", False)
        self.dve = Eng(self, nc.vector, "dve", False)
        self.pool = Eng(self, nc.gpsimd, "pool", False)
        self.sp = Eng(self, nc.sync, "sp", False)
        self.engs = [self.pe, self.act, self.dve, self.pool, self.sp]
        self.sp.init_dma(16)
        self.pool.init_dma(16)
        self.act.init_dma(8)
        self.all_dma_toks = []
        self.same_engine_sync = True
        self.n_inst = 0
        self.n_wait = 0

    def new_sem(self, name):
        s = self.es.enter_context(self.nc.semaphore(name))
        self._nsem += 1
        self._sems[id(s)] = self._nsem
        return s

    def semid(self, s):
        return self._sems[id(s)]

    def need(self, E, tok):
        if tok is None:
            return
        sem, val, sid = tok
        if sid == E.semid and (E.inorder or not self.same_engine_sync):
            return
        if E.seen.get(sid, 0) >= val:
            return
        E.eng.wait_ge(sem, val)
        self.n_wait += 1
        E.seen[sid] = val

    def _deps(self, E, reads, writes):
        for r in reads:
            r = r.res if isinstance(r, T) else r
            self.need(E, r.w)
        for w in writes:
            w = w.res if isinstance(w, T) else w
            self.need(E, w.w)
            for tok in w.rs.values():
                self.need(E, tok)

    def _post(self, tok, reads, writes):
        sid = tok[2]
        for r in reads:
            r = r.res if isinstance(r, T) else r
            old = r.rs.get(sid)
            if old is None or old[1] < tok[1]:
                r.rs[sid] = tok
        for w in writes:
            w = w.res if isinstance(w, T) else w
            w.w = tok
            w.rs = {}

    def op(self, E, fn, reads=(), writes=(), signal=True):
        self._deps(E, reads, writes)
        inst = fn()
        self.n_inst += 1
        tok = (E.sem, E.cnt + 1, E.semid)
        if signal:
            inst.then_inc(E.sem, 1)
            E.cnt += 1
            E.pending = False
        else:
            E.pending = True
        self._post(tok, reads, writes)
        return inst

    def dma(self, Q, out, in_, reads=(), writes=(), **kw):
        self._deps(Q, reads, writes)
        ns = len(Q.dsems)
        slot = Q.di % ns
        Q.di += 1
        sem = Q.dsems[slot]
        sid = Q.dids[slot]
        if Q.dvals[slot] > 0:
            self.need(Q, (sem, Q.dvals[slot], sid))
        inst = Q.eng.dma_start(out=out, in_=in_, **kw)
        inst.then_inc(sem, 16)
        self.n_inst += 1
        Q.dvals[slot] += 16
        tok = (sem, Q.dvals[slot], sid)
        self._post(tok, reads, writes)
        return inst

    def barrier(self):
        toks = []
        for F in self.engs:
            assert not F.pending, F.name
            if F.cnt > 0:
                toks.append((F.sem, F.cnt, F.semid))
            if F.dsems is not None:
                for s, v, i in zip(F.dsems, F.dvals, F.dids):
                    if v > 0:
                        toks.append((s, v, i))
        for E in self.engs:
            for tok in toks:
                if tok[2] == E.semid:
                    if E.inorder:
                        continue
                self.need(E, tok)

    def finish(self):
        self.barrier()

    def sb(self, es, name, shape, dtype):
        self._uid = getattr(self, "_uid", 0) + 1
        name = "sb%d_%s" % (self._uid, name)
        t = es.enter_context(self.nc.sbuf_tensor(name, shape, dtype))
        return T(t)

    def psum(self, es, name, shape, dtype=F32):
        self._uid = getattr(self, "_uid", 0) + 1
        name = "pp%d_%s" % (self._uid, name)
        t = es.enter_context(self.nc.psum_tensor(name, shape, dtype))
        return T(t)

    def mm(self, out, lhsT, rhs, start, stop, reads, writes, signal=None, **kw):
        if signal is None:
            signal = True
        return self.op(self.pe, lambda: self.nc.tensor.matmul(out, lhsT, rhs, start=start, stop=stop, **kw),
                       reads, writes, signal=signal)

    def transpose(self, out, in_, ident, reads, writes, signal=True):
        return self.op(self.pe, lambda: self.nc.tensor.transpose(out, in_, ident), reads, writes, signal=signal)

    def activation(self, out, in_, func, reads, writes, bias=None, scale=None, accum_out=None, E=None):
        E = E or self.act
        kw = {}
        if bias is not None:
            kw["bias"] = bias
        if scale is not None:
            kw["scale"] = scale
        if accum_out is not None:
            kw["accum_out"] = accum_out
        return self.op(E, lambda: E.eng.activation(out=out, in_=in_, func=func, **kw), reads, writes)

    def tt(self, out, in0, in1, op, reads, writes, E=None):
        E = E or self.dve
        return self.op(E, lambda: E.eng.tensor_tensor(out=out, in0=in0, in1=in1, op=op), reads, writes)

    def ts(self, out, in0, s1, s2, op0, op1=None, reads=(), writes=(), E=None, accum_out=None):
        E = E or self.dve
        kw = {}
        if op1 is not None:
            kw["op1"] = op1
        if accum_out is not None:
            kw["accum_out"] = accum_out
        return self.op(E, lambda: E.eng.tensor_scalar(out=out, in0=in0, scalar1=s1, scalar2=s2, op0=op0, **kw),
                       reads, writes)

    def stt(self, out, in0, scalar, in1, op0, op1, reads, writes):
        E = self.dve
        return self.op(E, lambda: E.eng.scalar_tensor_tensor(out=out, in0=in0, scalar=scalar, in1=in1, op0=op0, op1=op1),
                       reads, writes)

    def copy(self, out, in_, reads, writes, E=None):
        E = E or self.dve
        if E is self.act:
            return self.op(E, lambda: E.eng.copy(out=out, in_=in_), reads, writes)
        return self.op(E, lambda: E.eng.tensor_copy(out=out, in_=in_), reads, writes)

    def memset(self, ap, val, writes, E=None):
        E = E or self.dve
        return self.op(E, lambda: E.eng.memset(ap, val), (), writes)

DEPTH = 4
NT = 3072
NPR = 1024
NTB = 6
DFF = 2816
NJ = 22
EPS = 1e-6
INW = 3232
NCC = 26
R_AQ, R_AI, R_AFF, R_AFB, R_AOG = 0, 256, 512, 768, 1024
R_CQ, R_CKV, R_KR = 1280, 1536, 1664
R_CV, R_CX1, R_CX2 = 1696, 1952, 2208
R_DQ, R_DK, R_DV = 2464, 2720, 2976


class Ctx:
    pass


def mvec(g, l, kind, m, cond):
    i = (kind * 8 + m) * 2 + cond
    return g.modv[l][:, i:i + 1]


def xsrc_ap(g, m, tb):
    base = g.D["yT"] if (m, tb) in g.xwritten else g.D["xT"]
    return base[m, :, tb * 512:(tb + 1) * 512]


def phase_mod(g):
    k, nc, D = g.k, g.nc, g.D
    with ExitStack() as es:
        cT = k.sb(es, "cT", [128, 16], F32)
        sc = k.sb(es, "scT", [128, 16], BF16)
        k.dma(k.sp, cT[:], D["cT"], (), [cT])
        k.activation(sc[:], cT[:], AF.Silu, [cT], [sc])
        wm = [k.sb(es, "wm%d" % i, [128, 8 * 512], BF16) for i in range(3)]
        bm = [k.sb(es, "bm%d" % i, [128, 72], F32) for i in range(2)]
        ng = [k.sb(es, "ng%d" % i, [128, 24], F32) for i in range(2)]
        wsrc = D["w_mod"]
        for l in range(DEPTH):
            b_, n_ = bm[l % 2], ng[l % 2]
            k.dma(k.sp, b_[:], D["bmodT"][l], (), [b_])
            k.dma(k.sp, n_[:], D["normgT"][l], (), [n_])
            ps = g.PS[l % 2]
            wl = wsrc[l].rearrange("(k p) c -> p k c", p=128)
            for pc in range(18):
                w = wm[pc % 3]
                k.dma(k.pool, w[:].rearrange("p (k c) -> p k c", k=8), wl[:, :, pc * 512:(pc + 1) * 512], (), [w])
                for q4 in range(4):
                    q = pc * 4 + q4
                    for kk in range(8):
                        k.mm(ps[:, q * 2:q * 2 + 2], w[:, kk * 512 + q4 * 128: kk * 512 + (q4 + 1) * 128],
                             sc[:, kk * 2:kk * 2 + 2], kk == 0, kk == 7, [w, sc], [ps])
            mv = g.modv[l]
            mv3 = mv[:].rearrange("p (q c) -> p q c", c=2)
            ps3 = ps[:, 0:144].rearrange("p (q c) -> p q c", c=2)
            for cond in range(2):
                k.tt(mv3[:, :, cond], ps3[:, :, cond], b_[:], ALU.add, [ps, b_], [mv])
            for k3 in range(3):
                q0 = (3 * k3 + 1) * 8
                for cond in range(2):
                    k.stt(mv3[:, q0:q0 + 8, cond], mv3[:, q0:q0 + 8, cond], 1.0, n_[:, k3 * 8:(k3 + 1) * 8],
                          ALU.add, ALU.mult, [mv, n_], [mv])
            for k3 in (0, 2):
                q0 = (3 * k3 + 2) * 8
                k.ts(mv[:, q0 * 2:(q0 + 8) * 2], mv[:, q0 * 2:(q0 + 8) * 2], 0.5, None, ALU.mult, reads=[mv], writes=[mv])
        k.barrier()


def phase_norm(g, l, kn, h):
    k, nc, D = g.k, g.nc, g.D
    with ExitStack() as es:
        xt = [k.sb(es, "nx%d" % i, [128, 8 * 512], F32) for i in range(2)]
        sq = [k.sb(es, "nsq%d" % i, [128, 512], BF16) for i in range(4)]
        rts = [k.sb(es, "nrt%d" % i, [128, 512], F32) for i in range(2)]
        rss = [k.sb(es, "nrs%d" % i, [128, 512], F32) for i in range(2)]
        tmp = [k.sb(es, "ntmp%d" % i, [128, 512], F32) for i in range(4)]
        for tb in range(NTB):
            cond = 0 if tb < 2 else 1
            x = xt[tb % 2]
            for m in range(8):
                k.dma(k.sp, x[:, m * 512:(m + 1) * 512], xsrc_ap(g, m, tb), [g.XR[m][tb]], [x])
            ps = g.PS[6 + tb % 2]
            rt = rts[tb % 2]
            rs = rss[tb % 2]
            for m in range(8):
                s = sq[m % 4]
                xm = x[:, m * 512:(m + 1) * 512]
                if m % 2 == 0:
                    k.tt(s[:], xm, xm, ALU.mult, [x], [s], E=k.pool)
                else:
                    k.activation(s[:], xm, AF.Square, [x], [s])
                k.mm(ps[:], g.ones_ms[:], s[:], m == 0, m == 7, [s], [ps])
            k.activation(rt[:], ps[:], AF.Sqrt, [ps], [rt], bias=g.eps_t[:, 0:1])
            k.op(k.dve, lambda: nc.vector.reciprocal(out=rs[:], in_=rt[:]), [rt], [rs])
            for m in range(8):
                t = tmp[m % 4]
                k.tt(t[:], x[:, m * 512:(m + 1) * 512], rs[:], ALU.mult, [x, rs], [t], E=(k.pool if m % 4 == 3 else k.dve))
                k.activation(h[m][:, tb * 512:(tb + 1) * 512], t[:], AF.Identity, [t], [h[m]],
                             bias=mvec(g, l, 3 * kn, m, cond), scale=mvec(g, l, 3 * kn + 1, m, cond))
        k.barrier()


def resid_update(g, l, gate_kind, m, tb, po, xo, xn):
    k, D = g.k, g.D
    cond = 0 if tb < 2 else 1
    k.dma(k.sp, xo[:], xsrc_ap(g, m, tb), [g.XR[m][tb]], [xo])
    k.stt(xn[:], po[:], mvec(g, l, gate_kind, m, cond), xo[:], ALU.mult, ALU.add, [po, xo], [xn])
    g.xwritten.add((m, tb))
    k.dma(k.pool, D["yT"][m, :, tb * 512:(tb + 1) * 512], xn[:], [xn], [g.XR[m][tb]])


def phase_ffn(g, l, w):
    k, nc, D = g.k, g.nc, g.D
    kn = 0 if w == 0 else 2
    with ExitStack() as es:
        h = [k.sb(es, "h%d" % m, [128, NT], BF16) for m in range(8)]
        phase_norm(g, l, kn, h)
        act = [k.sb(es, "a%d" % j, [128, NT], BF16) for j in range(11)]
        wt = [k.sb(es, "fw%d" % i, [128, 2048], BF16) for i in range(2)]
        sa = [k.sb(es, "fsa%d" % i, [128, 512], F32) for i in range(2)]
        wdt = [k.sb(es, "fwd%d" % i, [128, 11 * 128], BF16) for i in range(2)]
        xo = [k.sb(es, "fxo%d" % i, [128, 512], F32) for i in range(2)]
        xn = [k.sb(es, "fxn%d" % i, [128, 512], F32) for i in range(2)]
        it = 0
        it2 = 0

        def load_wgu(j):
            k.dma(k.pool, wt[j % 2][:], D["wgu"][l, w, j], (), [wt[j % 2]])

        def load_wd(half, m):
            k.dma(k.pool, wdt[m % 2][:], D["wd"][l, w, m][:, half * 1408:(half + 1) * 1408], (), [wdt[m % 2]])

        load_wgu(0)
        for half in range(2):
            for jj in range(11):
                j = half * 11 + jj
                wj = wt[j % 2]
                if jj < 10:
                    load_wgu(j + 1)
                else:
                    load_wd(half, 0)
                for tb in range(NTB):
                    pa = g.PS[(it % 2) * 2]
                    pu = g.PS[(it % 2) * 2 + 1]
                    s = sa[it % 2]
                    it += 1
                    for kk in range(8):
                        k.mm(pa[:], wj[:, kk * 128:(kk + 1) * 128], h[kk][:, tb * 512:(tb + 1) * 512],
                             kk == 0, kk == 7, [wj, h[kk]], [pa])
                    for kk in range(8):
                        k.mm(pu[:], wj[:, 1024 + kk * 128:1024 + (kk + 1) * 128], h[kk][:, tb * 512:(tb + 1) * 512],
                             kk == 0, kk == 7, [wj, h[kk]], [pu])
                    k.activation(s[:], pa[:], AF.Silu, [pa], [s])
                    k.tt(act[jj][:, tb * 512:(tb + 1) * 512], s[:], pu[:], ALU.mult, [s, pu], [act[jj]])
            for m in range(8):
                wm_ = wdt[m % 2]
                if m < 7:
                    load_wd(half, m + 1)
                elif half == 0:
                    load_wgu(11)
                for tb in range(NTB):
                    po = g.PS[4 + it2 % 2]
                    for jj in range(11):
                        k.mm(po[:], wm_[:, jj * 128:(jj + 1) * 128], act[jj][:, tb * 512:(tb + 1) * 512],
                             jj == 0, jj == 10, [wm_, act[jj]], [po])
                    resid_update(g, l, 3 * kn + 2, m, tb, po, xo[it2 % 2], xn[it2 % 2])
                    it2 += 1
        k.barrier()


def phase_proj(g, l):
    k, nc, D = g.k, g.nc, g.D
    with ExitStack() as es:
        h = [k.sb(es, "h%d" % m, [128, NT], BF16) for m in range(8)]
        phase_norm(g, l, 1, h)
        wt = [k.sb(es, "pw%d" % i, [128, 1024], BF16) for i in range(2)]
        st = [k.sb(es, "pst%d" % i, [128, NT], F32) for i in range(2)]
        wtm = k.sb(es, "pwtm", [128, 8192], BF16)
        st2 = [k.sb(es, "pst2%d" % i, [128, 1024], F32) for i in range(2)]
        for q in range(4):
            k.dma(k.pool, wtm[:, q * 2048:(q + 1) * 2048], D["wtm"][l][:, q * 2048:(q + 1) * 2048], (), [wtm])
        it = 0
        for cc in range(NCC):
            wj = wt[cc % 2]
            k.dma(k.pool, wj[:], D["win"][l, cc], (), [wj])
            s = st[cc % 2]
            for tb in range(NTB):
                ps = g.PS[it % 4]
                for kk in range(8):
                    k.mm(ps[:], wj[:, kk * 128:(kk + 1) * 128], h[kk][:, tb * 512:(tb + 1) * 512],
                         kk == 0, kk == 7, [wj, h[kk]], [ps])
                k.copy(s[:, tb * 512:(tb + 1) * 512], ps[:], [ps], [s], E=(k.act if it % 2 == 0 else k.dve))
                it += 1
            k.dma(k.sp, D["PJ"][cc], s[:], [s], [g.PJR[cc]])
        for tt in range(NT // 128):
            s2 = st2[tt % 2]
            for hf in range(2):
                ps = g.PS[4 + it % 4]
                for kk in range(8):
                    k.mm(ps[:], h[kk][:, tt * 128:(tt + 1) * 128], wtm[:, kk * 1024 + hf * 512: kk * 1024 + (hf + 1) * 512],
                         kk == 0, kk == 7, [wtm, h[kk]], [ps])
                k.copy(s2[:, hf * 512:(hf + 1) * 512], ps[:], [ps], [s2], E=(k.act if it % 2 == 0 else k.dve))
                it += 1
            k.dma(k.sp, D["PT"][tt * 128:(tt + 1) * 128, :], s2[:], [s2], [g.PTR])
        k.barrier()


def phase_out(g, l):
    k, nc, D = g.k, g.nc, g.D
    with ExitStack() as es:
        wo = k.sb(es, "wo", [128, 8192], BF16)
        wsrc = D["w_out"][l].rearrange("(k p) c -> p k c", p=128)
        for kk in range(8):
            k.dma(k.pool, wo[:, kk * 1024:(kk + 1) * 1024], wsrc[:, kk, :], (), [wo])
        ot = [k.sb(es, "oo%d" % i, [128, 8 * 512], BF16) for i in range(2)]
        xo = [k.sb(es, "oxo%d" % i, [128, 512], F32) for i in range(2)]
        xn = [k.sb(es, "oxn%d" % i, [128, 512], F32) for i in range(2)]
        it = 0
        for tb in range(NTB):
            o = ot[tb % 2]
            for kk in range(8):
                k.dma(k.sp, o[:, kk * 512:(kk + 1) * 512], D["O"][kk, :, tb * 512:(tb + 1) * 512], [g.OR[kk]], [o])
            for m in range(8):
                po = g.PS[it % 4]
                for kk in range(8):
                    k.mm(po[:], wo[:, kk * 1024 + m * 128: kk * 1024 + (m + 1) * 128], o[:, kk * 512:(kk + 1) * 512],
                         kk == 0, kk == 7, [wo, o], [po])
                resid_update(g, l, 5, m, tb, po, xo[it % 2], xn[it % 2])
                it += 1
        k.barrier()

SEQS = [(0, 256, False), (256, 256, False), (512, 256, False), (768, 256, False), (1024, 2048, True)]
GROUPS = [(0, 1024, False, [(0, 256), (256, 256), (512, 256), (768, 256)]), (1024, 2048, True, [(0, 2048)])]
MAXKEY = 1.0 - 1e-6


def mixer_input_shapes():
    return {
        "ropeB": [2, 96, 2048], "ropeD": [2, 128, 2048], "rotB": [96, 96], "rotD": [128, 128], "bd32": [128, 128], "bd64": [128, 128], "esel": [32, 96],
        "wuq": [DEPTH, 128, 768], "wukv": [DEPTH, 128, 512], "gains": [DEPTH, 128, 8], "dlam": [DEPTH, 32, 4],
        "cmlaT": [DEPTH, 160, 512], "cdkT": [DEPTH, 256, 512], "cdv": [DEPTH, 512, 256],
        "s0": [DEPTH, 2, 4, 64, 64], "lbl": [64, 32], "gon": [DEPTH, 64, 1], "mask4": [2, 128, 256],
        "maskbd": [2, 128, 128], "ident": [64, 64],
        **hyena_input_shapes(),
    }


def extra_scratch():
    return hyena_scratch()


def _rope_tables(rot_dim, ngroups_before):
    n_f = rot_dim // 4
    inv = (10000.0 ** (-np.arange(n_f, dtype=np.float32) / n_f)).astype(np.float32)
    t = np.arange(2048)
    ang_r = (t // 64).astype(np.float32)[:, None] * inv
    ang_c = (t % 64).astype(np.float32)[:, None] * inv
    half = rot_dim // 2
    C = np.zeros((rot_dim, 2048), np.float32)
    S_ = np.zeros((rot_dim, 2048), np.float32)
    for d in range(rot_dim):
        ang = ang_r if d < half else ang_c
        i = (d % half) % n_f
        C[d] = np.cos(ang[:, i])
        S_[d] = np.sin(ang[:, i])
    R = np.zeros((rot_dim, rot_dim), np.float32)
    for d in range(rot_dim):
        dd = d % half
        if dd < n_f:
            R[d + n_f, d] = -1.0
        else:
            R[d - n_f, d] = 1.0
    return C, S_, R


def mixer_host_shared(inp):
    f = np.float32
    S = {}
    C, Sn, R = _rope_tables(32, 0)
    ropeB = np.zeros((2, 96, 2048), f)
    ropeB[0, :64] = 1.0
    ropeB[0, 64:] = C
    ropeB[1, 64:] = Sn
    S["ropeB"] = ropeB
    S["ropeD"] = np.ascontiguousarray(np.tile(np.stack([C, Sn]).astype(f), (1, 4, 1)))
    rotB = np.zeros((96, 96), f)
    rotB[64:, 64:] = R
    S["rotB"] = rotB
    S["rotD"] = np.kron(np.eye(4, dtype=f), R.astype(f))
    S["bd32"] = np.kron(np.eye(4, dtype=f), np.full((32, 32), 1.0 / 32.0, f))
    S["bd64"] = np.kron(np.eye(2, dtype=f), np.full((64, 64), 1.0 / 64.0, f))
    es_ = np.zeros((32, 96), f)
    es_[np.arange(32), 64 + np.arange(32)] = 1.0
    S["esel"] = es_
    S["wuq"] = np.ascontiguousarray(inp["mla_w_uq"].reshape(DEPTH, 2, 128, 384).transpose(0, 2, 1, 3).reshape(DEPTH, 128, 768), dtype=f)
    S["wukv"] = np.ascontiguousarray(inp["mla_w_ukv"], dtype=f)
    gains = np.zeros((DEPTH, 128, 8), f)
    gains[:, :, 0:2] = inp["mla_q_norm"].reshape(DEPTH, 2, 128).transpose(0, 2, 1)
    gains[:, :, 2] = inp["mla_kv_norm"]
    gains[:, :96, 3:5] = inp["mla_qk_norm"].transpose(0, 2, 1)
    gains[:, :, 5:7] = np.tile(inp["diff_qk_norm"].transpose(0, 2, 1), (1, 4, 1))
    gains[:, :64, 7] = inp["diff_subln"]
    S["gains"] = gains
    S["dlam"] = np.ascontiguousarray(inp["diff_lambda"].transpose(0, 2, 1), dtype=f)
    lb = inp["hgrn_lb_logits"].reshape(2, DEPTH, 4, 64)
    S["lbl"] = np.ascontiguousarray(lb.transpose(3, 0, 2, 1).reshape(64, 32), dtype=f)
    S["gon"] = np.ascontiguousarray(inp["hgrn_onorm"].reshape(DEPTH, 64, 1), dtype=f)
    s_ = np.arange(128)
    m4 = np.zeros((2, 128, 4, 64), f)
    for j in range(4):
        m4[0, s_ // 32 == j, j, :] = 1.0
        m4[1, s_ // 32 == 3 - j, j, :] = 1.0
    S["mask4"] = m4.reshape(2, 128, 256)
    same = (s_[:, None] // 32) == (s_[None, :] // 32)
    S["maskbd"] = np.stack([same & (s_[:, None] <= s_[None, :]), same & (s_[:, None] >= s_[None, :])]).astype(f)
    S["ident"] = np.eye(64, dtype=f)
    S.update(hyena_host_shared(inp))
    return S


def mixer_host_core(inp, core):
    f = np.float32
    C = {}
    C["cmlaT"] = np.ascontiguousarray(inp["cache_mla"][core].transpose(0, 2, 1), dtype=f)
    C["cdkT"] = np.ascontiguousarray(inp["cache_diff_k"][core].reshape(DEPTH, 512, 256).transpose(0, 2, 1), dtype=f)
    C["cdv"] = np.ascontiguousarray(inp["cache_diff_v"][core].reshape(DEPTH, 512, 256), dtype=f)
    C["s0"] = np.ascontiguousarray(inp["state_hgrn"][core], dtype=f)
    return C


def setup_consts(g, es):
    k, D = g.k, g.D
    g.ones = {}
    for gs in (256, 128, 96, 64, 32):
        t = k.sb(es, "ones%d" % gs, [128, 128], BF16)
        k.memset(t[:], 1.0 / gs, [t])
        g.ones[gs] = t
    g.rotB = k.sb(es, "rotB", [96, 96], BF16)
    g.rotD = k.sb(es, "rotD", [128, 128], BF16)
    g.bd32 = k.sb(es, "bd32", [128, 128], BF16)
    k.dma(k.pool, g.bd32[:], D["bd32"], (), [g.bd32])
    g.bd64 = k.sb(es, "bd64", [128, 128], BF16)
    k.dma(k.pool, g.bd64[:], D["bd64"], (), [g.bd64])
    g.esel = k.sb(es, "esel", [32, 96], BF16)
    k.dma(k.pool, g.rotB[:], D["rotB"], (), [g.rotB])
    k.dma(k.pool, g.rotD[:], D["rotD"], (), [g.rotD])
    k.dma(k.pool, g.esel[:], D["esel"], (), [g.esel])
    g.onesf = k.sb(es, "onesf", [32, 64], F32)
    k.memset(g.onesf[:], 1.0, [g.onesf])
    hgrn_setup(g, es)
    hyena_setup(g, es)


def pj_rows(g, r0, n, t0, nt):
    flat = g.D["PJ"].rearrange("c p t -> (c p) t")
    res = [g.PJR[c] for c in range(r0 // 128, (r0 + n - 1) // 128 + 1)]
    return flat[r0:r0 + n, t0:t0 + nt], res


def group_norm_rope(g, W, src, srcres, gs, n, gain, rope, out, outres, rope_off=0, f32out=None, f32res=None, onesT=None,
                    gres=()):
    k, nc = g.k, g.nc
    sq, rs, t1, t2, pn, pr = W["sq"], W["rs"], W["t1"], W["t2"], W["pn"], W["pr"]
    k.activation(sq[0:gs, 0:n], src, AF.Square, srcres, [sq])
    oT = onesT if onesT is not None else g.ones[gs]
    k.mm(pn[0:gs, 0:n], oT[0:gs, 0:gs], sq[0:gs, 0:n], True, True, [sq, oT], [pn])
    k.activation(rs[0:gs, 0:n], pn[0:gs, 0:n], AF.Ln, [pn], [rs], bias=g.eps_t[0:gs, 0:1])
    k.activation(rs[0:gs, 0:n], rs[0:gs, 0:n], AF.Exp, [rs], [rs], scale=-0.5)
    if rope is None:
        if f32out is not None:
            k.stt(f32out, src, gain, rs[0:gs, 0:n], ALU.mult, ALU.mult, list(srcres) + [rs] + list(gres), f32res)
            k.copy(out, f32out, f32res, outres, E=k.pool)
        else:
            k.stt(out, src, gain, rs[0:gs, 0:n], ALU.mult, ALU.mult, list(srcres) + [rs] + list(gres), outres)
        return
    rot, Ct, St = rope
    k.stt(t1[0:gs, 0:n], src, gain, rs[0:gs, 0:n], ALU.mult, ALU.mult, list(srcres) + [rs] + list(gres), [t1])
    k.copy(t2[0:gs, 0:n], t1[0:gs, 0:n], [t1], [t2], E=k.act)
    k.mm(pr[0:gs, 0:n], rot[0:gs, 0:gs], t2[0:gs, 0:n], True, True, [t2], [pr])
    k.tt(t1[0:gs, 0:n], t1[0:gs, 0:n], Ct[0:gs, rope_off:rope_off + n], ALU.mult, [t1, Ct], [t1], E=k.pool)
    k.tt(rs[0:gs, 0:n], pr[0:gs, 0:n], St[0:gs, rope_off:rope_off + n], ALU.mult, [pr, St], [rs])
    k.tt(out, t1[0:gs, 0:n], rs[0:gs, 0:n], ALU.add, [t1, rs], outres, E=k.pool)


def attn_pipeline(g, W, streams, nk, nq):
    k = g.k
    nkt = nk // 128
    items = [(st, kt) for kt in range(nkt) for st in streams]
    NB = 3
    LA = 2

    def emit_sc(i):
        st, kt = items[i]
        sc = g.PS[i % NB]
        dk_ = st["dk"]
        kc = st.get("kt0", 0) + kt
        k.mm(sc[:, 0:nq], st["KT"][0:dk_, kc * 128:(kc + 1) * 128], st["QT"][0:dk_, st["q0"]:st["q0"] + nq], True, True,
             [st["KT"], st["QT"]], [sc])

    for i in range(min(LA, len(items))):
        emit_sc(i)
    for i in range(len(items)):
        if i + LA < len(items):
            emit_sc(i + LA)
        st, kt = items[i]
        sc = g.PS[i % NB]
        pt = W["pt"][i % NB]
        k.activation(pt[:, 0:nq], sc[:, 0:nq], AF.Exp, [sc], [pt], scale=st["scale"])
        h = st["hcol"]
        kc = st.get("kt0", 0) + kt
        k.mm(st["acc"][:, 0:nq], st["VA"][:, kc * 512 + h * 128: kc * 512 + (h + 1) * 128], pt[:, 0:nq], kt == 0, kt == nkt - 1,
             [st["VA"], pt], [st["acc"]])


def phase_mla(g, l):
    k, nc, D = g.k, g.nc, g.D
    scale = 96.0 ** -0.5
    with ExitStack() as es:
        g.ropeB = [k.sb(es, "ropeB%d" % i, [96, 2048], F32) for i in range(2)]
        for i in range(2):
            k.dma(k.sp, g.ropeB[i][:], D["ropeB"][i], (), [g.ropeB[i]])
        gn = k.sb(es, "gn", [128, 8], F32)
        k.dma(k.sp, gn[:], D["gains"][l], (), [gn])
        wuq = k.sb(es, "wuq", [128, 768], BF16)
        k.dma(k.pool, wuq[:], D["wuq"][l], (), [wuq])
        wkv = k.sb(es, "wkv", [128, 512], BF16)
        k.dma(k.pool, wkv[:], D["wukv"][l], (), [wkv])
        wkp = k.sb(es, "wkp", [128, 4 * 96], BF16)
        wv = k.sb(es, "wv", [128, 256], BF16)
        k.memset(wkp[:], 0.0, [wkp])
        for h in range(4):
            k.copy(wkp[:, h * 96:h * 96 + 64], wkv[:, h * 128:h * 128 + 64], [wkv], [wkp])
            k.copy(wv[:, h * 64:(h + 1) * 64], wkv[:, h * 128 + 64:(h + 1) * 128], [wkv], [wv])
        QT = [k.sb(es, "QT%d" % h, [96, 2048], BF16) for h in range(4)]
        KT = [k.sb(es, "KT%d" % h, [96, 2560], BF16) for h in range(4)]
        VA = k.sb(es, "VA", [128, 20 * 512], BF16)
        k.memset(VA[:], 1.0, [VA])
        W = {"sq": k.sb(es, "wsq", [128, 512], BF16), "rs": k.sb(es, "wrs", [128, 512], F32),
             "t1": k.sb(es, "wt1", [128, 512], F32), "t2": k.sb(es, "wt2", [128, 512], BF16),
             "pn": g.PS[3], "pr": g.PS[3], "pt": [k.sb(es, "wpt%d" % i, [128, 512], BF16) for i in range(3)]}
        cq = k.sb(es, "cq", [128, 1024], F32)
        cqn = k.sb(es, "cqn", [128, 1024], BF16)
        ckv = k.sb(es, "ckv", [128, 512], F32)
        ckn = k.sb(es, "ckn", [128, 512], F32)
        ckb = k.sb(es, "ckb", [128, 512], BF16)
        kr = k.sb(es, "kr", [32, 512], F32)
        krb = k.sb(es, "krb", [32, 512], BF16)
        rd = [k.sb(es, "rd%d" % i, [64, 512], F32) for i in range(2)]
        ob = [k.sb(es, "ob%d" % i, [64, 512], BF16) for i in range(4)]
        pq = g.PS[4]
        pv = g.PS[5]
        oi = 0

        def keys_block(ckb_, krb_, n, kcol, rope_off):
            for h in range(4):
                k.mm(pq[0:96, 0:n], wkp[:, h * 96:(h + 1) * 96], ckb_[:, 0:n], True, False, [wkp, ckb_], [pq])
                k.mm(pq[0:96, 0:n], g.esel[:, :], krb_[:, 0:n], False, True, [krb_], [pq])
                rope = None if rope_off is None else (g.rotB, g.ropeB[0], g.ropeB[1])
                group_norm_rope(g, W, pq[0:96, 0:n], [pq], 96, n, gn[0:96, 4:5], rope, KT[h][:, kcol:kcol + n], [KT[h]],
                                rope_off=rope_off or 0, gres=[gn])
            for tt in range(n // 128):
                k.mm(pv[:, 0:256], ckb_[:, tt * 128:(tt + 1) * 128], wv[:, :], True, True, [ckb_, wv], [pv])
                kt = kcol // 128 + tt
                k.copy(VA[:, kt * 512:(kt + 1) * 512].rearrange("p (h c) -> p h c", h=4)[:, :, 0:64],
                       pv[:, 0:256].rearrange("p (h c) -> p h c", h=4), [pv], [VA])

        for (t0, L, ctx, seqs) in GROUPS:
            koff = 512 if ctx else 0
            if ctx:
                k.dma(k.sp, ckn[:], D["cmlaT"][l, 0:128, :], (), [ckn])
                k.dma(k.sp, kr[:], D["cmlaT"][l, 128:160, :], (), [kr])
                k.copy(ckb[:], ckn[:], [ckn], [ckb])
                k.copy(krb[:], kr[:], [kr], [krb])
                keys_block(ckb, krb, 512, 0, None)
            for b0 in range(0, L, 512):
                n = min(512, L - b0)
                tg = t0 + b0
                for m in range(2):
                    ap_, rr = pj_rows(g, R_CQ + m * 128, 128, tg, n)
                    k.dma(k.sp, cq[:, m * 512:m * 512 + n], ap_, rr, [cq])
                ap_, rr = pj_rows(g, R_CKV, 128, tg, n)
                k.dma(k.sp, ckv[:, 0:n], ap_, rr, [ckv])
                ap_, rr = pj_rows(g, R_KR, 32, tg, n)
                k.dma(k.sp, kr[:, 0:n], ap_, rr, [kr])
                pn = W["pn"]
                for m in range(2):
                    k.activation(W["sq"][:, 0:n], cq[:, m * 512:m * 512 + n], AF.Square, [cq], [W["sq"]])
                    k.mm(pn[:, 0:n], g.ones[256][:, :], W["sq"][:, 0:n], m == 0, m == 1, [W["sq"]], [pn])
                k.activation(W["rs"][:, 0:n], pn[:, 0:n], AF.Sqrt, [pn], [W["rs"]], bias=g.eps_t[:, 0:1])
                k.op(k.dve, lambda: nc.vector.reciprocal(out=W["rs"][:, 0:n], in_=W["rs"][:, 0:n]), [W["rs"]], [W["rs"]])
                for m in range(2):
                    k.stt(cqn[:, m * 512:m * 512 + n], cq[:, m * 512:m * 512 + n], gn[:, m:m + 1], W["rs"][:, 0:n],
                          ALU.mult, ALU.mult, [cq, W["rs"], gn], [cqn])
                group_norm_rope(g, W, ckv[:, 0:n], [ckv], 128, n, gn[:, 2:3], None, ckb[:, 0:n], [ckb],
                                f32out=ckn[:, 0:n], f32res=[ckn], gres=[gn])
                k.copy(krb[:, 0:n], kr[:, 0:n], [kr], [krb])
                if not ctx:
                    k.dma(k.sp, D["mlaT"][l, 0:128, tg:tg + n], ckn[:, 0:n], [ckn], [])
                    k.dma(k.sp, D["mlaT"][l, 128:160, tg:tg + n], kr[:, 0:n], [kr], [])
                for h in range(4):
                    for m in range(2):
                        k.mm(pq[0:96, 0:n], wuq[:, m * 384 + h * 96: m * 384 + (h + 1) * 96], cqn[:, m * 512:m * 512 + n],
                             m == 0, m == 1, [wuq, cqn], [pq])
                    rope = (g.rotB, g.ropeB[0], g.ropeB[1]) if ctx else None
                    group_norm_rope(g, W, pq[0:96, 0:n], [pq], 96, n, gn[0:96, 3:4], rope, QT[h][:, b0:b0 + n], [QT[h]],
                                    rope_off=b0, gres=[gn])
                keys_block(ckb, krb, n, koff + b0, b0 if ctx else None)
            for (s0, Ls) in seqs:
                nk = Ls + koff
                kt0 = 0 if ctx else s0 // 128
                for q0 in range(0, Ls, 512):
                    nq = min(512, Ls - q0)
                    for hp in range(2):
                        base = 4 + 2 * (oi % 2)
                        oi += 1
                        sts = []
                        for j in range(2):
                            h = hp * 2 + j
                            sts.append(dict(KT=KT[h], QT=QT[h], dk=96, q0=s0 + q0, kt0=kt0, VA=VA, hcol=h, scale=scale,
                                            acc=g.PS[base + j]))
                        attn_pipeline(g, W, sts, nk, nq)
                        for j in range(2):
                            h = hp * 2 + j
                            acc = sts[j]["acc"]
                            o_ = ob[(oi * 2 + j) % 4]
                            k.op(k.dve, lambda: nc.vector.reciprocal(out=rd[j][:, 0:nq], in_=acc[64:128, 0:nq]), [acc], [rd[j]])
                            k.tt(o_[:, 0:nq], acc[0:64, 0:nq], rd[j][:, 0:nq], ALU.mult, [acc, rd[j]], [o_])
                            r0 = 256 + h * 64
                            tq = t0 + s0 + q0
                            k.dma(k.sp, D["O"][r0 // 128, r0 % 128:r0 % 128 + 64, tq:tq + nq], o_[:, 0:nq], [o_],
                                  [g.OR[r0 // 128]])
        k.barrier()


def phase_diff(g, l):
    k, nc, D = g.k, g.nc, g.D
    scale = 32.0 ** -0.5
    lam_init = 0.8 - 0.6 * float(np.exp(-0.3 * l))
    with ExitStack() as es:
        g.ropeD = [k.sb(es, "ropeD%d" % i, [128, 2048], F32) for i in range(2)]
        for i in range(2):
            k.dma(k.sp, g.ropeD[i][:], D["ropeD"][i], (), [g.ropeD[i]])
        gn = k.sb(es, "gn", [128, 8], F32)
        k.dma(k.sp, gn[:], D["gains"][l], (), [gn])
        dl = k.sb(es, "dl", [32, 4], F32)
        k.dma(k.sp, dl[:], D["dlam"][l], (), [dl])
        pr2 = k.sb(es, "pr2", [32, 2], F32)
        k.tt(pr2[:, 0:1], dl[:, 0:1], dl[:, 1:2], ALU.mult, [dl], [pr2])
        k.tt(pr2[:, 1:2], dl[:, 2:3], dl[:, 3:4], ALU.mult, [dl], [pr2])
        pl = g.PS[2]
        k.mm(pl[0:64, 0:2], g.onesf[:, :], pr2[:, :], True, True, [pr2, g.onesf], [pl])
        lam = k.sb(es, "lam", [64, 4], F32)
        k.activation(lam[:, 0:2], pl[0:64, 0:2], AF.Exp, [pl], [lam])
        k.stt(lam[:, 2:3], lam[:, 1:2], -lam_init, lam[:, 0:1], ALU.add, ALU.subtract, [lam], [lam])
        k.ts(lam[:, 3:4], gn[0:64, 7:8], 1.0 - lam_init, None, ALU.mult, reads=[gn], writes=[lam])
        QT = [k.sb(es, "dQ%d" % i, [128, 2048], BF16) for i in range(2)]
        KT = [k.sb(es, "dK%d" % i, [128, 2560], BF16) for i in range(2)]
        VA = k.sb(es, "dVA", [128, 20 * 512], BF16)
        k.memset(VA[:], 1.0, [VA])
        W = {"sq": k.sb(es, "wsq", [128, 512], BF16), "rs": k.sb(es, "wrs", [128, 512], F32),
             "t1": k.sb(es, "wt1", [128, 512], F32), "t2": k.sb(es, "wt2", [128, 512], BF16),
             "pn": g.PS[3], "pr": g.PS[3], "pt": [k.sb(es, "wpt%d" % i, [128, 512], BF16) for i in range(8)]}
        xin128 = [k.sb(es, "dxin128%d" % i, [128, 512], F32) for i in range(2)]
        kf = k.sb(es, "dkf", [128, 512], F32)
        vin = [k.sb(es, "dvin%d" % i, [128, 256], F32) for i in range(2)]
        rd = [k.sb(es, "rd%d" % i, [64, 512], F32) for i in range(4)]
        oo = [k.sb(es, "doo%d" % i, [64, 512], F32) for i in range(4)]
        ob = [k.sb(es, "dob%d" % i, [64, 512], BF16) for i in range(2)]
        oi = 0
        xi = 0
        pti = 0
        for (t0, L, ctx, seqs) in GROUPS:
            koff = 512 if ctx else 0
            rope = (g.rotD, g.ropeD[0], g.ropeD[1]) if ctx else None
            if ctx:
                for c2 in range(2):
                    x_ = xin128[xi % 2]
                    xi += 1
                    k.dma(k.sp, x_[:], D["cdkT"][l, c2 * 128:(c2 + 1) * 128, :], (), [x_])
                    k.copy(KT[c2][:, 0:512], x_[:], [x_], [KT[c2]])
                for tt in range(4):
                    v_ = vin[tt % 2]
                    k.dma(k.sp, v_[:], D["cdv"][l, tt * 128:(tt + 1) * 128, :], (), [v_])
                    k.copy(VA[:, tt * 512:(tt + 1) * 512].rearrange("p (h c) -> p h c", h=4)[:, :, 0:64],
                           v_[:].rearrange("p (h c) -> p h c", h=4), [v_], [VA])
            for b0 in range(0, L, 512):
                n = min(512, L - b0)
                tg = t0 + b0
                for c2 in range(2):
                    for isk in range(2):
                        x_ = xin128[xi % 2]
                        xi += 1
                        ap_, rr = pj_rows(g, (R_DK if isk else R_DQ) + c2 * 128, 128, tg, n)
                        k.dma(k.sp, x_[:, 0:n], ap_, rr, [x_])
                        dst = KT[c2] if isk else QT[c2]
                        off = (koff + b0) if isk else b0
                        if isk and not ctx:
                            group_norm_rope(g, W, x_[:, 0:n], [x_], 128, n, gn[:, 6:7], None, dst[:, off:off + n], [dst],
                                            f32out=kf[:, 0:n], f32res=[kf], onesT=g.bd32, gres=[gn])
                            k.dma(k.sp, D["dkT"][l, c2 * 128:(c2 + 1) * 128, tg:tg + n], kf[:, 0:n], [kf], [])
                        else:
                            group_norm_rope(g, W, x_[:, 0:n], [x_], 128, n, gn[:, 5 + isk:6 + isk], rope, dst[:, off:off + n], [dst],
                                            rope_off=b0, onesT=g.bd32, gres=[gn])
                for tt in range(n // 128):
                    v_ = vin[tt % 2]
                    k.dma(k.sp, v_[:], D["PT"][tg + tt * 128:tg + (tt + 1) * 128, 768:1024], [g.PTR], [v_])
                    kt = (koff + b0) // 128 + tt
                    k.copy(VA[:, kt * 512:(kt + 1) * 512].rearrange("p (h c) -> p h c", h=4)[:, :, 0:64],
                           v_[:].rearrange("p (h c) -> p h c", h=4), [v_], [VA])
            if not ctx:
                ap_, rr = pj_rows(g, R_DV, 256, t0, L)
                k.dma(k.sp, D["dvT"][l, :, t0:t0 + L], ap_, rr, [])
            for (s0, Ls) in seqs:
              nkt = (Ls + koff) // 128
              kt0 = 0 if ctx else s0 // 128
              for h in range(4):
                c2 = h // 2
                rb = 64 * (h % 2)
                for q0 in range(0, Ls, 512):
                    nq = min(512, Ls - q0)
                    qc = s0 + q0
                    base = 4 + 2 * (oi % 2)
                    accs = [g.PS[base], g.PS[base + 1]]

                    def emit_sc(kt):
                        kc = kt0 + kt
                        for m in range(2):
                            r_ = rb + 32 * m
                            sc = g.PS[(kt % 2) * 2 + m]
                            k.mm(sc[:, 0:nq], KT[c2][r_:r_ + 32, kc * 128:(kc + 1) * 128], QT[c2][r_:r_ + 32, qc:qc + nq],
                                 True, True, [KT[c2], QT[c2]], [sc], tile_position=(r_, 0))

                    emit_sc(0)
                    for kt in range(nkt):
                        if kt + 1 < nkt:
                            emit_sc(kt + 1)
                        kc = kt0 + kt
                        for m in range(2):
                            sc = g.PS[(kt % 2) * 2 + m]
                            pt = W["pt"][pti % 8]
                            pti += 1
                            k.activation(pt[:, 0:nq], sc[:, 0:nq], AF.Exp, [sc], [pt], scale=scale)
                            k.mm(accs[m][:, 0:nq], VA[:, kc * 512 + h * 128: kc * 512 + (h + 1) * 128], pt[:, 0:nq],
                                 kt == 0, kt == nkt - 1, [VA, pt], [accs[m]])
                    for m in range(2):
                        k.op(k.dve, lambda: nc.vector.reciprocal(out=rd[m][:, 0:nq], in_=accs[m][64:128, 0:nq]), [accs[m]], [rd[m]])
                        k.tt(oo[m][:, 0:nq], accs[m][0:64, 0:nq], rd[m][:, 0:nq], ALU.mult, [accs[m], rd[m]], [oo[m]])
                    o0, o1 = oo[0], oo[1]
                    o_ = ob[oi % 2]
                    oi += 1
                    k.stt(o0[:, 0:nq], o1[:, 0:nq], lam[:, 2:3], o0[:, 0:nq], ALU.mult, ALU.add, [o0, o1, lam], [o0])
                    group_norm_rope(g, W, o0[:, 0:nq], [o0], 64, nq, lam[:, 3:4], None, o_[:, 0:nq], [o_], gres=[lam])
                    r0 = 768 + h * 64
                    tq = t0 + s0 + q0
                    k.dma(k.sp, D["O"][r0 // 128, r0 % 128:r0 % 128 + 64, tq:tq + nq], o_[:, 0:nq], [o_],
                          [g.OR[r0 // 128]])
        k.barrier()


def phase_zero(g, chunks):
    k, D = g.k, g.D
    with ExitStack() as es:
        z = k.sb(es, "z", [128, NT], BF16)
        k.memset(z[:], 0.0, [z])
        for kk in chunks:
            k.dma(k.sp, D["O"][kk], z[:], [z], [g.OR[kk]])
        k.barrier()


def phase_mixers(g, l, mixers):
    if "a" in mixers:
        phase_hgrn(g, l)
    else:
        phase_zero(g, [0, 1])
    if "b" in mixers:
        phase_mla(g, l)
    else:
        phase_zero(g, [2, 3])
    if "c" in mixers:
        phase_hyena(g, l)
    else:
        phase_zero(g, [4, 5])
    if "d" in mixers:
        phase_diff(g, l)
    else:
        phase_zero(g, [6, 7])


def hgrn_setup(g, es):
    k, nc, D = g.k, g.nc, g.D
    lbl = k.sb(es, "lbl", [64, 32], F32)
    k.dma(k.sp, lbl[:], D["lbl"], (), [lbl])
    e = k.sb(es, "lbe", [64, 32], F32)
    k.activation(e[:], lbl[:], AF.Exp, [lbl], [e])
    sm = k.sb(es, "lbs", [64, 8], F32)
    e3 = e[:].rearrange("p (a l) -> p a l", l=4)
    k.op(k.dve, lambda: nc.vector.tensor_reduce(out=sm[:], in_=e3, axis=AX.X, op=ALU.add), [e], [sm])
    k.op(k.dve, lambda: nc.vector.reciprocal(out=sm[:], in_=sm[:]), [sm], [sm])
    k.tt(e3, e3, sm[:].unsqueeze(2).to_broadcast([64, 8, 4]), ALU.mult, [e, sm], [e])
    g.oml = k.sb(es, "oml", [64, 32], F32)
    o3 = g.oml[:].rearrange("p (a l) -> p a l", l=4)
    k.memset(g.oml[:], 1.0, [g.oml])
    for l in range(1, 4):
        k.tt(o3[:, :, l], o3[:, :, l - 1], e3[:, :, l], ALU.subtract, [g.oml, e], [g.oml])
    g.mask4 = [k.sb(es, "mask4%d" % i, [128, 256], BF16) for i in range(2)]
    g.maskbd = [k.sb(es, "maskbd%d" % i, [128, 128], F32) for i in range(2)]
    for i in range(2):
        k.dma(k.pool, g.mask4[i][:], D["mask4"][i], (), [g.mask4[i]])
        k.dma(k.sp, g.maskbd[i][:], D["maskbd"][i], (), [g.maskbd[i]])
    g.ident = k.sb(es, "ident", [64, 64], BF16)
    k.dma(k.pool, g.ident[:], D["ident"], (), [g.ident])
    g.rmask = k.sb(es, "rmask", [128, 2048], F32)
    k.memset(g.rmask[:], 1.0, [g.rmask])
    k.memset(g.rmask[:].rearrange("p (a b) -> p a b", b=32)[:, :, 0:1], 0.0, [g.rmask])
    g.one_t = k.sb(es, "one_t", [128, 1], F32)
    k.memset(g.one_t[:], 1.0, [g.one_t])


def phase_hgrn(g, l):
    k, nc, D = g.k, g.nc, g.D
    for (t0, nseq, Ls, ctx) in ((0, 4, 256, False), (1024, 1, 2048, True)):
        L = nseq * Ls
        nch = L // 32
        nchs = Ls // 32
        ntl = L // 128
        nsl = nseq * (nchs + 1)

        def slot_of(i):
            return (i // nchs) * (nchs + 1) + 1 + (i % nchs)

        with ExitStack() as es:
            f32t = lambda nm: k.sb(es, nm, [128, L], F32)
            af, og, qs, kk, cum, e1, of = [f32t(n_) for n_ in ("af", "og", "qs", "kk", "cum", "e1", "of")]
            aq = e1
            gl = af
            qt, kt, k2 = [k.sb(es, n_, [128, L], BF16) for n_ in ("qt", "kt", "k2")]
            el = k.sb(es, "el", [128, nch], F32)
            k2T = k.sb(es, "k2T", [128, ntl * 128], BF16)
            vin = [k.sb(es, "hvin%d" % i, [128, 256], F32) for i in range(2)]
            vt = k.sb(es, "vt", [128, ntl * 256], BF16)
            V4 = [k.sb(es, "V4%d" % i, [128, 256], BF16) for i in range(4)]
            U3, A3, S3 = [k.sb(es, n_, [128, 64 * nsl], F32) for n_ in ("U3", "A3", "S3")]
            S16 = k.sb(es, "S16", [128, 64 * nsl], BF16)
            AT = [k.sb(es, "AT%d" % i, [128, 128], BF16) for i in range(4)]
            s0t = k.sb(es, "s0t", [128, 64], F32)
            fin = [k.sb(es, "fin%d" % i, [128, 64], F32) for i in range(2)]
            gon = k.sb(es, "gon", [128, 1], F32)
            k.dma(k.sp, gon[0:64, :], D["gon"][l], (), [gon])
            k.dma(k.sp, gon[64:128, :], D["gon"][l], (), [gon])
            omlp = k.sb(es, "omlp", [128, 4], F32)
            for hp in range(2):
                for d in range(2):
                    for j in range(2):
                        col = (d * 4 + hp * 2 + j) * 4 + l
                        k.copy(omlp[64 * j:64 * j + 64, hp * 2 + d:hp * 2 + d + 1], g.oml[:, col:col + 1], [g.oml], [omlp])
            obf = [k.sb(es, "hob%d" % i, [128, 512], BF16) for i in range(2)]
            W = {"sq": k.sb(es, "wsq", [128, 512], BF16), "rs": k.sb(es, "wrs", [128, 512], F32),
                 "t1": k.sb(es, "wt1", [128, 512], F32), "t2": k.sb(es, "wt2", [128, 512], BF16),
                 "pn": g.PS[2], "pr": g.PS[3]}
            v3 = lambda t_: t_[:].rearrange("p (d s) -> p d s", s=nsl)
            c3 = lambda t_: t_[:].rearrange("p (c j) -> p c j", j=32)
            cnt = 0
            for tt in range(ntl):
                v_ = vin[tt % 2]
                k.dma(k.sp, v_[:], D["PT"][t0 + tt * 128:t0 + (tt + 1) * 128, 0:256], [g.PTR], [v_])
                k.copy(vt[:, tt * 256:(tt + 1) * 256], v_[:], [v_], [vt], E=(k.act if tt % 2 else k.dve))
            for hp in range(2):
                ap_, rr = pj_rows(g, R_AQ + hp * 128, 128, t0, L)
                k.dma(k.sp, aq[:], ap_, rr, [aq])
                ap_, rr = pj_rows(g, R_AOG + hp * 128, 128, t0, L)
                k.dma(k.sp, og[:], ap_, rr, [og])
                k.activation(qs[:], aq[:], AF.Silu, [aq], [qs])
                for d in range(2):
                    ap_, rr = pj_rows(g, (R_AFF if d == 0 else R_AFB) + hp * 128, 128, t0, L)
                    k.dma(k.sp, af[:], ap_, rr, [af])
                    k.activation(kk[:], af[:], AF.Sigmoid, [af], [kk], scale=-1.0)
                    k.ts(kk[:], kk[:], omlp[:, hp * 2 + d:hp * 2 + d + 1], MAXKEY, ALU.mult, ALU.min, reads=[kk, omlp], writes=[kk])
                    k.activation(gl[:], kk[:], AF.Ln, [kk], [gl], scale=-1.0, bias=g.one_t[:, 0:1])
                    if d == 0:
                        k.op(k.dve, lambda: nc.vector.tensor_tensor_scan(out=cum[:], data0=g.rmask[:, 0:L], data1=gl[:],
                             initial=0.0, op0=ALU.mult, op1=ALU.add), [gl, g.rmask], [cum])
                        totv = c3(cum)[:, :, 31]
                    else:
                        k.op(k.dve, lambda: nc.vector.tensor_tensor_scan(out=cum[:, ::-1], data0=g.rmask[:, 0:L],
                             data1=gl[:, ::-1], initial=0.0, op0=ALU.mult, op1=ALU.add), [gl, g.rmask], [cum])
                        totv = c3(cum)[:, :, 0]
                    k.activation(e1[:], cum[:], AF.Exp, [cum], [e1])
                    k.stt(qt[:], e1[:], 0.125, qs[:], ALU.mult, ALU.mult, [e1, qs], [qt])
                    k.activation(e1[:], cum[:], AF.Exp, [cum], [e1], scale=-1.0)
                    k.tt(kt[:], kk[:], e1[:], ALU.mult, [kk, e1], [kt], E=k.pool)
                    k.activation(el[:], totv, AF.Exp, [cum], [el])
                    k.tt(c3(e1), totv.unsqueeze(2).to_broadcast([128, nch, 32]), c3(cum), ALU.subtract, [cum], [e1])
                    k.activation(e1[:], e1[:], AF.Exp, [e1], [e1])
                    k.tt(k2[:], kk[:], e1[:], ALU.mult, [kk, e1], [k2], E=k.pool)
                    for tt in range(ntl):
                        pT = g.PS[tt % 2]
                        k.mm(pT[:, 0:128], k2[:, tt * 128:(tt + 1) * 128], g.ident128[:, :], True, True, [k2, g.ident128], [pT])
                        k.copy(k2T[:, tt * 128:(tt + 1) * 128], pT[:, 0:128], [pT], [k2T], E=(k.act if tt % 2 else k.dve))
                    init_sl = slice(0, nsl, nchs + 1)
                    if ctx:
                        for j in range(2):
                            k.dma(k.sp, s0t[64 * j:64 * j + 64, :], D["s0"][l, d, hp * 2 + j], (), [s0t])
                        k.copy(v3(U3)[:, :, 0], s0t[:], [s0t], [U3])
                    else:
                        k.memset(v3(U3)[:, :, init_sl], 0.0, [U3])
                    k.memset(v3(A3)[:, :, init_sl], 0.0, [A3])
                    elv = el[:] if d == 0 else el[:, ::-1]
                    for sp_ in range(nseq):
                        s_lo = sp_ * (nchs + 1) + 1
                        k.copy(v3(A3)[:, :, s_lo:s_lo + nchs],
                               elv[:, sp_ * nchs:(sp_ + 1) * nchs].unsqueeze(1).to_broadcast([128, 64, nchs]), [el], [A3],
                               E=k.pool)
                    for tt in range(ntl):
                        i0 = 4 * tt if d == 0 else nch - 4 - 4 * tt
                        s_lo = slot_of(i0)
                        for j in range(2):
                            h = hp * 2 + j
                            v4 = V4[(tt * 2 + j) % 4]
                            k.tt(v4[:].rearrange("p (j c) -> p j c", j=4),
                                 vt[:, tt * 256 + h * 64: tt * 256 + (h + 1) * 64].unsqueeze(1).to_broadcast([128, 4, 64]),
                                 g.mask4[d][:].rearrange("p (j c) -> p j c", j=4), ALU.mult, [vt, g.mask4[d]], [v4],
                                 E=k.pool)
                            pu = g.PS[4 + j]
                            k.mm(pu[:, 0:256], k2T[:, tt * 128:(tt + 1) * 128], v4[:], True, True, [k2T, v4], [pu])
                            k.copy(v3(U3)[64 * j:64 * j + 64, :, s_lo:s_lo + 4].rearrange("p d s -> p s d"),
                                   pu[64 * j:64 * j + 64, 0:256].rearrange("p (s d) -> p s d", s=4), [pu], [U3],
                                   E=(k.act if j else k.dve))
                    k.op(k.dve, lambda: nc.vector.tensor_tensor_scan(out=S3[:], data0=A3[:], data1=U3[:], initial=0.0,
                         op0=ALU.mult, op1=ALU.add), [A3, U3], [S3])
                    k.copy(S16[:], S3[:], [S3], [S16], E=k.act)
                    if not ctx:
                        for sp_ in range(nseq):
                            si = sp_ if d == 0 else nseq - 1 - sp_
                            f_ = fin[cnt % 2]
                            cnt += 1
                            k.copy(f_[:], v3(S3)[:, :, sp_ * (nchs + 1) + nchs], [S3], [f_])
                            for j in range(2):
                                k.dma(k.sp, D["hg"][l, si, d, hp * 2 + j], f_[64 * j:64 * j + 64, :], [f_], [])
                    for tt in range(ntl):
                        for j in range(2):
                            h = hp * 2 + j
                            hb = 64 * j
                            pa = g.PS[j]
                            at = AT[(tt * 2 + j) % 4]
                            k.mm(pa[:, 0:128], kt[hb:hb + 64, tt * 128:(tt + 1) * 128], qt[hb:hb + 64, tt * 128:(tt + 1) * 128],
                                 True, True, [kt, qt], [pa])
                            k.tt(at[:], pa[:, 0:128], g.maskbd[d][:], ALU.mult, [pa, g.maskbd[d]], [at])
                        for j in range(2):
                            h = hp * 2 + j
                            hb = 64 * j
                            at = AT[(tt * 2 + j) % 4]
                            po = g.PS[6 + j]
                            cb = (tt % 4) * 128
                            k.mm(po[0:64, cb:cb + 128], vt[:, tt * 256 + h * 64: tt * 256 + (h + 1) * 64], at[:], True, False,
                                 [vt, at], [po])
                            for jj in range(4):
                                c = 4 * tt + jj
                                i = c if d == 0 else nch - 1 - c
                                k.mm(po[0:64, cb + 32 * jj:cb + 32 * jj + 32], v3(S16)[hb:hb + 64, :, slot_of(i) - 1],
                                     qt[hb:hb + 64, c * 32:(c + 1) * 32], False, jj == 3, [S16, qt], [po])
                            if tt % 4 == 3 or tt == ntl - 1:
                                b0 = (tt // 4) * 512
                                n = cb + 128
                                if d == 0:
                                    k.copy(of[hb:hb + 64, b0:b0 + n], po[0:64, 0:n], [po], [of], E=k.act)
                                else:
                                    k.tt(of[hb:hb + 64, b0:b0 + n], of[hb:hb + 64, b0:b0 + n], po[0:64, 0:n], ALU.add, [of, po], [of])
                k.activation(og[:], og[:], AF.Silu, [og], [og])
                for b0 in range(0, L, 512):
                    n = min(512, L - b0)
                    o_ = obf[(b0 // 512) % 2]
                    group_norm_rope(g, W, of[:, b0:b0 + n], [of], 128, n, gon[:, 0:1], None, W["t2"][:, 0:n], [W["t2"]],
                                    f32out=e1[:, b0:b0 + n], f32res=[e1], onesT=g.bd64, gres=[gon])
                    k.tt(o_[:, 0:n], e1[:, b0:b0 + n], og[:, b0:b0 + n], ALU.mult, [e1, og], [o_])
                    k.dma(k.sp, D["O"][hp, :, t0 + b0:t0 + b0 + n], o_[:, 0:n], [o_], [g.OR[hp]])
            k.barrier()


HYN = ((256, "P", SEQS[0:4]), (2048, "S", SEQS[4:5]))
TWO_PI = 6.283185307179586


def hyena_input_shapes():
    s = {"hw1": [DEPTH, 17, 64], "hb1": [DEPTH, 64, 1], "hw2": [DEPTH, 64, 64], "hb2": [DEPTH, 64, 1],
         "hw3": [DEPTH, 64, 1024], "hld": [DEPTH, 128, 1024], "hsT": [DEPTH, 128, 18], "hbT": [DEPTH, 128, 4],
         "hmsk": [128, 4], "ident128": [128, 128]}
    for n, sfx, _ in HYN:
        tbw = min(512, n)
        s["feats" + sfx] = [17, n]
        s["negtn" + sfx] = [128, n // 128]
        s["Fr" + sfx] = [2 * n // 128, 128, n]
        s["Gr" + sfx] = [n // tbw, 128, (2 * n // 128) * tbw]
        s["Fq" + sfx] = [2, 128, n]
    return s


def hyena_scratch():
    return [("HS" + sfx, [n // 128, 3, 128, 512], F32) for n, sfx, _ in HYN]


def hyena_host_shared(inp):
    f = np.float32
    S = {}
    S["hw1"] = np.ascontiguousarray(inp["hy_w1"], dtype=f)
    S["hb1"] = np.ascontiguousarray(inp["hy_b1"].reshape(DEPTH, 64, 1), dtype=f)
    S["hw2"] = np.ascontiguousarray(inp["hy_w2"], dtype=f)
    S["hb2"] = np.ascontiguousarray(inp["hy_b2"].reshape(DEPTH, 64, 1), dtype=f)
    S["hw3"] = np.ascontiguousarray(inp["hy_w3"], dtype=f)
    S["hld"] = np.ascontiguousarray(np.broadcast_to(inp["hy_log_decay"].reshape(DEPTH, 1, 1024), (DEPTH, 128, 1024)), dtype=f)
    hs = inp["hy_short"].reshape(DEPTH, 3, 6, 128)
    S["hsT"] = np.ascontiguousarray(hs.transpose(0, 3, 2, 1).reshape(DEPTH, 128, 18), dtype=f)
    hb = inp["hy_bias"].reshape(DEPTH, 2, 2, 128)
    S["hbT"] = np.ascontiguousarray(hb.transpose(0, 3, 1, 2).reshape(DEPTH, 128, 4), dtype=f)
    msk = np.ones((128, 4), f)
    msk[0, 1] = 0.0
    msk[:, 2] = 0.0
    msk[0, 2] = 1.0
    msk[:, 3] = -1.0
    S["hmsk"] = msk
    S["ident128"] = np.eye(128, dtype=f)
    for n, sfx, _ in HYN:
        tn = (np.arange(n, dtype=f) / f(n)).astype(f)
        ang = (f(2.0 * np.pi) * tn[:, None] * np.arange(1, 9, dtype=f)).astype(f)
        feats = np.concatenate([tn[:, None], np.cos(ang), np.sin(ang)], axis=-1).astype(f)
        S["feats" + sfx] = np.ascontiguousarray(feats.T)
        S["negtn" + sfx] = np.ascontiguousarray((-tn).reshape(n // 128, 128).T)
        t = np.arange(n, dtype=np.float64)[:, None]
        fr = np.arange(n, dtype=np.float64)[None, :]
        th = np.pi * t * fr / n
        Fm = np.concatenate([np.cos(th), -np.sin(th)], axis=1)
        Fm[:, n] = (-1.0) ** np.arange(n)
        Fr = Fm.reshape(n // 128, 128, 2 * n // 128, 128).transpose(2, 1, 0, 3).reshape(2 * n // 128, 128, n)
        S["Fr" + sfx] = np.ascontiguousarray(Fr.astype(f).astype(ml_dtypes.bfloat16))
        fq = np.zeros((2, 128, n // 128, 128), np.float64)
        fq[0] = Fr[n // 128].reshape(128, n // 128, 128)
        fq[1, :, :, 0] = fq[0, :, :, 0]
        fq[0, :, :, 0] = 0.0
        S["Fq" + sfx] = np.ascontiguousarray(fq.reshape(2, 128, n).astype(f).astype(ml_dtypes.bfloat16))
        Gm = np.concatenate([np.cos(th.T), -np.sin(th.T)], axis=0) * (2.0 / (2 * n))
        Gm[0, :] = 1.0 / (2 * n)
        Gm[n, :] = ((-1.0) ** np.arange(n)) / (2 * n)
        tbw = min(512, n)
        Gr = Gm.reshape(2 * n // 128, 128, n // tbw, tbw).transpose(2, 1, 0, 3).reshape(n // tbw, 128, (2 * n // 128) * tbw)
        S["Gr" + sfx] = np.ascontiguousarray(Gr.astype(f).astype(ml_dtypes.bfloat16))
    return S


def hyena_setup(g, es):
    k, D = g.k, g.D
    g.hmsk = k.sb(es, "hmsk", [128, 4], F32)
    k.dma(k.sp, g.hmsk[:], D["hmsk"], (), [g.hmsk])
    g.ident128 = k.sb(es, "ident128", [128, 128], BF16)
    k.dma(k.pool, g.ident128[:], D["ident128"], (), [g.ident128])
    g.onesF = k.sb(es, "onesF", [128, 128], F32)
    k.memset(g.onesF[:], 1.0, [g.onesF])
    g.HSR = {sfx: [Res() for _ in range(n // 128)] for n, sfx, _ in HYN}


def hyena_filters(g, l, n, sfx):
    k, nc, D = g.k, g.nc, g.D
    ntl = n // 128
    npair = n // 128
    nfc = 2 * npair
    with ExitStack() as es:
        w1 = k.sb(es, "hw1", [17, 64], F32)
        b1 = k.sb(es, "hb1", [64, 1], F32)
        w2 = k.sb(es, "hw2", [64, 64], F32)
        b2 = k.sb(es, "hb2", [64, 1], F32)
        w3 = k.sb(es, "hw3", [64, 1024], F32)
        eld = k.sb(es, "eld", [128, 1024], F32)
        ft = k.sb(es, "feat", [17, n], F32)
        ntn = k.sb(es, "ntn", [128, ntl], F32)
        for t_, nm in ((w1, "hw1"), (b1, "hb1"), (w2, "hw2"), (b2, "hb2"), (w3, "hw3"), (eld, "hld")):
            k.dma(k.sp, t_[:], D[nm][l], (), [t_])
        k.dma(k.sp, ft[:], D["feats" + sfx], (), [ft])
        k.dma(k.sp, ntn[:], D["negtn" + sfx], (), [ntn])
        k.activation(eld[:], eld[:], AF.Exp, [eld], [eld])
        h1 = k.sb(es, "h1", [64, n], F32)
        h2 = k.sb(es, "h2", [64, n], F32)
        y = k.sb(es, "hy", [64, 512], F32)
        kq = k.sb(es, "hkq", [64, 512], F32)
        ti = k.sb(es, "hti", [64, 512], I32)
        FBS = k.sb(es, "FBS", [128, ntl * 512], BF16)
        FBD = k.sb(es, "FBD", [128, ntl * 512], BF16)
        dec = k.sb(es, "dec", [128, 512], F32)
        fls = [k.sb(es, "fl%d" % i, [128, 512], F32) for i in range(2)]
        ab = k.sb(es, "ab", [128, 512], F32)
        rn = k.sb(es, "rn", [128, 512], F32)
        ReH = k.sb(es, "ReH", [128, npair * 512], F32)
        Fi = [k.sb(es, "Fi%d" % i, [128, n], BF16) for i in range(3)]
        Bt = [k.sb(es, "Bt%d" % i, [128, 512], F32) for i in range(2)]
        Ct = k.sb(es, "Ct", [128, 512], F32)

        def sin_layer(w, K_, rhs, bias, out):
            for b0 in range(0, n, 512):
                nb = min(512, n - b0)
                ps = g.PS[(b0 // 512) % 2]
                k.mm(ps[0:64, 0:nb], w[0:K_, :], rhs[0:K_, b0:b0 + nb], True, True, [w, rhs], [ps])
                k.ts(y[:, 0:nb], ps[0:64, 0:nb], bias[:, 0:1], None, ALU.add, reads=[ps, bias], writes=[y])
                k.ts(kq[:, 0:nb], y[:, 0:nb], 1.0 / TWO_PI, None, ALU.mult, reads=[y], writes=[kq])
                k.copy(ti[:, 0:nb], kq[:, 0:nb], [kq], [ti])
                k.copy(kq[:, 0:nb], ti[:, 0:nb], [ti], [kq])
                k.stt(y[:, 0:nb], kq[:, 0:nb], -TWO_PI, y[:, 0:nb], ALU.mult, ALU.add, [kq, y], [y])
                k.ts(y[:, 0:nb], y[:, 0:nb], -3.141592, 3.141592, ALU.max, ALU.min, reads=[y], writes=[y])
                k.activation(out[:, b0:b0 + nb], y[:, 0:nb], AF.Sin, [y], [out])

        sin_layer(w1, 17, ft, b1, h1)
        sin_layer(w2, 64, h1, b2, h2)
        accs = [g.PS[4], g.PS[5]]
        for tt in range(ntl):
            for hf in range(2):
                ps = g.PS[2 + hf]
                fl = fls[hf]
                k.mm(ps[:, :], h2[:, tt * 128:(tt + 1) * 128], w3[:, hf * 512:(hf + 1) * 512], True, True, [h2, w3], [ps])
                k.activation(dec[:], eld[:, hf * 512:(hf + 1) * 512], AF.Exp, [eld, ntn], [dec], scale=ntn[:, tt:tt + 1])
                k.tt(fl[:], ps[:, :], dec[:], ALU.mult, [ps, dec], [fl])
                if hf == 1 and tt == 0:
                    k.ts(fl[:], fl[:], g.hmsk[:, 1:2], None, ALU.mult, reads=[fl, g.hmsk], writes=[fl])
                k.activation(ab[:], fl[:], AF.Abs, [fl], [ab])
                k.mm(accs[hf][:, :], g.onesF[:, :], ab[:], tt == 0, tt == ntl - 1, [ab, g.onesF], [accs[hf]])
            k.tt(FBS[:, tt * 512:(tt + 1) * 512], fls[0][:], fls[1][:], ALU.add, [fls[0], fls[1]], [FBS], E=k.pool)
            k.tt(FBD[:, tt * 512:(tt + 1) * 512], fls[0][:], fls[1][:], ALU.subtract, [fls[0], fls[1]], [FBD])
        k.copy(rn[:], accs[0][:, :], [accs[0]], [rn], E=k.act)
        k.tt(rn[:], rn[:], accs[1][:, :], ALU.add, [rn, accs[1]], [rn])
        k.ts(rn[:], rn[:], EPS, None, ALU.add, reads=[rn], writes=[rn])
        k.op(k.dve, lambda: nc.vector.reciprocal(out=rn[:], in_=rn[:]), [rn], [rn])
        Fq = [k.sb(es, "Fq%d" % i, [128, n], BF16) for i in range(2)]
        for i in range(2):
            k.dma(k.sp, Fq[i][:], D["Fq" + sfx][i], (), [Fq[i]])
        for i in range(nfc):
            ip = i % npair
            isim = i >= npair
            ps = g.PS[i % 2]
            if i == npair:
                for kk in range(ntl):
                    k.mm(ps[:, :], Fq[0][:, kk * 128:(kk + 1) * 128], FBD[:, kk * 512:(kk + 1) * 512], kk == 0, False,
                         [Fq[0], FBD], [ps])
                for kk in range(ntl):
                    k.mm(ps[:, :], Fq[1][:, kk * 128:(kk + 1) * 128], FBS[:, kk * 512:(kk + 1) * 512], False, kk == ntl - 1,
                         [Fq[1], FBS], [ps])
            else:
                F_ = Fi[i % 3]
                k.dma(k.sp, F_[:], D["Fr" + sfx][i], (), [F_])
                FB_ = FBD if isim else FBS
                for kk in range(ntl):
                    k.mm(ps[:, :], F_[:, kk * 128:(kk + 1) * 128], FB_[:, kk * 512:(kk + 1) * 512], kk == 0, kk == ntl - 1,
                         [F_, FB_], [ps])
            Re_ = ReH[:, ip * 512:(ip + 1) * 512]
            if not isim:
                k.tt(Re_, ps[:, :], rn[:], ALU.mult, [ps, rn], [ReH])
            else:
                B_ = Bt[ip % 2]
                k.tt(B_[:], ps[:, :], rn[:], ALU.mult, [ps, rn], [B_])
                hs_ = D["HS" + sfx][ip]
                hr = g.HSR[sfx][ip]
                if ip > 0:
                    k.dma(k.pool, hs_[0], Re_, [ReH], [hr])
                    k.dma(k.pool, hs_[1], B_[:], [B_], [hr])
                    k.dma(k.pool, hs_[2], Re_, [ReH], [hr])
                else:
                    k.ts(Ct[:], Re_, g.hmsk[:, 1:2], None, ALU.mult, reads=[ReH, g.hmsk], writes=[Ct])
                    k.stt(Ct[:], B_[:], g.hmsk[:, 2:3], Ct[:], ALU.mult, ALU.add, [B_, Ct, g.hmsk], [Ct])
                    k.ts(B_[:], B_[:], g.hmsk[:, 1:2], None, ALU.mult, reads=[B_, g.hmsk], writes=[B_])
                    k.dma(k.pool, hs_[0], Re_, [ReH], [hr])
                    k.dma(k.pool, hs_[1], B_[:], [B_], [hr])
                    k.dma(k.pool, hs_[2], Ct[:], [Ct], [hr])
        k.barrier()


def hyena_convs(g, l, n, sfx, seqs):
    k, nc, D = g.k, g.nc, g.D
    ntl = n // 128
    npair = n // 128
    nfc = 2 * npair
    tbw = min(512, n)
    ntb = n // tbw
    with ExitStack() as es:
        hs = k.sb(es, "hsT", [128, 18], F32)
        hb = k.sb(es, "hbT", [128, 4], F32)
        k.dma(k.sp, hs[:], D["hsT"][l], (), [hs])
        k.dma(k.sp, hb[:], D["hbT"][l], (), [hb])
        U = k.sb(es, "hU", [128, 2 * n], F32)
        X = k.sb(es, "hX", [128, 2 * n], F32)
        xin = k.sb(es, "hxin", [128, n], F32)
        ubf = k.sb(es, "hubf", [128, 2 * n], BF16)
        utm = k.sb(es, "hutm", [128, ntl * 256], BF16)
        Yt = k.sb(es, "hYt", [128, nfc * 256], BF16)
        Gts = [k.sb(es, "hGt%d" % i, [128, nfc * tbw], BF16) for i in range(2)]
        Fi = [k.sb(es, "hFi%d" % i, [128, n], BF16) for i in range(4)]
        Hs = [k.sb(es, "hHs%d" % i, [128, 768], F32) for i in range(2)]
        vre = k.sb(es, "hvre", [128, 256], F32)
        vim = k.sb(es, "hvim", [128, 256], F32)
        t1 = k.sb(es, "ht1", [128, 256], F32)
        t2 = k.sb(es, "ht2", [128, 256], F32)
        ob = [k.sb(es, "hob%d" % i, [128, 512], BF16) for i in range(2)]

        def short_conv(dst, ch0, t0):
            for cc in range(2):
                ap_, rr = pj_rows(g, R_CV + (ch0 + cc) * 128, 128, t0, n)
                k.dma(k.sp, xin[:], ap_, rr, [xin])
                c3 = (ch0 + cc) * 3
                d_ = dst[:, cc * n:(cc + 1) * n]
                k.ts(d_, xin[:], hs[:, c3 + 1:c3 + 2], None, ALU.mult, reads=[xin, hs], writes=[dst])
                k.stt(d_[:, 1:n], xin[:, 0:n - 1], hs[:, c3:c3 + 1], d_[:, 1:n], ALU.mult, ALU.add, [xin, hs, dst], [dst])
                k.stt(d_[:, 0:n - 1], xin[:, 1:n], hs[:, c3 + 2:c3 + 3], d_[:, 0:n - 1], ALU.mult, ALU.add, [xin, hs, dst], [dst])

        def long_conv(o, t0, last):
            k.copy(ubf[:], U[:], [U], [ubf], E=k.act)
            for tt in range(ntl):
                for cc in range(2):
                    pT = g.PS[(tt * 2 + cc) % 2]
                    k.mm(pT[:, 0:128], ubf[:, cc * n + tt * 128: cc * n + (tt + 1) * 128], g.ident128[:, :], True, True,
                         [ubf, g.ident128], [pT])
                    k.copy(utm[:, tt * 256 + cc * 128: tt * 256 + (cc + 1) * 128], pT[:, 0:128], [pT], [utm],
                           E=(k.act if cc else k.dve))
            for ip in range(npair):
                Fre = Fi[(ip % 2) * 2]
                Fim = Fi[(ip % 2) * 2 + 1]
                k.dma(k.sp, Fre[:], D["Fr" + sfx][ip], (), [Fre])
                k.dma(k.sp, Fim[:], D["Fr" + sfx][npair + ip], (), [Fim])
                H_ = Hs[ip % 2]
                k.dma(k.sp, H_[:].rearrange("p (a c) -> p a c", a=3),
                      D["HS" + sfx][ip].rearrange("a p c -> p a c")[:, :, o * 256:(o + 1) * 256], [g.HSR[sfx][ip]], [H_])
                pre = g.PS[2 + (ip % 2) * 2]
                pim = g.PS[3 + (ip % 2) * 2]
                for kk in range(ntl):
                    k.mm(pre[:, 0:256], Fre[:, kk * 128:(kk + 1) * 128], utm[:, kk * 256:(kk + 1) * 256], kk == 0, kk == ntl - 1,
                         [Fre, utm], [pre])
                for kk in range(ntl):
                    k.mm(pim[:, 0:256], Fim[:, kk * 128:(kk + 1) * 128], utm[:, kk * 256:(kk + 1) * 256], kk == 0, kk == ntl - 1,
                         [Fim, utm], [pim])
                k.copy(vre[:], pre[:, 0:256], [pre], [vre], E=k.act)
                k.copy(vim[:], pim[:, 0:256], [pim], [vim], E=k.act)
                A_, B_, C_ = H_[:, 0:256], H_[:, 256:512], H_[:, 512:768]
                k.tt(t1[:], vre[:], A_, ALU.mult, [vre, H_], [t1])
                k.tt(t2[:], vim[:], B_, ALU.mult, [vim, H_], [t2], E=k.pool)
                k.tt(Yt[:, ip * 256:(ip + 1) * 256], t1[:], t2[:], ALU.subtract, [t1, t2], [Yt])
                k.tt(t1[:], vre[:], B_, ALU.mult, [vre, H_], [t1])
                k.tt(t2[:], vim[:], C_, ALU.mult, [vim, H_], [t2], E=k.pool)
                k.tt(Yt[:, (npair + ip) * 256:(npair + ip + 1) * 256], t1[:], t2[:], ALU.add, [t1, t2], [Yt])
            oi = 0
            for tb in range(ntb):
                Gt = Gts[tb % 2]
                for q in range(4):
                    w_ = nfc * tbw // 4
                    k.dma(k.sp, Gt[:, q * w_:(q + 1) * w_], D["Gr" + sfx][tb][:, q * w_:(q + 1) * w_], (), [Gt])
                for cc in range(2):
                    py = g.PS[6 + cc]
                    for fc in range(nfc):
                        k.mm(py[:, 0:tbw], Yt[:, fc * 256 + cc * 128: fc * 256 + (cc + 1) * 128], Gt[:, fc * tbw:(fc + 1) * tbw],
                             fc == 0, fc == nfc - 1, [Yt, Gt], [py])
                    sl = slice(cc * n + tb * tbw, cc * n + (tb + 1) * tbw)
                    k.stt(U[:, sl], U[:, sl], hb[:, o * 2 + cc:o * 2 + cc + 1], py[:, 0:tbw], ALU.mult, ALU.add, [U, hb, py], [U])
                    if not last:
                        k.tt(U[:, sl], U[:, sl], X[:, sl], ALU.mult, [U, X], [U])
                    else:
                        o_ = ob[oi % 2]
                        oi += 1
                        k.tt(o_[:, 0:tbw], U[:, sl], X[:, sl], ALU.mult, [U, X], [o_])
                        k.dma(k.pool, D["O"][4 + cc, :, t0 + tb * tbw:t0 + (tb + 1) * tbw], o_[:, 0:tbw], [o_], [g.OR[4 + cc]])

        for (t0, _, ctx) in seqs:
            short_conv(U, 0, t0)
            short_conv(X, 2, t0)
            long_conv(0, t0, False)
            short_conv(X, 4, t0)
            long_conv(1, t0, True)
        k.barrier()


def phase_hyena(g, l):
    for n, sfx, seqs in HYN:
        hyena_filters(g, l, n, sfx)
        hyena_convs(g, l, n, sfx, seqs)

def build(depth=DEPTH, mixers=("a", "b", "c", "d"), debug=False):
    nc = bass.Bass("TRN2", target_bir_lowering=False)
    D = {}

    def din(name, shape, dt=F32):
        D[name] = nc.dram_tensor(name, list(shape), dt, kind="ExternalInput").ap()

    def dout(name, shape, dt=F32):
        D[name] = nc.dram_tensor(name, list(shape), dt, kind="ExternalOutput").ap()

    def dscr(name, shape, dt=F32):
        kind = "ExternalOutput" if debug else "Internal"
        D[name] = nc.dram_tensor(name, list(shape), dt, kind=kind).ap()

    for name, shape in input_shapes().items():
        din(name, shape, BF16 if name[:2] in ("Fr", "Gr", "Fq") else F32)
    dout("yT", [8, 128, NT])
    dout("mlaT", [DEPTH, 160, NPR])
    dout("dkT", [DEPTH, 256, NPR])
    dout("dvT", [DEPTH, 256, NPR])
    dout("hg", [DEPTH, 4, 2, 4, 64, 64])
    dscr("PJ", [NCC, 128, NT])
    dscr("PT", [NT, 1024])
    dscr("O", [8, 128, NT], BF16)
    for nm, shp, dt in extra_scratch():
        dscr(nm, shp, dt)

    with ExitStack() as es:
        k = K(nc, es)
        g = Ctx()
        g.k, g.nc, g.D = k, nc, D
        g.PS = [k.psum(es, "ps%d" % i, [128, 512]) for i in range(8)]
        g.modv = [k.sb(es, "modv%d" % l, [128, 144], F32) for l in range(DEPTH)]
        g.XR = [[Res() for _ in range(NTB)] for _ in range(8)]
        g.PJR = [Res() for _ in range(NCC)]
        g.PTR = Res()
        g.OR = [Res() for _ in range(8)]
        g.xwritten = set()
        g.ones_ms = k.sb(es, "ones_ms", [128, 128], BF16)
        k.memset(g.ones_ms[:], 1.0 / 1024.0, [g.ones_ms])
        g.eps_t = k.sb(es, "eps_t", [128, 1], F32)
        k.memset(g.eps_t[:], EPS, [g.eps_t])
        setup_consts(g, es)
        phase_mod(g)
        if debug == "mod":
            for l in range(DEPTH):
                k.dma(k.sp, D["PT"][l * 128:(l + 1) * 128, 0:144], g.modv[l][:], [g.modv[l]], [])
            depth = 0
        for l in range(depth):
            phase_ffn(g, l, 0)
            phase_proj(g, l)
            phase_mixers(g, l, mixers)
            phase_out(g, l)
            phase_ffn(g, l, 1)
        k.finish()
        g.stats = (k.n_inst, k.n_wait)
    return nc, g


def input_shapes():
    s = {
        "xT": [8, 128, NT], "cT": [128, 16], "w_mod": [DEPTH, 1024, 9216], "bmodT": [DEPTH, 128, 72],
        "normgT": [DEPTH, 128, 24], "wgu": [DEPTH, 2, NJ, 128, 2048], "wd": [DEPTH, 2, 8, 128, NJ * 128],
        "win": [DEPTH, NCC, 128, 1024], "wtm": [DEPTH, 128, 8192], "w_out": [DEPTH, 1024, 1024],
    }
    s.update(mixer_input_shapes())
    return s


def host_shared(inp):
    f = np.float32
    S = {}
    S["w_mod"] = np.ascontiguousarray(inp["w_mod"], dtype=f)
    S["bmodT"] = np.ascontiguousarray(inp["b_mod"].reshape(DEPTH, 72, 128).transpose(0, 2, 1), dtype=f)
    S["normgT"] = np.ascontiguousarray(inp["norm_g"].reshape(DEPTH, 24, 128).transpose(0, 2, 1), dtype=f)
    wgu = inp["ffn_w_gu"].reshape(DEPTH, 2, 8, 128, 2, NJ, 128)
    S["wgu"] = np.ascontiguousarray(wgu.transpose(0, 1, 5, 3, 4, 2, 6).reshape(DEPTH, 2, NJ, 128, 2048), dtype=f)
    wd = inp["ffn_w_down"].reshape(DEPTH, 2, NJ, 128, 8, 128)
    S["wd"] = np.ascontiguousarray(wd.transpose(0, 1, 4, 3, 2, 5).reshape(DEPTH, 2, 8, 128, NJ * 128), dtype=f)
    win = np.zeros((DEPTH, 1024, NCC * 128), f)
    win[:, :, :INW] = inp["w_in"]
    win = win.reshape(DEPTH, 8, 128, NCC, 128)
    S["win"] = np.ascontiguousarray(win.transpose(0, 3, 2, 1, 4).reshape(DEPTH, NCC, 128, 1024))
    wi = inp["w_in"]
    tm = np.concatenate([wi[:, :, R_AI:R_AI + 256], wi[:, :, R_AFF:R_AFF + 256], wi[:, :, R_AFB:R_AFB + 256],
                         wi[:, :, R_DV:R_DV + 256]], axis=2)
    tm = tm.reshape(DEPTH, 8, 128, 1024).transpose(0, 2, 1, 3)
    S["wtm"] = np.ascontiguousarray(tm.reshape(DEPTH, 128, 8192), dtype=f)
    S["w_out"] = np.ascontiguousarray(inp["w_out"], dtype=f)
    S.update(mixer_host_shared(inp))
    return S


def host_core(inp, core):
    f = np.float32
    C = {}
    xp = inp["x_prompt"][core * 4:(core + 1) * 4].reshape(NPR, 1024)
    xs = inp["x_sample"][core]
    x = np.concatenate([xp, xs], axis=0)
    C["xT"] = np.ascontiguousarray(x.T.reshape(8, 128, NT), dtype=f)
    cc = np.stack([inp["c_ctx"], inp["c"][core]], axis=1)
    C["cT"] = np.ascontiguousarray(cc.reshape(8, 128, 2).transpose(1, 0, 2).reshape(128, 16), dtype=f)
    C.update(mixer_host_core(inp, core))
    return C


_CACHE = {}


def kernel(**inp):
    inp = {k_: np.asarray(v) for k_, v in inp.items()}
    if "nc" not in _CACHE:
        _CACHE["nc"] = build()[0]
    nc = _CACHE["nc"]
    S = host_shared(inp)
    in_maps = []
    for core in range(8):
        m = dict(S)
        m.update(host_core(inp, core))
        in_maps.append(m)
    res = run_bass_kernel_spmd(nc, in_maps, core_ids=list(range(8)))
    R = res.results
    f = np.float32
    yp = np.zeros((32, 256, 1024), f)
    ys = np.zeros((8, 2048, 1024), f)
    mla = np.zeros((32, DEPTH, 256, 160), f)
    dk = np.zeros((32, DEPTH, 256, 4, 2, 32), f)
    dv = np.zeros((32, DEPTH, 256, 4, 64), f)
    hg = np.zeros((32, DEPTH, 2, 4, 64, 64), f)
    for core in range(8):
        r = R[core]
        y = np.asarray(r["yT"]).reshape(1024, NT).T
        yp[core * 4:(core + 1) * 4] = y[:NPR].reshape(4, 256, 1024)
        ys[core] = y[NPR:]
        m_ = np.asarray(r["mlaT"]).reshape(DEPTH, 160, 4, 256)
        mla[core * 4:(core + 1) * 4] = m_.transpose(2, 0, 3, 1)
        k_ = np.asarray(r["dkT"]).reshape(DEPTH, 256, 4, 256)
        dk[core * 4:(core + 1) * 4] = k_.transpose(2, 0, 3, 1).reshape(4, DEPTH, 256, 4, 2, 32)
        v_ = np.asarray(r["dvT"]).reshape(DEPTH, 256, 4, 256)
        dv[core * 4:(core + 1) * 4] = v_.transpose(2, 0, 3, 1).reshape(4, DEPTH, 256, 4, 64)
        h_ = np.asarray(r["hg"])
        hg[core * 4:(core + 1) * 4] = h_.transpose(1, 0, 2, 3, 4, 5)
    return (yp, ys, mla, dk, dv, hg)
```

```python
from concourse.bass_utils import run_bass_kernel_spmd
import ml_dtypes
import numpy as np
from contextlib import ExitStack
import concourse.bass as bass
import concourse.mybir as mybir

F32 = mybir.dt.float32
BF16 = mybir.dt.bfloat16
I32 = mybir.dt.int32
AF = mybir.ActivationFunctionType
ALU = mybir.AluOpType
AX = mybir.AxisListType


class Res:
    __slots__ = ("w", "rs")

    def __init__(self):
        self.w = None
        self.rs = {}


class T:
    __slots__ = ("t", "res")

    def __init__(self, t, res=None):
        self.t = t
        self.res = res if res is not None else Res()

    def __getitem__(self, idx):
        return self.t[idx]


class Eng:
    def __init__(self, k, eng, name, inorder):
        self.k = k
        self.eng = eng
        self.name = name
        self.inorder = inorder
        self.sem = k.new_sem("s_" + name)
        self.semid = k.semid(self.sem)
        self.cnt = 0
        self.pending = False
        self.seen = {}
        self.dsems = None
        self.dvals = None
        self.di = 0

    def init_dma(self, ns):
        self.dsems = [self.k.new_sem("d_%s_%d" % (self.name, i)) for i in range(ns)]
        self.dids = [self.k.semid(s) for s in self.dsems]
        self.dvals = [0] * ns


class K:
    def __init__(self, nc, es):
        self.nc = nc
        self.es = es
        self._sems = {}
        self._nsem = 0
        self.pe = Eng(self, nc.tensor, "pe", True)
        self.act = Eng(self, nc.scalar, "act", False)
        self.dve = Eng(self, nc.vector, "dve", False)
        self.pool = Eng(self, nc.gpsimd, "pool", False)
        self.sp = Eng(self, nc.sync, "sp", False)
        self.engs = [self.pe, self.act, self.dve, self.pool, self.sp]
        self.sp.init_dma(16)
        self.pool.init_dma(16)
        self.act.init_dma(8)
        self.all_dma_toks = []
        self.same_engine_sync = True
        self.n_inst = 0
        self.n_wait = 0

    def new_sem(self, name):
        s = self.es.enter_context(self.nc.semaphore(name))
        self._nsem += 1
        self._sems[id(s)] = self._nsem
        return s

    def semid(self, s):
        return self._sems[id(s)]

    def need(self, E, tok):
        if tok is None:
            return
        sem, val, sid = tok
        if sid == E.semid and (E.inorder or not self.same_engine_sync):
            return
        if E.seen.get(sid, 0) >= val:
            return
        E.eng.wait_ge(sem, val)
        self.n_wait += 1
        E.seen[sid] = val

    def _deps(self, E, reads, writes):
        for r in reads:
            r = r.res if isinstance(r, T) else r
            self.need(E, r.w)
        for w in writes:
            w = w.res if isinstance(w, T) else w
            self.need(E, w.w)
            for tok in w.rs.values():
                self.need(E, tok)

    def _post(self, tok, reads, writes):
        sid = tok[2]
        for r in reads:
            r = r.res if isinstance(r, T) else r
            old = r.rs.get(sid)
            if old is None or old[1] < tok[1]:
                r.rs[sid] = tok
        for w in writes:
            w = w.res if isinstance(w, T) else w
            w.w = tok
            w.rs = {}

    def op(self, E, fn, reads=(), writes=(), signal=True):
        self._deps(E, reads, writes)
        inst = fn()
        self.n_inst += 1
        tok = (E.sem, E.cnt + 1, E.semid)
        if signal:
            inst.then_inc(E.sem, 1)
            E.cnt += 1
            E.pending = False
        else:
            E.pending = True
        self._post(tok, reads, writes)
        return inst

    def dma(self, Q, out, in_, reads=(), writes=(), **kw):
        self._deps(Q, reads, writes)
        ns = len(Q.dsems)
        slot = Q.di % ns
        Q.di += 1
        sem = Q.dsems[slot]
        sid = Q.dids[slot]
        if Q.dvals[slot] > 0:
            self.need(Q, (sem, Q.dvals[slot], sid))
        inst = Q.eng.dma_start(out=out, in_=in_, **kw)
        inst.then_inc(sem, 16)
        self.n_inst += 1
        Q.dvals[slot] += 16
        tok = (sem, Q.dvals[slot], sid)
        self._post(tok, reads, writes)
        return inst

    def barrier(self):
        toks = []
        for F in self.engs:
            assert not F.pending, F.name
            if F.cnt > 0:
                toks.append((F.sem, F.cnt, F.semid))
            if F.dsems is not None:
                for s, v, i in zip(F.dsems, F.dvals, F.dids):
                    if v > 0:
                        toks.append((s, v, i))
        for E in self.engs:
            for tok in toks:
                if tok[2] == E.semid:
                    if E.inorder:
                        continue
                self.need(E, tok)

    def finish(self):
        self.barrier()

    def sb(self, es, name, shape, dtype):
        self._uid = getattr(self, "_uid", 0) + 1
        name = "sb%d_%s" % (self._uid, name)
        t = es.enter_context(self.nc.sbuf_tensor(name, shape, dtype))
        return T(t)

    def psum(self, es, name, shape, dtype=F32):
        self._uid = getattr(self, "_uid", 0) + 1
        name = "pp%d_%s" % (self._uid, name)
        t = es.enter_context(self.nc.psum_tensor(name, shape, dtype))
        return T(t)

    def mm(self, out, lhsT, rhs, start, stop, reads, writes, signal=None, **kw):
        if signal is None:
            signal = True
        return self.op(self.pe, lambda: self.nc.tensor.matmul(out, lhsT, rhs, start=start, stop=stop, **kw),
                       reads, writes, signal=signal)

    def transpose(self, out, in_, ident, reads, writes, signal=True):
        return self.op(self.pe, lambda: self.nc.tensor.transpose(out, in_, ident), reads, writes, signal=signal)

    def activation(self, out, in_, func, reads, writes, bias=None, scale=None, accum_out=None, E=None):
        E = E or self.act
        kw = {}
        if bias is not None:
            kw["bias"] = bias
        if scale is not None:
            kw["scale"] = scale
        if accum_out is not None:
            kw["accum_out"] = accum_out
        return self.op(E, lambda: E.eng.activation(out=out, in_=in_, func=func, **kw), reads, writes)

    def tt(self, out, in0, in1, op, reads, writes, E=None):
        E = E or self.dve
        return self.op(E, lambda: E.eng.tensor_tensor(out=out, in0=in0, in1=in1, op=op), reads, writes)

    def ts(self, out, in0, s1, s2, op0, op1=None, reads=(), writes=(), E=None, accum_out=None):
        E = E or self.dve
        kw = {}
        if op1 is not None:
            kw["op1"] = op1
        if accum_out is not None:
            kw["accum_out"] = accum_out
        return self.op(E, lambda: E.eng.tensor_scalar(out=out, in0=in0, scalar1=s1, scalar2=s2, op0=op0, **kw),
                       reads, writes)

    def stt(self, out, in0, scalar, in1, op0, op1, reads, writes):
        E = self.dve
        return self.op(E, lambda: E.eng.scalar_tensor_tensor(out=out, in0=in0, scalar=scalar, in1=in1, op0=op0, op1=op1),
                       reads, writes)

    def copy(self, out, in_, reads, writes, E=None):
        E = E or self.dve
        if E is self.act:
            return self.op(E, lambda: E.eng.copy(out=out, in_=in_), reads, writes)
        return self.op(E, lambda: E.eng.tensor_copy(out=out, in_=in_), reads, writes)

    def memset(self, ap, val, writes, E=None):
        E = E or self.dve
        return self.op(E, lambda: E.eng.memset(ap, val), (), writes)

DEPTH = 4
NT = 3072
NPR = 1024
NTB = 6
DFF = 2816
NJ = 22
EPS = 1e-6
INW = 3232
NCC = 26
R_AQ, R_AI, R_AFF, R_AFB, R_AOG = 0, 256, 512, 768, 1024
R_CQ, R_CKV, R_KR = 1280, 1536, 1664
R_CV, R_CX1, R_CX2 = 1696, 1952, 2208
R_DQ, R_DK, R_DV = 2464, 2720, 2976


class Ctx:
    pass


def mvec(g, l, kind, m, cond):
    i = (kind * 8 + m) * 2 + cond
    return g.modv[l][:, i:i + 1]


def xsrc_ap(g, m, tb):
    base = g.D["yT"] if (m, tb) in g.xwritten else g.D["xT"]
    return base[m, :, tb * 512:(tb + 1) * 512]


def phase_mod(g):
    k, nc, D = g.k, g.nc, g.D
    with ExitStack() as es:
        cT = k.sb(es, "cT", [128, 16], F32)
        sc = k.sb(es, "scT", [128, 16], BF16)
        k.dma(k.sp, cT[:], D["cT"], (), [cT])
        k.activation(sc[:], cT[:], AF.Silu, [cT], [sc])
        wm = [k.sb(es, "wm%d" % i, [128, 8 * 512], BF16) for i in range(3)]
        bm = [k.sb(es, "bm%d" % i, [128, 72], F32) for i in range(2)]
        ng = [k.sb(es, "ng%d" % i, [128, 24], F32) for i in range(2)]
        wsrc = D["w_mod"]
        for l in range(DEPTH):
            b_, n_ = bm[l % 2], ng[l % 2]
            k.dma(k.sp, b_[:], D["bmodT"][l], (), [b_])
            k.dma(k.sp, n_[:], D["normgT"][l], (), [n_])
            ps = g.PS[l % 2]
            wl = wsrc[l].rearrange("(k p) c -> p k c", p=128)
            for pc in range(18):
                w = wm[pc % 3]
                k.dma(k.pool, w[:].rearrange("p (k c) -> p k c", k=8), wl[:, :, pc * 512:(pc + 1) * 512], (), [w])
                for q4 in range(4):
                    q = pc * 4 + q4
                    for kk in range(8):
                        k.mm(ps[:, q * 2:q * 2 + 2], w[:, kk * 512 + q4 * 128: kk * 512 + (q4 + 1) * 128],
                             sc[:, kk * 2:kk * 2 + 2], kk == 0, kk == 7, [w, sc], [ps])
            mv = g.modv[l]
            mv3 = mv[:].rearrange("p (q c) -> p q c", c=2)
            ps3 = ps[:, 0:144].rearrange("p (q c) -> p q c", c=2)
            for cond in range(2):
                k.tt(mv3[:, :, cond], ps3[:, :, cond], b_[:], ALU.add, [ps, b_], [mv])
            for k3 in range(3):
                q0 = (3 * k3 + 1) * 8
                for cond in range(2):
                    k.stt(mv3[:, q0:q0 + 8, cond], mv3[:, q0:q0 + 8, cond], 1.0, n_[:, k3 * 8:(k3 + 1) * 8],
                          ALU.add, ALU.mult, [mv, n_], [mv])
            for k3 in (0, 2):
                q0 = (3 * k3 + 2) * 8
                k.ts(mv[:, q0 * 2:(q0 + 8) * 2], mv[:, q0 * 2:(q0 + 8) * 2], 0.5, None, ALU.mult, reads=[mv], writes=[mv])
        k.barrier()


def phase_norm(g, l, kn, h):
    k, nc, D = g.k, g.nc, g.D
    with ExitStack() as es:
        xt = [k.sb(es, "nx%d" % i, [128, 8 * 512], F32) for i in range(2)]
        sq = [k.sb(es, "nsq%d" % i, [128, 512], BF16) for i in range(4)]
        rts = [k.sb(es, "nrt%d" % i, [128, 512], F32) for i in range(2)]
        rss = [k.sb(es, "nrs%d" % i, [128, 512], F32) for i in range(2)]
        tmp = [k.sb(es, "ntmp%d" % i, [128, 512], F32) for i in range(4)]
        for tb in range(NTB):
            cond = 0 if tb < 2 else 1
            x = xt[tb % 2]
            for m in range(8):
                k.dma(k.sp, x[:, m * 512:(m + 1) * 512], xsrc_ap(g, m, tb), [g.XR[m][tb]], [x])
            ps = g.PS[6 + tb % 2]
            rt = rts[tb % 2]
            rs = rss[tb % 2]
            for m in range(8):
                s = sq[m % 4]
                xm = x[:, m * 512:(m + 1) * 512]
                if m % 2 == 0:
                    k.tt(s[:], xm, xm, ALU.mult, [x], [s], E=k.pool)
                else:
                    k.activation(s[:], xm, AF.Square, [x], [s])
                k.mm(ps[:], g.ones_ms[:], s[:], m == 0, m == 7, [s], [ps])
            k.activation(rt[:], ps[:], AF.Sqrt, [ps], [rt], bias=g.eps_t[:, 0:1])
            k.op(k.dve, lambda: nc.vector.reciprocal(out=rs[:], in_=rt[:]), [rt], [rs])
            for m in range(8):
                t = tmp[m % 4]
                k.tt(t[:], x[:, m * 512:(m + 1) * 512], rs[:], ALU.mult, [x, rs], [t], E=(k.pool if m % 4 == 3 else k.dve))
                k.activation(h[m][:, tb * 512:(tb + 1) * 512], t[:], AF.Identity, [t], [h[m]],
                             bias=mvec(g, l, 3 * kn, m, cond), scale=mvec(g, l, 3 * kn + 1, m, cond))
        k.barrier()


def resid_update(g, l, gate_kind, m, tb, po, xo, xn):
    k, D = g.k, g.D
    cond = 0 if tb < 2 else 1
    k.dma(k.sp, xo[:], xsrc_ap(g, m, tb), [g.XR[m][tb]], [xo])
    k.stt(xn[:], po[:], mvec(g, l, gate_kind, m, cond), xo[:], ALU.mult, ALU.add, [po, xo], [xn])
    g.xwritten.add((m, tb))
    k.dma(k.pool, D["yT"][m, :, tb * 512:(tb + 1) * 512], xn[:], [xn], [g.XR[m][tb]])


def phase_ffn(g, l, w):
    k, nc, D = g.k, g.nc, g.D
    kn = 0 if w == 0 else 2
    with ExitStack() as es:
        h = [k.sb(es, "h%d" % m, [128, NT], BF16) for m in range(8)]
        phase_norm(g, l, kn, h)
        act = [k.sb(es, "a%d" % j, [128, NT], BF16) for j in range(11)]
        wt = [k.sb(es, "fw%d" % i, [128, 2048], BF16) for i in range(2)]
        sa = [k.sb(es, "fsa%d" % i, [128, 512], F32) for i in range(2)]
        wdt = [k.sb(es, "fwd%d" % i, [128, 11 * 128], BF16) for i in range(2)]
        xo = [k.sb(es, "fxo%d" % i, [128, 512], F32) for i in range(4)]
        xn = [k.sb(es, "fxn%d" % i, [128, 512], F32) for i in range(4)]
        it = 0
        it2 = 0

        def load_wgu(j):
            k.dma(k.pool, wt[j % 2][:], D["wgu"][l, w, j], (), [wt[j % 2]])

        def load_wd(half, m):
            k.dma(k.pool, wdt[m % 2][:], D["wd"][l, w, m][:, half * 1408:(half + 1) * 1408], (), [wdt[m % 2]])

        load_wgu(0)
        for half in range(2):
            for jj in range(11):
                j = half * 11 + jj
                wj = wt[j % 2]
                if jj < 10:
                    load_wgu(j + 1)
                else:
                    load_wd(half, 0)
                for tb in range(NTB):
                    pa = g.PS[(it % 2) * 2]
                    pu = g.PS[(it % 2) * 2 + 1]
                    s = sa[it % 2]
                    it += 1
                    for kk in range(8):
                        k.mm(pa[:], wj[:, kk * 128:(kk + 1) * 128], h[kk][:, tb * 512:(tb + 1) * 512],
                             kk == 0, kk == 7, [wj, h[kk]], [pa])
                    for kk in range(8):
                        k.mm(pu[:], wj[:, 1024 + kk * 128:1024 + (kk + 1) * 128], h[kk][:, tb * 512:(tb + 1) * 512],
                             kk == 0, kk == 7, [wj, h[kk]], [pu])
                    k.activation(s[:], pa[:], AF.Silu, [pa], [s])
                    k.tt(act[jj][:, tb * 512:(tb + 1) * 512], s[:], pu[:], ALU.mult, [s, pu], [act[jj]])
            for m in range(8):
                wm_ = wdt[m % 2]
                if m < 7:
                    load_wd(half, m + 1)
                elif half == 0:
                    load_wgu(11)
                for tb in range(NTB):
                    po = g.PS[4 + it2 % 2]
                    for jj in range(11):
                        k.mm(po[:], wm_[:, jj * 128:(jj + 1) * 128], act[jj][:, tb * 512:(tb + 1) * 512],
                             jj == 0, jj == 10, [wm_, act[jj]], [po])
                    resid_update(g, l, 3 * kn + 2, m, tb, po, xo[it2 % 4], xn[it2 % 4])
                    it2 += 1
        k.barrier()


def phase_proj(g, l):
    k, nc, D = g.k, g.nc, g.D
    with ExitStack() as es:
        h = [k.sb(es, "h%d" % m, [128, NT], BF16) for m in range(8)]
        phase_norm(g, l, 1, h)
        wt = [k.sb(es, "pw%d" % i, [128, 1024], BF16) for i in range(2)]
        st = [k.sb(es, "pst%d" % i, [128, NT], F32) for i in range(2)]
        wtm = k.sb(es, "pwtm", [128, 8192], BF16)
        st2 = [k.sb(es, "pst2%d" % i, [128, 1024], F32) for i in range(2)]
        for q in range(4):
            k.dma(k.pool, wtm[:, q * 2048:(q + 1) * 2048], D["wtm"][l][:, q * 2048:(q + 1) * 2048], (), [wtm])
        it = 0
        for cc in range(NCC):
            wj = wt[cc % 2]
            k.dma(k.pool, wj[:], D["win"][l, cc], (), [wj])
            s = st[cc % 2]
            for tb in range(NTB):
                ps = g.PS[it % 4]
                for kk in range(8):
                    k.mm(ps[:], wj[:, kk * 128:(kk + 1) * 128], h[kk][:, tb * 512:(tb + 1) * 512],
                         kk == 0, kk == 7, [wj, h[kk]], [ps])
                k.copy(s[:, tb * 512:(tb + 1) * 512], ps[:], [ps], [s], E=(k.act if it % 2 == 0 else k.dve))
                it += 1
            k.dma(k.sp, D["PJ"][cc], s[:], [s], [g.PJR[cc]])
        for tt in range(NT // 128):
            s2 = st2[tt % 2]
            for hf in range(2):
                ps = g.PS[4 + it % 4]
                for kk in range(8):
                    k.mm(ps[:], h[kk][:, tt * 128:(tt + 1) * 128], wtm[:, kk * 1024 + hf * 512: kk * 1024 + (hf + 1) * 512],
                         kk == 0, kk == 7, [wtm, h[kk]], [ps])
                k.copy(s2[:, hf * 512:(hf + 1) * 512], ps[:], [ps], [s2], E=(k.act if it % 2 == 0 else k.dve))
                it += 1
            k.dma(k.sp, D["PT"][tt * 128:(tt + 1) * 128, :], s2[:], [s2], [g.PTR])
        k.barrier()


def phase_out(g, l):
    k, nc, D = g.k, g.nc, g.D
    with ExitStack() as es:
        wo = k.sb(es, "wo", [128, 8192], BF16)
        wsrc = D["w_out"][l].rearrange("(k p) c -> p k c", p=128)
        for kk in range(8):
            k.dma(k.pool, wo[:, kk * 1024:(kk + 1) * 1024], wsrc[:, kk, :], (), [wo])
        ot = [k.sb(es, "oo%d" % i, [128, 8 * 512], BF16) for i in range(2)]
        xo = [k.sb(es, "oxo%d" % i, [128, 512], F32) for i in range(4)]
        xn = [k.sb(es, "oxn%d" % i, [128, 512], F32) for i in range(4)]
        it = 0
        for tb in range(NTB):
            o = ot[tb % 2]
            for kk in range(8):
                k.dma(k.sp, o[:, kk * 512:(kk + 1) * 512], D["O"][kk, :, tb * 512:(tb + 1) * 512], [g.OR[kk]], [o])
            for m in range(8):
                po = g.PS[it % 4]
                for kk in range(8):
                    k.mm(po[:], wo[:, kk * 1024 + m * 128: kk * 1024 + (m + 1) * 128], o[:, kk * 512:(kk + 1) * 512],
                         kk == 0, kk == 7, [wo, o], [po])
                resid_update(g, l, 5, m, tb, po, xo[it % 4], xn[it % 4])
                it += 1
        k.barrier()

SEQS = [(0, 256, False), (256, 256, False), (512, 256, False), (768, 256, False), (1024, 2048, True)]
GROUPS = [(0, 1024, False, [(0, 256), (256, 256), (512, 256), (768, 256)]), (1024, 2048, True, [(0, 2048)])]
MAXKEY = 1.0 - 1e-6


def mixer_input_shapes():
    return {
        "ropeB": [2, 96, 2048], "ropeD": [2, 128, 2048], "rotB": [96, 96], "rotD": [128, 128], "bd32": [128, 128], "bd64": [128, 128], "esel": [32, 96],
        "wuq": [DEPTH, 128, 768], "wukv": [DEPTH, 128, 512], "gains": [DEPTH, 128, 8], "dlam": [DEPTH, 32, 4],
        "cmlaT": [DEPTH, 160, 512], "cdkT": [DEPTH, 256, 512], "cdv": [DEPTH, 512, 256],
        "s0": [DEPTH, 2, 4, 64, 64], "lbl": [64, 32], "gon": [DEPTH, 64, 1], "mask4": [2, 128, 256],
        "maskbd": [2, 128, 128], "ident": [64, 64],
        **hyena_input_shapes(),
    }


def extra_scratch():
    return hyena_scratch()


def _rope_tables(rot_dim, ngroups_before):
    n_f = rot_dim // 4
    inv = (10000.0 ** (-np.arange(n_f, dtype=np.float32) / n_f)).astype(np.float32)
    t = np.arange(2048)
    ang_r = (t // 64).astype(np.float32)[:, None] * inv
    ang_c = (t % 64).astype(np.float32)[:, None] * inv
    half = rot_dim // 2
    C = np.zeros((rot_dim, 2048), np.float32)
    S_ = np.zeros((rot_dim, 2048), np.float32)
    for d in range(rot_dim):
        ang = ang_r if d < half else ang_c
        i = (d % half) % n_f
        C[d] = np.cos(ang[:, i])
        S_[d] = np.sin(ang[:, i])
    R = np.zeros((rot_dim, rot_dim), np.float32)
    for d in range(rot_dim):
        dd = d % half
        if dd < n_f:
            R[d + n_f, d] = -1.0
        else:
            R[d - n_f, d] = 1.0
    return C, S_, R


def mixer_host_shared(inp):
    f = np.float32
    S = {}
    C, Sn, R = _rope_tables(32, 0)
    ropeB = np.zeros((2, 96, 2048), f)
    ropeB[0, :64] = 1.0
    ropeB[0, 64:] = C
    ropeB[1, 64:] = Sn
    S["ropeB"] = ropeB
    S["ropeD"] = np.ascontiguousarray(np.tile(np.stack([C, Sn]).astype(f), (1, 4, 1)))
    rotB = np.zeros((96, 96), f)
    rotB[64:, 64:] = R
    S["rotB"] = rotB
    S["rotD"] = np.kron(np.eye(4, dtype=f), R.astype(f))
    S["bd32"] = np.kron(np.eye(4, dtype=f), np.full((32, 32), 1.0 / 32.0, f))
    S["bd64"] = np.kron(np.eye(2, dtype=f), np.full((64, 64), 1.0 / 64.0, f))
    es_ = np.zeros((32, 96), f)
    es_[np.arange(32), 64 + np.arange(32)] = 1.0
    S["esel"] = es_
    S["wuq"] = np.ascontiguousarray(inp["mla_w_uq"].reshape(DEPTH, 2, 128, 384).transpose(0, 2, 1, 3).reshape(DEPTH, 128, 768), dtype=f)
    S["wukv"] = np.ascontiguousarray(inp["mla_w_ukv"], dtype=f)
    gains = np.zeros((DEPTH, 128, 8), f)
    gains[:, :, 0:2] = inp["mla_q_norm"].reshape(DEPTH, 2, 128).transpose(0, 2, 1)
    gains[:, :, 2] = inp["mla_kv_norm"]
    gains[:, :96, 3:5] = inp["mla_qk_norm"].transpose(0, 2, 1)
    gains[:, :, 5:7] = np.tile(inp["diff_qk_norm"].transpose(0, 2, 1), (1, 4, 1))
    gains[:, :64, 7] = inp["diff_subln"]
    S["gains"] = gains
    S["dlam"] = np.ascontiguousarray(inp["diff_lambda"].transpose(0, 2, 1), dtype=f)
    lb = inp["hgrn_lb_logits"].reshape(2, DEPTH, 4, 64)
    S["lbl"] = np.ascontiguousarray(lb.transpose(3, 0, 2, 1).reshape(64, 32), dtype=f)
    S["gon"] = np.ascontiguousarray(inp["hgrn_onorm"].reshape(DEPTH, 64, 1), dtype=f)
    s_ = np.arange(128)
    m4 = np.zeros((2, 128, 4, 64), f)
    for j in range(4):
        m4[0, s_ // 32 == j, j, :] = 1.0
        m4[1, s_ // 32 == 3 - j, j, :] = 1.0
    S["mask4"] = m4.reshape(2, 128, 256)
    same = (s_[:, None] // 32) == (s_[None, :] // 32)
    S["maskbd"] = np.stack([same & (s_[:, None] <= s_[None, :]), same & (s_[:, None] >= s_[None, :])]).astype(f)
    S["ident"] = np.eye(64, dtype=f)
    S.update(hyena_host_shared(inp))
    return S


def mixer_host_core(inp, core):
    f = np.float32
    C = {}
    C["cmlaT"] = np.ascontiguousarray(inp["cache_mla"][core].transpose(0, 2, 1), dtype=f)
    C["cdkT"] = np.ascontiguousarray(inp["cache_diff_k"][core].reshape(DEPTH, 512, 256).transpose(0, 2, 1), dtype=f)
    C["cdv"] = np.ascontiguousarray(inp["cache_diff_v"][core].reshape(DEPTH, 512, 256), dtype=f)
    C["s0"] = np.ascontiguousarray(inp["state_hgrn"][core], dtype=f)
    return C


def setup_consts(g, es):
    k, D = g.k, g.D
    g.ones = {}
    for gs in (256, 128, 96, 64, 32):
        t = k.sb(es, "ones%d" % gs, [128, 128], BF16)
        k.memset(t[:], 1.0 / gs, [t])
        g.ones[gs] = t
    g.rotB = k.sb(es, "rotB", [96, 96], BF16)
    g.rotD = k.sb(es, "rotD", [128, 128], BF16)
    g.bd32 = k.sb(es, "bd32", [128, 128], BF16)
    k.dma(k.pool, g.bd32[:], D["bd32"], (), [g.bd32])
    g.bd64 = k.sb(es, "bd64", [128, 128], BF16)
    k.dma(k.pool, g.bd64[:], D["bd64"], (), [g.bd64])
    g.esel = k.sb(es, "esel", [32, 96], BF16)
    k.dma(k.pool, g.rotB[:], D["rotB"], (), [g.rotB])
    k.dma(k.pool, g.rotD[:], D["rotD"], (), [g.rotD])
    k.dma(k.pool, g.esel[:], D["esel"], (), [g.esel])
    g.onesf = k.sb(es, "onesf", [32, 64], F32)
    k.memset(g.onesf[:], 1.0, [g.onesf])
    hgrn_setup(g, es)
    hyena_setup(g, es)


def pj_rows(g, r0, n, t0, nt):
    flat = g.D["PJ"].rearrange("c p t -> (c p) t")
    res = [g.PJR[c] for c in range(r0 // 128, (r0 + n - 1) // 128 + 1)]
    return flat[r0:r0 + n, t0:t0 + nt], res


def group_norm_rope(g, W, src, srcres, gs, n, gain, rope, out, outres, rope_off=0, f32out=None, f32res=None, onesT=None,
                    gres=()):
    k, nc = g.k, g.nc
    sq, rs, t1, t2, pn, pr = W["sq"], W["rs"], W["t1"], W["t2"], W["pn"], W["pr"]
    k.activation(sq[0:gs, 0:n], src, AF.Square, srcres, [sq])
    oT = onesT if onesT is not None else g.ones[gs]
    k.mm(pn[0:gs, 0:n], oT[0:gs, 0:gs], sq[0:gs, 0:n], True, True, [sq, oT], [pn])
    k.activation(rs[0:gs, 0:n], pn[0:gs, 0:n], AF.Ln, [pn], [rs], bias=g.eps_t[0:gs, 0:1])
    k.activation(rs[0:gs, 0:n], rs[0:gs, 0:n], AF.Exp, [rs], [rs], scale=-0.5)
    if rope is None:
        if f32out is not None:
            k.stt(f32out, src, gain, rs[0:gs, 0:n], ALU.mult, ALU.mult, list(srcres) + [rs] + list(gres), f32res)
            k.copy(out, f32out, f32res, outres, E=k.pool)
        else:
            k.stt(out, src, gain, rs[0:gs, 0:n], ALU.mult, ALU.mult, list(srcres) + [rs] + list(gres), outres)
        return
    rot, Ct, St = rope
    k.stt(t1[0:gs, 0:n], src, gain, rs[0:gs, 0:n], ALU.mult, ALU.mult, list(srcres) + [rs] + list(gres), [t1])
    k.copy(t2[0:gs, 0:n], t1[0:gs, 0:n], [t1], [t2], E=k.act)
    k.mm(pr[0:gs, 0:n], rot[0:gs, 0:gs], t2[0:gs, 0:n], True, True, [t2], [pr])
    k.tt(t1[0:gs, 0:n], t1[0:gs, 0:n], Ct[0:gs, rope_off:rope_off + n], ALU.mult, [t1, Ct], [t1], E=k.pool)
    k.tt(rs[0:gs, 0:n], pr[0:gs, 0:n], St[0:gs, rope_off:rope_off + n], ALU.mult, [pr, St], [rs])
    k.tt(out, t1[0:gs, 0:n], rs[0:gs, 0:n], ALU.add, [t1, rs], outres, E=k.pool)


def attn_pipeline(g, W, streams, nk, nq):
    k = g.k
    nkt = nk // 128
    items = [(st, kt) for kt in range(nkt) for st in streams]
    NB = 3
    LA = 2

    def emit_sc(i):
        st, kt = items[i]
        sc = g.PS[i % NB]
        dk_ = st["dk"]
        kc = st.get("kt0", 0) + kt
        k.mm(sc[:, 0:nq], st["KT"][0:dk_, kc * 128:(kc + 1) * 128], st["QT"][0:dk_, st["q0"]:st["q0"] + nq], True, True,
             [st["KT"], st["QT"]], [sc])

    for i in range(min(LA, len(items))):
        emit_sc(i)
    for i in range(len(items)):
        if i + LA < len(items):
            emit_sc(i + LA)
        st, kt = items[i]
        sc = g.PS[i % NB]
        pt = W["pt"][i % NB]
        k.activation(pt[:, 0:nq], sc[:, 0:nq], AF.Exp, [sc], [pt], scale=st["scale"])
        h = st["hcol"]
        kc = st.get("kt0", 0) + kt
        k.mm(st["acc"][:, 0:nq], st["VA"][:, kc * 512 + h * 128: kc * 512 + (h + 1) * 128], pt[:, 0:nq], kt == 0, kt == nkt - 1,
             [st["VA"], pt], [st["acc"]])


def phase_mla(g, l):
    k, nc, D = g.k, g.nc, g.D
    scale = 96.0 ** -0.5
    with ExitStack() as es:
        g.ropeB = [k.sb(es, "ropeB%d" % i, [96, 2048], F32) for i in range(2)]
        for i in range(2):
            k.dma(k.sp, g.ropeB[i][:], D["ropeB"][i], (), [g.ropeB[i]])
        gn = k.sb(es, "gn", [128, 8], F32)
        k.dma(k.sp, gn[:], D["gains"][l], (), [gn])
        wuq = k.sb(es, "wuq", [128, 768], BF16)
        k.dma(k.pool, wuq[:], D["wuq"][l], (), [wuq])
        wkv = k.sb(es, "wkv", [128, 512], BF16)
        k.dma(k.pool, wkv[:], D["wukv"][l], (), [wkv])
        wkp = k.sb(es, "wkp", [128, 4 * 96], BF16)
        wv = k.sb(es, "wv", [128, 256], BF16)
        k.memset(wkp[:], 0.0, [wkp])
        for h in range(4):
            k.copy(wkp[:, h * 96:h * 96 + 64], wkv[:, h * 128:h * 128 + 64], [wkv], [wkp])
            k.copy(wv[:, h * 64:(h + 1) * 64], wkv[:, h * 128 + 64:(h + 1) * 128], [wkv], [wv])
        QT = [k.sb(es, "QT%d" % h, [96, 2048], BF16) for h in range(4)]
        KT = [k.sb(es, "KT%d" % h, [96, 2560], BF16) for h in range(4)]
        VA = k.sb(es, "VA", [128, 20 * 512], BF16)
        k.memset(VA[:], 1.0, [VA])
        W = {"sq": k.sb(es, "wsq", [128, 512], BF16), "rs": k.sb(es, "wrs", [128, 512], F32),
             "t1": k.sb(es, "wt1", [128, 512], F32), "t2": k.sb(es, "wt2", [128, 512], BF16),
             "pn": g.PS[3], "pr": g.PS[3], "pt": [k.sb(es, "wpt%d" % i, [128, 512], BF16) for i in range(3)]}
        cq = k.sb(es, "cq", [128, 1024], F32)
        cqn = k.sb(es, "cqn", [128, 1024], BF16)
        ckv = k.sb(es, "ckv", [128, 512], F32)
        ckn = k.sb(es, "ckn", [128, 512], F32)
        ckb = k.sb(es, "ckb", [128, 512], BF16)
        kr = k.sb(es, "kr", [32, 512], F32)
        krb = k.sb(es, "krb", [32, 512], BF16)
        rd = [k.sb(es, "rd%d" % i, [64, 512], F32) for i in range(2)]
        ob = [k.sb(es, "ob%d" % i, [64, 512], BF16) for i in range(4)]
        pq = g.PS[4]
        pv = g.PS[5]
        oi = 0

        def keys_block(ckb_, krb_, n, kcol, rope_off):
            for h in range(4):
                k.mm(pq[0:96, 0:n], wkp[:, h * 96:(h + 1) * 96], ckb_[:, 0:n], True, False, [wkp, ckb_], [pq])
                k.mm(pq[0:96, 0:n], g.esel[:, :], krb_[:, 0:n], False, True, [krb_], [pq])
                rope = None if rope_off is None else (g.rotB, g.ropeB[0], g.ropeB[1])
                group_norm_rope(g, W, pq[0:96, 0:n], [pq], 96, n, gn[0:96, 4:5], rope, KT[h][:, kcol:kcol + n], [KT[h]],
                                rope_off=rope_off or 0, gres=[gn])
            for tt in range(n // 128):
                k.mm(pv[:, 0:256], ckb_[:, tt * 128:(tt + 1) * 128], wv[:, :], True, True, [ckb_, wv], [pv])
                kt = kcol // 128 + tt
                k.copy(VA[:, kt * 512:(kt + 1) * 512].rearrange("p (h c) -> p h c", h=4)[:, :, 0:64],
                       pv[:, 0:256].rearrange("p (h c) -> p h c", h=4), [pv], [VA])

        for (t0, L, ctx, seqs) in GROUPS:
            koff = 512 if ctx else 0
            if ctx:
                k.dma(k.sp, ckn[:], D["cmlaT"][l, 0:128, :], (), [ckn])
                k.dma(k.sp, kr[:], D["cmlaT"][l, 128:160, :], (), [kr])
                k.copy(ckb[:], ckn[:], [ckn], [ckb])
                k.copy(krb[:], kr[:], [kr], [krb])
                keys_block(ckb, krb, 512, 0, None)
            for b0 in range(0, L, 512):
                n = min(512, L - b0)
                tg = t0 + b0
                for m in range(2):
                    ap_, rr = pj_rows(g, R_CQ + m * 128, 128, tg, n)
                    k.dma(k.sp, cq[:, m * 512:m * 512 + n], ap_, rr, [cq])
                ap_, rr = pj_rows(g, R_CKV, 128, tg, n)
                k.dma(k.sp, ckv[:, 0:n], ap_, rr, [ckv])
                ap_, rr = pj_rows(g, R_KR, 32, tg, n)
                k.dma(k.sp, kr[:, 0:n], ap_, rr, [kr])
                pn = W["pn"]
                for m in range(2):
                    k.activation(W["sq"][:, 0:n], cq[:, m * 512:m * 512 + n], AF.Square, [cq], [W["sq"]])
                    k.mm(pn[:, 0:n], g.ones[256][:, :], W["sq"][:, 0:n], m == 0, m == 1, [W["sq"]], [pn])
                k.activation(W["rs"][:, 0:n], pn[:, 0:n], AF.Sqrt, [pn], [W["rs"]], bias=g.eps_t[:, 0:1])
                k.op(k.dve, lambda: nc.vector.reciprocal(out=W["rs"][:, 0:n], in_=W["rs"][:, 0:n]), [W["rs"]], [W["rs"]])
                for m in range(2):
                    k.stt(cqn[:, m * 512:m * 512 + n], cq[:, m * 512:m * 512 + n], gn[:, m:m + 1], W["rs"][:, 0:n],
                          ALU.mult, ALU.mult, [cq, W["rs"], gn], [cqn])
                group_norm_rope(g, W, ckv[:, 0:n], [ckv], 128, n, gn[:, 2:3], None, ckb[:, 0:n], [ckb],
                                f32out=ckn[:, 0:n], f32res=[ckn], gres=[gn])
                k.copy(krb[:, 0:n], kr[:, 0:n], [kr], [krb])
                if not ctx:
                    k.dma(k.sp, D["mlaT"][l, 0:128, tg:tg + n], ckn[:, 0:n], [ckn], [])
                    k.dma(k.sp, D["mlaT"][l, 128:160, tg:tg + n], kr[:, 0:n], [kr], [])
                for h in range(4):
                    for m in range(2):
                        k.mm(pq[0:96, 0:n], wuq[:, m * 384 + h * 96: m * 384 + (h + 1) * 96], cqn[:, m * 512:m * 512 + n],
                             m == 0, m == 1, [wuq, cqn], [pq])
                    rope = (g.rotB, g.ropeB[0], g.ropeB[1]) if ctx else None
                    group_norm_rope(g, W, pq[0:96, 0:n], [pq], 96, n, gn[0:96, 3:4], rope, QT[h][:, b0:b0 + n], [QT[h]],
                                    rope_off=b0, gres=[gn])
                keys_block(ckb, krb, n, koff + b0, b0 if ctx else None)
            for (s0, Ls) in seqs:
                nk = Ls + koff
                kt0 = 0 if ctx else s0 // 128
                for q0 in range(0, Ls, 512):
                    nq = min(512, Ls - q0)
                    for hp in range(2):
                        base = 4 + 2 * (oi % 2)
                        oi += 1
                        sts = []
                        for j in range(2):
                            h = hp * 2 + j
                            sts.append(dict(KT=KT[h], QT=QT[h], dk=96, q0=s0 + q0, kt0=kt0, VA=VA, hcol=h, scale=scale,
                                            acc=g.PS[base + j]))
                        attn_pipeline(g, W, sts, nk, nq)
                        for j in range(2):
                            h = hp * 2 + j
                            acc = sts[j]["acc"]
                            o_ = ob[(oi * 2 + j) % 4]
                            k.op(k.dve, lambda: nc.vector.reciprocal(out=rd[j][:, 0:nq], in_=acc[64:128, 0:nq]), [acc], [rd[j]])
                            k.tt(o_[:, 0:nq], acc[0:64, 0:nq], rd[j][:, 0:nq], ALU.mult, [acc, rd[j]], [o_])
                            r0 = 256 + h * 64
                            tq = t0 + s0 + q0
                            k.dma(k.sp, D["O"][r0 // 128, r0 % 128:r0 % 128 + 64, tq:tq + nq], o_[:, 0:nq], [o_],
                                  [g.OR[r0 // 128]])
        k.barrier()


def phase_diff(g, l):
    k, nc, D = g.k, g.nc, g.D
    scale = 32.0 ** -0.5
    lam_init = 0.8 - 0.6 * float(np.exp(-0.3 * l))
    with ExitStack() as es:
        g.ropeD = [k.sb(es, "ropeD%d" % i, [128, 2048], F32) for i in range(2)]
        for i in range(2):
            k.dma(k.sp, g.ropeD[i][:], D["ropeD"][i], (), [g.ropeD[i]])
        gn = k.sb(es, "gn", [128, 8], F32)
        k.dma(k.sp, gn[:], D["gains"][l], (), [gn])
        dl = k.sb(es, "dl", [32, 4], F32)
        k.dma(k.sp, dl[:], D["dlam"][l], (), [dl])
        pr2 = k.sb(es, "pr2", [32, 2], F32)
        k.tt(pr2[:, 0:1], dl[:, 0:1], dl[:, 1:2], ALU.mult, [dl], [pr2])
        k.tt(pr2[:, 1:2], dl[:, 2:3], dl[:, 3:4], ALU.mult, [dl], [pr2])
        pl = g.PS[2]
        k.mm(pl[0:64, 0:2], g.onesf[:, :], pr2[:, :], True, True, [pr2, g.onesf], [pl])
        lam = k.sb(es, "lam", [64, 4], F32)
        k.activation(lam[:, 0:2], pl[0:64, 0:2], AF.Exp, [pl], [lam])
        k.stt(lam[:, 2:3], lam[:, 1:2], -lam_init, lam[:, 0:1], ALU.add, ALU.subtract, [lam], [lam])
        k.ts(lam[:, 3:4], gn[0:64, 7:8], 1.0 - lam_init, None, ALU.mult, reads=[gn], writes=[lam])
        QT = [k.sb(es, "dQ%d" % i, [128, 2048], BF16) for i in range(2)]
        KT = [k.sb(es, "dK%d" % i, [128, 2560], BF16) for i in range(2)]
        QP = [k.sb(es, "dQP%d" % i, [128, 2048], BF16) for i in range(8)]
        for i in range(8):
            k.memset(QP[i][:], 0.0, [QP[i]], E=(k.pool if i % 2 else k.dve))
        VA = k.sb(es, "dVA", [128, 20 * 512], BF16)
        k.memset(VA[:], 1.0, [VA])
        W = {"sq": k.sb(es, "wsq", [128, 512], BF16), "rs": k.sb(es, "wrs", [128, 512], F32),
             "t1": k.sb(es, "wt1", [128, 512], F32), "t2": k.sb(es, "wt2", [128, 512], BF16),
             "pn": g.PS[3], "pr": g.PS[3], "pt": [k.sb(es, "wpt%d" % i, [128, 512], BF16) for i in range(8)]}
        xin128 = [k.sb(es, "dxin128%d" % i, [128, 512], F32) for i in range(2)]
        kf = k.sb(es, "dkf", [128, 512], F32)
        vin = [k.sb(es, "dvin%d" % i, [128, 256], F32) for i in range(2)]
        rd = [k.sb(es, "rd%d" % i, [64, 512], F32) for i in range(4)]
        oo = [k.sb(es, "doo%d" % i, [64, 512], F32) for i in range(4)]
        ob = [k.sb(es, "dob%d" % i, [64, 512], BF16) for i in range(2)]
        oi = 0
        xi = 0
        pti = 0
        for (t0, L, ctx, seqs) in GROUPS:
            koff = 512 if ctx else 0
            rope = (g.rotD, g.ropeD[0], g.ropeD[1]) if ctx else None
            if ctx:
                for c2 in range(2):
                    x_ = xin128[xi % 2]
                    xi += 1
                    k.dma(k.sp, x_[:], D["cdkT"][l, c2 * 128:(c2 + 1) * 128, :], (), [x_])
                    k.copy(KT[c2][:, 0:512], x_[:], [x_], [KT[c2]])
                for tt in range(4):
                    v_ = vin[tt % 2]
                    k.dma(k.sp, v_[:], D["cdv"][l, tt * 128:(tt + 1) * 128, :], (), [v_])
                    k.copy(VA[:, tt * 512:(tt + 1) * 512].rearrange("p (h c) -> p h c", h=4)[:, :, 0:64],
                           v_[:].rearrange("p (h c) -> p h c", h=4), [v_], [VA])
            for b0 in range(0, L, 512):
                n = min(512, L - b0)
                tg = t0 + b0
                for c2 in range(2):
                    for isk in range(2):
                        x_ = xin128[xi % 2]
                        xi += 1
                        ap_, rr = pj_rows(g, (R_DK if isk else R_DQ) + c2 * 128, 128, tg, n)
                        k.dma(k.sp, x_[:, 0:n], ap_, rr, [x_])
                        dst = KT[c2] if isk else QT[c2]
                        off = (koff + b0) if isk else b0
                        if isk and not ctx:
                            group_norm_rope(g, W, x_[:, 0:n], [x_], 128, n, gn[:, 6:7], None, dst[:, off:off + n], [dst],
                                            f32out=kf[:, 0:n], f32res=[kf], onesT=g.bd32, gres=[gn])
                            k.dma(k.sp, D["dkT"][l, c2 * 128:(c2 + 1) * 128, tg:tg + n], kf[:, 0:n], [kf], [])
                        else:
                            group_norm_rope(g, W, x_[:, 0:n], [x_], 128, n, gn[:, 5 + isk:6 + isk], rope, dst[:, off:off + n], [dst],
                                            rope_off=b0, onesT=g.bd32, gres=[gn])
                        if not isk:
                            for j in range(4):
                                k.copy(QP[c2 * 4 + j][32 * j:32 * j + 32, b0:b0 + n], QT[c2][32 * j:32 * j + 32, b0:b0 + n],
                                       [QT[c2]], [QP[c2 * 4 + j]], E=(k.pool if j % 2 else k.act))
                for tt in range(n // 128):
                    v_ = vin[tt % 2]
                    k.dma(k.sp, v_[:], D["PT"][tg + tt * 128:tg + (tt + 1) * 128, 768:1024], [g.PTR], [v_])
                    kt = (koff + b0) // 128 + tt
                    k.copy(VA[:, kt * 512:(kt + 1) * 512].rearrange("p (h c) -> p h c", h=4)[:, :, 0:64],
                           v_[:].rearrange("p (h c) -> p h c", h=4), [v_], [VA])
            if not ctx:
                ap_, rr = pj_rows(g, R_DV, 256, t0, L)
                k.dma(k.sp, D["dvT"][l, :, t0:t0 + L], ap_, rr, [])
            for (s0, Ls) in seqs:
              nkt = (Ls + koff) // 128
              kt0 = 0 if ctx else s0 // 128
              for h in range(4):
                c2 = h // 2
                rb = 64 * (h % 2)
                for q0 in range(0, Ls, 512):
                    nq = min(512, Ls - q0)
                    qc = s0 + q0
                    base = 4 + 2 * (oi % 2)
                    accs = [g.PS[base], g.PS[base + 1]]

                    def emit_sc(kt):
                        kc = kt0 + kt
                        for m in range(2):
                            sc = g.PS[(kt % 2) * 2 + m]
                            qp = QP[h * 2 + m]
                            k.mm(sc[:, 0:nq], KT[c2][:, kc * 128:(kc + 1) * 128], qp[:, qc:qc + nq],
                                 True, True, [KT[c2], qp], [sc])

                    emit_sc(0)
                    for kt in range(nkt):
                        if kt + 1 < nkt:
                            emit_sc(kt + 1)
                        kc = kt0 + kt
                        for m in range(2):
                            sc = g.PS[(kt % 2) * 2 + m]
                            pt = W["pt"][pti % 8]
                            pti += 1
                            k.activation(pt[:, 0:nq], sc[:, 0:nq], AF.Exp, [sc], [pt], scale=scale)
                            k.mm(accs[m][:, 0:nq], VA[:, kc * 512 + h * 128: kc * 512 + (h + 1) * 128], pt[:, 0:nq],
                                 kt == 0, kt == nkt - 1, [VA, pt], [accs[m]])
                    for m in range(2):
                        k.op(k.dve, lambda: nc.vector.reciprocal(out=rd[m][:, 0:nq], in_=accs[m][64:128, 0:nq]), [accs[m]], [rd[m]])
                        k.tt(oo[m][:, 0:nq], accs[m][0:64, 0:nq], rd[m][:, 0:nq], ALU.mult, [accs[m], rd[m]], [oo[m]])
                    o0, o1 = oo[0], oo[1]
                    o_ = ob[oi % 2]
                    oi += 1
                    k.stt(o0[:, 0:nq], o1[:, 0:nq], lam[:, 2:3], o0[:, 0:nq], ALU.mult, ALU.add, [o0, o1, lam], [o0])
                    group_norm_rope(g, W, o0[:, 0:nq], [o0], 64, nq, lam[:, 3:4], None, o_[:, 0:nq], [o_], gres=[lam])
                    r0 = 768 + h * 64
                    tq = t0 + s0 + q0
                    k.dma(k.sp, D["O"][r0 // 128, r0 % 128:r0 % 128 + 64, tq:tq + nq], o_[:, 0:nq], [o_],
                          [g.OR[r0 // 128]])
        k.barrier()


def phase_zero(g, chunks):
    k, D = g.k, g.D
    with ExitStack() as es:
        z = k.sb(es, "z", [128, NT], BF16)
        k.memset(z[:], 0.0, [z])
        for kk in chunks:
            k.dma(k.sp, D["O"][kk], z[:], [z], [g.OR[kk]])
        k.barrier()


def phase_mixers(g, l, mixers):
    if "a" in mixers:
        phase_hgrn(g, l)
    else:
        phase_zero(g, [0, 1])
    if "b" in mixers:
        phase_mla(g, l)
    else:
        phase_zero(g, [2, 3])
    if "c" in mixers:
        phase_hyena(g, l)
    else:
        phase_zero(g, [4, 5])
    if "d" in mixers:
        phase_diff(g, l)
    else:
        phase_zero(g, [6, 7])


def hgrn_setup(g, es):
    k, nc, D = g.k, g.nc, g.D
    lbl = k.sb(es, "lbl", [64, 32], F32)
    k.dma(k.sp, lbl[:], D["lbl"], (), [lbl])
    e = k.sb(es, "lbe", [64, 32], F32)
    k.activation(e[:], lbl[:], AF.Exp, [lbl], [e])
    sm = k.sb(es, "lbs", [64, 8], F32)
    e3 = e[:].rearrange("p (a l) -> p a l", l=4)
    k.op(k.dve, lambda: nc.vector.tensor_reduce(out=sm[:], in_=e3, axis=AX.X, op=ALU.add), [e], [sm])
    k.op(k.dve, lambda: nc.vector.reciprocal(out=sm[:], in_=sm[:]), [sm], [sm])
    k.tt(e3, e3, sm[:].unsqueeze(2).to_broadcast([64, 8, 4]), ALU.mult, [e, sm], [e])
    g.oml = k.sb(es, "oml", [64, 32], F32)
    o3 = g.oml[:].rearrange("p (a l) -> p a l", l=4)
    k.memset(g.oml[:], 1.0, [g.oml])
    for l in range(1, 4):
        k.tt(o3[:, :, l], o3[:, :, l - 1], e3[:, :, l], ALU.subtract, [g.oml, e], [g.oml])
    g.mask4 = [k.sb(es, "mask4%d" % i, [128, 256], BF16) for i in range(2)]
    g.maskbd = [k.sb(es, "maskbd%d" % i, [128, 128], F32) for i in range(2)]
    for i in range(2):
        k.dma(k.pool, g.mask4[i][:], D["mask4"][i], (), [g.mask4[i]])
        k.dma(k.sp, g.maskbd[i][:], D["maskbd"][i], (), [g.maskbd[i]])
    g.ident = k.sb(es, "ident", [64, 64], BF16)
    k.dma(k.pool, g.ident[:], D["ident"], (), [g.ident])
    g.rmask = k.sb(es, "rmask", [128, 2048], F32)
    k.memset(g.rmask[:], 1.0, [g.rmask])
    k.memset(g.rmask[:].rearrange("p (a b) -> p a b", b=32)[:, :, 0:1], 0.0, [g.rmask])
    g.one_t = k.sb(es, "one_t", [128, 1], F32)
    k.memset(g.one_t[:], 1.0, [g.one_t])


def phase_hgrn(g, l):
    k, nc, D = g.k, g.nc, g.D
    for (t0, nseq, Ls, ctx) in ((0, 4, 256, False), (1024, 1, 2048, True)):
        L = nseq * Ls
        nch = L // 32
        nchs = Ls // 32
        ntl = L // 128
        nsl = nseq * (nchs + 1)

        def slot_of(i):
            return (i // nchs) * (nchs + 1) + 1 + (i % nchs)

        with ExitStack() as es:
            f32t = lambda nm: k.sb(es, nm, [128, L], F32)
            af, og, qs, kk, cum, e1, of = [f32t(n_) for n_ in ("af", "og", "qs", "kk", "cum", "e1", "of")]
            aq = e1
            gl = af
            qt, kt, k2 = [k.sb(es, n_, [128, L], BF16) for n_ in ("qt", "kt", "k2")]
            el = k.sb(es, "el", [128, nch], F32)
            k2T = k.sb(es, "k2T", [128, ntl * 128], BF16)
            vin = [k.sb(es, "hvin%d" % i, [128, 256], F32) for i in range(2)]
            vt = k.sb(es, "vt", [128, ntl * 256], BF16)
            V4 = [k.sb(es, "V4%d" % i, [128, 256], BF16) for i in range(4)]
            U3, A3, S3 = [k.sb(es, n_, [128, 64 * nsl], F32) for n_ in ("U3", "A3", "S3")]
            S16 = k.sb(es, "S16", [128, 64 * nsl], BF16)
            AT = [k.sb(es, "AT%d" % i, [128, 128], BF16) for i in range(4)]
            s0t = k.sb(es, "s0t", [128, 64], F32)
            fin = [k.sb(es, "fin%d" % i, [128, 64], F32) for i in range(2)]
            gon = k.sb(es, "gon", [128, 1], F32)
            k.dma(k.sp, gon[0:64, :], D["gon"][l], (), [gon])
            k.dma(k.sp, gon[64:128, :], D["gon"][l], (), [gon])
            omlp = k.sb(es, "omlp", [128, 4], F32)
            for hp in range(2):
                for d in range(2):
                    for j in range(2):
                        col = (d * 4 + hp * 2 + j) * 4 + l
                        k.copy(omlp[64 * j:64 * j + 64, hp * 2 + d:hp * 2 + d + 1], g.oml[:, col:col + 1], [g.oml], [omlp])
            obf = [k.sb(es, "hob%d" % i, [128, 512], BF16) for i in range(2)]
            W = {"sq": k.sb(es, "wsq", [128, 512], BF16), "rs": k.sb(es, "wrs", [128, 512], F32),
                 "t1": k.sb(es, "wt1", [128, 512], F32), "t2": k.sb(es, "wt2", [128, 512], BF16),
                 "pn": g.PS[2], "pr": g.PS[3]}
            v3 = lambda t_: t_[:].rearrange("p (d s) -> p d s", s=nsl)
            c3 = lambda t_: t_[:].rearrange("p (c j) -> p c j", j=32)
            cnt = 0
            for tt in range(ntl):
                v_ = vin[tt % 2]
                k.dma(k.sp, v_[:], D["PT"][t0 + tt * 128:t0 + (tt + 1) * 128, 0:256], [g.PTR], [v_])
                k.copy(vt[:, tt * 256:(tt + 1) * 256], v_[:], [v_], [vt], E=(k.act if tt % 2 else k.dve))
            for hp in range(2):
                ap_, rr = pj_rows(g, R_AQ + hp * 128, 128, t0, L)
                k.dma(k.sp, aq[:], ap_, rr, [aq])
                ap_, rr = pj_rows(g, R_AOG + hp * 128, 128, t0, L)
                k.dma(k.sp, og[:], ap_, rr, [og])
                k.activation(qs[:], aq[:], AF.Silu, [aq], [qs])
                for d in range(2):
                    ap_, rr = pj_rows(g, (R_AFF if d == 0 else R_AFB) + hp * 128, 128, t0, L)
                    k.dma(k.sp, af[:], ap_, rr, [af])
                    k.activation(kk[:], af[:], AF.Sigmoid, [af], [kk], scale=-1.0)
                    k.ts(kk[:], kk[:], omlp[:, hp * 2 + d:hp * 2 + d + 1], MAXKEY, ALU.mult, ALU.min, reads=[kk, omlp], writes=[kk])
                    k.activation(gl[:], kk[:], AF.Ln, [kk], [gl], scale=-1.0, bias=g.one_t[:, 0:1])
                    if d == 0:
                        k.op(k.dve, lambda: nc.vector.tensor_tensor_scan(out=cum[:], data0=g.rmask[:, 0:L], data1=gl[:],
                             initial=0.0, op0=ALU.mult, op1=ALU.add), [gl, g.rmask], [cum])
                        totv = c3(cum)[:, :, 31]
                    else:
                        k.op(k.dve, lambda: nc.vector.tensor_tensor_scan(out=cum[:, ::-1], data0=g.rmask[:, 0:L],
                             data1=gl[:, ::-1], initial=0.0, op0=ALU.mult, op1=ALU.add), [gl, g.rmask], [cum])
                        totv = c3(cum)[:, :, 0]
                    k.activation(e1[:], cum[:], AF.Exp, [cum], [e1])
                    k.stt(qt[:], e1[:], 0.125, qs[:], ALU.mult, ALU.mult, [e1, qs], [qt])
                    k.activation(e1[:], cum[:], AF.Exp, [cum], [e1], scale=-1.0)
                    k.tt(kt[:], kk[:], e1[:], ALU.mult, [kk, e1], [kt], E=k.pool)
                    k.activation(el[:], totv, AF.Exp, [cum], [el])
                    k.tt(c3(e1), totv.unsqueeze(2).to_broadcast([128, nch, 32]), c3(cum), ALU.subtract, [cum], [e1])
                    k.activation(e1[:], e1[:], AF.Exp, [e1], [e1])
                    k.tt(k2[:], kk[:], e1[:], ALU.mult, [kk, e1], [k2], E=k.pool)
                    for tt in range(ntl):
                        pT = g.PS[tt % 2]
                        k.mm(pT[:, 0:128], k2[:, tt * 128:(tt + 1) * 128], g.ident128[:, :], True, True, [k2, g.ident128], [pT])
                        k.copy(k2T[:, tt * 128:(tt + 1) * 128], pT[:, 0:128], [pT], [k2T], E=(k.act if tt % 2 else k.dve))
                    init_sl = slice(0, nsl, nchs + 1)
                    if ctx:
                        for j in range(2):
                            k.dma(k.sp, s0t[64 * j:64 * j + 64, :], D["s0"][l, d, hp * 2 + j], (), [s0t])
                        k.copy(v3(U3)[:, :, 0], s0t[:], [s0t], [U3])
                    else:
                        k.memset(v3(U3)[:, :, init_sl], 0.0, [U3])
                    k.memset(v3(A3)[:, :, init_sl], 0.0, [A3])
                    elv = el[:] if d == 0 else el[:, ::-1]
                    for sp_ in range(nseq):
                        s_lo = sp_ * (nchs + 1) + 1
                        k.copy(v3(A3)[:, :, s_lo:s_lo + nchs],
                               elv[:, sp_ * nchs:(sp_ + 1) * nchs].unsqueeze(1).to_broadcast([128, 64, nchs]), [el], [A3],
                               E=k.pool)
                    for tt in range(ntl):
                        i0 = 4 * tt if d == 0 else nch - 4 - 4 * tt
                        s_lo = slot_of(i0)
                        for j in range(2):
                            h = hp * 2 + j
                            v4 = V4[(tt * 2 + j) % 4]
                            k.tt(v4[:].rearrange("p (j c) -> p j c", j=4),
                                 vt[:, tt * 256 + h * 64: tt * 256 + (h + 1) * 64].unsqueeze(1).to_broadcast([128, 4, 64]),
                                 g.mask4[d][:].rearrange("p (j c) -> p j c", j=4), ALU.mult, [vt, g.mask4[d]], [v4],
                                 E=k.pool)
                            pu = g.PS[4 + j]
                            k.mm(pu[:, 0:256], k2T[:, tt * 128:(tt + 1) * 128], v4[:], True, True, [k2T, v4], [pu])
                            k.copy(v3(U3)[64 * j:64 * j + 64, :, s_lo:s_lo + 4].rearrange("p d s -> p s d"),
                                   pu[64 * j:64 * j + 64, 0:256].rearrange("p (s d) -> p s d", s=4), [pu], [U3],
                                   E=(k.act if j else k.dve))
                    k.op(k.dve, lambda: nc.vector.tensor_tensor_scan(out=S3[:], data0=A3[:], data1=U3[:], initial=0.0,
                         op0=ALU.mult, op1=ALU.add), [A3, U3], [S3])
                    k.copy(S16[:], S3[:], [S3], [S16], E=k.act)
                    if not ctx:
                        for sp_ in range(nseq):
                            si = sp_ if d == 0 else nseq - 1 - sp_
                            f_ = fin[cnt % 2]
                            cnt += 1
                            k.copy(f_[:], v3(S3)[:, :, sp_ * (nchs + 1) + nchs], [S3], [f_])
                            for j in range(2):
                                k.dma(k.sp, D["hg"][l, si, d, hp * 2 + j], f_[64 * j:64 * j + 64, :], [f_], [])
                    for tt in range(ntl):
                        for j in range(2):
                            h = hp * 2 + j
                            hb = 64 * j
                            pa = g.PS[j]
                            at = AT[(tt * 2 + j) % 4]
                            k.mm(pa[:, 0:128], kt[hb:hb + 64, tt * 128:(tt + 1) * 128], qt[hb:hb + 64, tt * 128:(tt + 1) * 128],
                                 True, True, [kt, qt], [pa])
                            k.tt(at[:], pa[:, 0:128], g.maskbd[d][:], ALU.mult, [pa, g.maskbd[d]], [at])
                        for j in range(2):
                            h = hp * 2 + j
                            hb = 64 * j
                            at = AT[(tt * 2 + j) % 4]
                            po = g.PS[6 + j]
                            cb = (tt % 4) * 128
                            k.mm(po[0:64, cb:cb + 128], vt[:, tt * 256 + h * 64: tt * 256 + (h + 1) * 64], at[:], True, False,
                                 [vt, at], [po])
                            for jj in range(4):
                                c = 4 * tt + jj
                                i = c if d == 0 else nch - 1 - c
                                k.mm(po[0:64, cb + 32 * jj:cb + 32 * jj + 32], v3(S16)[hb:hb + 64, :, slot_of(i) - 1],
                                     qt[hb:hb + 64, c * 32:(c + 1) * 32], False, jj == 3, [S16, qt], [po])
                            if tt % 4 == 3 or tt == ntl - 1:
                                b0 = (tt // 4) * 512
                                n = cb + 128
                                if d == 0:
                                    k.copy(of[hb:hb + 64, b0:b0 + n], po[0:64, 0:n], [po], [of], E=k.act)
                                else:
                                    k.tt(of[hb:hb + 64, b0:b0 + n], of[hb:hb + 64, b0:b0 + n], po[0:64, 0:n], ALU.add, [of, po], [of])
                k.activation(og[:], og[:], AF.Silu, [og], [og])
                for b0 in range(0, L, 512):
                    n = min(512, L - b0)
                    o_ = obf[(b0 // 512) % 2]
                    group_norm_rope(g, W, of[:, b0:b0 + n], [of], 128, n, gon[:, 0:1], None, W["t2"][:, 0:n], [W["t2"]],
                                    f32out=e1[:, b0:b0 + n], f32res=[e1], onesT=g.bd64, gres=[gon])
                    k.tt(o_[:, 0:n], e1[:, b0:b0 + n], og[:, b0:b0 + n], ALU.mult, [e1, og], [o_])
                    k.dma(k.sp, D["O"][hp, :, t0 + b0:t0 + b0 + n], o_[:, 0:n], [o_], [g.OR[hp]])
            k.barrier()


HYN = ((256, "P", SEQS[0:4]), (2048, "S", SEQS[4:5]))
TWO_PI = 6.283185307179586


def hyena_input_shapes():
    s = {"hw1": [DEPTH, 17, 64], "hb1": [DEPTH, 64, 1], "hw2": [DEPTH, 64, 64], "hb2": [DEPTH, 64, 1],
         "hw3": [DEPTH, 64, 1024], "hld": [DEPTH, 128, 1024], "hsT": [DEPTH, 128, 18], "hbT": [DEPTH, 128, 4],
         "hmsk": [128, 4], "ident128": [128, 128]}
    for n, sfx, _ in HYN:
        tbw = min(512, n)
        s["feats" + sfx] = [17, n]
        s["negtn" + sfx] = [128, n // 128]
        s["Fr" + sfx] = [2 * n // 128, 128, n]
        s["Gr" + sfx] = [n // tbw, 128, (2 * n // 128) * tbw]
        s["Fq" + sfx] = [2, 128, n]
    return s


def hyena_scratch():
    return [("HS" + sfx, [n // 128, 3, 128, 512], F32) for n, sfx, _ in HYN]


def hyena_host_shared(inp):
    f = np.float32
    S = {}
    S["hw1"] = np.ascontiguousarray(inp["hy_w1"], dtype=f)
    S["hb1"] = np.ascontiguousarray(inp["hy_b1"].reshape(DEPTH, 64, 1), dtype=f)
    S["hw2"] = np.ascontiguousarray(inp["hy_w2"], dtype=f)
    S["hb2"] = np.ascontiguousarray(inp["hy_b2"].reshape(DEPTH, 64, 1), dtype=f)
    S["hw3"] = np.ascontiguousarray(inp["hy_w3"], dtype=f)
    S["hld"] = np.ascontiguousarray(np.broadcast_to(inp["hy_log_decay"].reshape(DEPTH, 1, 1024), (DEPTH, 128, 1024)), dtype=f)
    hs = inp["hy_short"].reshape(DEPTH, 3, 6, 128)
    S["hsT"] = np.ascontiguousarray(hs.transpose(0, 3, 2, 1).reshape(DEPTH, 128, 18), dtype=f)
    hb = inp["hy_bias"].reshape(DEPTH, 2, 2, 128)
    S["hbT"] = np.ascontiguousarray(hb.transpose(0, 3, 1, 2).reshape(DEPTH, 128, 4), dtype=f)
    msk = np.ones((128, 4), f)
    msk[0, 1] = 0.0
    msk[:, 2] = 0.0
    msk[0, 2] = 1.0
    msk[:, 3] = -1.0
    S["hmsk"] = msk
    S["ident128"] = np.eye(128, dtype=f)
    for n, sfx, _ in HYN:
        tn = (np.arange(n, dtype=f) / f(n)).astype(f)
        ang = (f(2.0 * np.pi) * tn[:, None] * np.arange(1, 9, dtype=f)).astype(f)
        feats = np.concatenate([tn[:, None], np.cos(ang), np.sin(ang)], axis=-1).astype(f)
        S["feats" + sfx] = np.ascontiguousarray(feats.T)
        S["negtn" + sfx] = np.ascontiguousarray((-tn).reshape(n // 128, 128).T)
        t = np.arange(n, dtype=np.float64)[:, None]
        fr = np.arange(n, dtype=np.float64)[None, :]
        th = np.pi * t * fr / n
        Fm = np.concatenate([np.cos(th), -np.sin(th)], axis=1)
        Fm[:, n] = (-1.0) ** np.arange(n)
        Fr = Fm.reshape(n // 128, 128, 2 * n // 128, 128).transpose(2, 1, 0, 3).reshape(2 * n // 128, 128, n)
        S["Fr" + sfx] = np.ascontiguousarray(Fr.astype(f).astype(ml_dtypes.bfloat16))
        fq = np.zeros((2, 128, n // 128, 128), np.float64)
        fq[0] = Fr[n // 128].reshape(128, n // 128, 128)
        fq[1, :, :, 0] = fq[0, :, :, 0]
        fq[0, :, :, 0] = 0.0
        S["Fq" + sfx] = np.ascontiguousarray(fq.reshape(2, 128, n).astype(f).astype(ml_dtypes.bfloat16))
        Gm = np.concatenate([np.cos(th.T), -np.sin(th.T)], axis=0) * (2.0 / (2 * n))
        Gm[0, :] = 1.0 / (2 * n)
        Gm[n, :] = ((-1.0) ** np.arange(n)) / (2 * n)
        tbw = min(512, n)
        Gr = Gm.reshape(2 * n // 128, 128, n // tbw, tbw).transpose(2, 1, 0, 3).reshape(n // tbw, 128, (2 * n // 128) * tbw)
        S["Gr" + sfx] = np.ascontiguousarray(Gr.astype(f).astype(ml_dtypes.bfloat16))
    return S


def hyena_setup(g, es):
    k, D = g.k, g.D
    g.hmsk = k.sb(es, "hmsk", [128, 4], F32)
    k.dma(k.sp, g.hmsk[:], D["hmsk"], (), [g.hmsk])
    g.ident128 = k.sb(es, "ident128", [128, 128], BF16)
    k.dma(k.pool, g.ident128[:], D["ident128"], (), [g.ident128])
    g.onesF = k.sb(es, "onesF", [128, 128], F32)
    k.memset(g.onesF[:], 1.0, [g.onesF])
    g.HSR = {sfx: [Res() for _ in range(n // 128)] for n, sfx, _ in HYN}


def hyena_filters(g, l, n, sfx):
    k, nc, D = g.k, g.nc, g.D
    ntl = n // 128
    npair = n // 128
    nfc = 2 * npair
    with ExitStack() as es:
        w1 = k.sb(es, "hw1", [17, 64], F32)
        b1 = k.sb(es, "hb1", [64, 1], F32)
        w2 = k.sb(es, "hw2", [64, 64], F32)
        b2 = k.sb(es, "hb2", [64, 1], F32)
        w3 = k.sb(es, "hw3", [64, 1024], F32)
        eld = k.sb(es, "eld", [128, 1024], F32)
        ft = k.sb(es, "feat", [17, n], F32)
        ntn = k.sb(es, "ntn", [128, ntl], F32)
        for t_, nm in ((w1, "hw1"), (b1, "hb1"), (w2, "hw2"), (b2, "hb2"), (w3, "hw3"), (eld, "hld")):
            k.dma(k.sp, t_[:], D[nm][l], (), [t_])
        k.dma(k.sp, ft[:], D["feats" + sfx], (), [ft])
        k.dma(k.sp, ntn[:], D["negtn" + sfx], (), [ntn])
        k.activation(eld[:], eld[:], AF.Exp, [eld], [eld])
        h1 = k.sb(es, "h1", [64, n], F32)
        h2 = k.sb(es, "h2", [64, n], F32)
        y = k.sb(es, "hy", [64, 512], F32)
        kq = k.sb(es, "hkq", [64, 512], F32)
        ti = k.sb(es, "hti", [64, 512], I32)
        FBS = k.sb(es, "FBS", [128, ntl * 512], BF16)
        FBD = k.sb(es, "FBD", [128, ntl * 512], BF16)
        dec = k.sb(es, "dec", [128, 512], F32)
        fls = [k.sb(es, "fl%d" % i, [128, 512], F32) for i in range(2)]
        ab = k.sb(es, "ab", [128, 512], F32)
        rn = k.sb(es, "rn", [128, 512], F32)
        ReH = k.sb(es, "ReH", [128, npair * 512], F32)
        Fi = [k.sb(es, "Fi%d" % i, [128, n], BF16) for i in range(3)]
        Bt = [k.sb(es, "Bt%d" % i, [128, 512], F32) for i in range(2)]
        Ct = k.sb(es, "Ct", [128, 512], F32)

        def sin_layer(w, K_, rhs, bias, out):
            for b0 in range(0, n, 512):
                nb = min(512, n - b0)
                ps = g.PS[(b0 // 512) % 2]
                k.mm(ps[0:64, 0:nb], w[0:K_, :], rhs[0:K_, b0:b0 + nb], True, True, [w, rhs], [ps])
                k.ts(y[:, 0:nb], ps[0:64, 0:nb], bias[:, 0:1], None, ALU.add, reads=[ps, bias], writes=[y])
                k.ts(kq[:, 0:nb], y[:, 0:nb], 1.0 / TWO_PI, None, ALU.mult, reads=[y], writes=[kq])
                k.copy(ti[:, 0:nb], kq[:, 0:nb], [kq], [ti])
                k.copy(kq[:, 0:nb], ti[:, 0:nb], [ti], [kq])
                k.stt(y[:, 0:nb], kq[:, 0:nb], -TWO_PI, y[:, 0:nb], ALU.mult, ALU.add, [kq, y], [y])
                k.ts(y[:, 0:nb], y[:, 0:nb], -3.141592, 3.141592, ALU.max, ALU.min, reads=[y], writes=[y])
                k.activation(out[:, b0:b0 + nb], y[:, 0:nb], AF.Sin, [y], [out])

        sin_layer(w1, 17, ft, b1, h1)
        sin_layer(w2, 64, h1, b2, h2)
        accs = [g.PS[4], g.PS[5]]
        for tt in range(ntl):
            for hf in range(2):
                ps = g.PS[2 + hf]
                fl = fls[hf]
                k.mm(ps[:, :], h2[:, tt * 128:(tt + 1) * 128], w3[:, hf * 512:(hf + 1) * 512], True, True, [h2, w3], [ps])
                k.activation(dec[:], eld[:, hf * 512:(hf + 1) * 512], AF.Exp, [eld, ntn], [dec], scale=ntn[:, tt:tt + 1])
                k.tt(fl[:], ps[:, :], dec[:], ALU.mult, [ps, dec], [fl])
                if hf == 1 and tt == 0:
                    k.ts(fl[:], fl[:], g.hmsk[:, 1:2], None, ALU.mult, reads=[fl, g.hmsk], writes=[fl])
                k.activation(ab[:], fl[:], AF.Abs, [fl], [ab])
                k.mm(accs[hf][:, :], g.onesF[:, :], ab[:], tt == 0, tt == ntl - 1, [ab, g.onesF], [accs[hf]])
            k.tt(FBS[:, tt * 512:(tt + 1) * 512], fls[0][:], fls[1][:], ALU.add, [fls[0], fls[1]], [FBS], E=k.pool)
            k.tt(FBD[:, tt * 512:(tt + 1) * 512], fls[0][:], fls[1][:], ALU.subtract, [fls[0], fls[1]], [FBD])
        k.copy(rn[:], accs[0][:, :], [accs[0]], [rn], E=k.act)
        k.tt(rn[:], rn[:], accs[1][:, :], ALU.add, [rn, accs[1]], [rn])
        k.ts(rn[:], rn[:], EPS, None, ALU.add, reads=[rn], writes=[rn])
        k.op(k.dve, lambda: nc.vector.reciprocal(out=rn[:], in_=rn[:]), [rn], [rn])
        Fq = [k.sb(es, "Fq%d" % i, [128, n], BF16) for i in range(2)]
        for i in range(2):
            k.dma(k.sp, Fq[i][:], D["Fq" + sfx][i], (), [Fq[i]])
        for i in range(nfc):
            ip = i % npair
            isim = i >= npair
            ps = g.PS[i % 2]
            if i == npair:
                for kk in range(ntl):
                    k.mm(ps[:, :], Fq[0][:, kk * 128:(kk + 1) * 128], FBD[:, kk * 512:(kk + 1) * 512], kk == 0, False,
                         [Fq[0], FBD], [ps])
                for kk in range(ntl):
                    k.mm(ps[:, :], Fq[1][:, kk * 128:(kk + 1) * 128], FBS[:, kk * 512:(kk + 1) * 512], False, kk == ntl - 1,
                         [Fq[1], FBS], [ps])
            else:
                F_ = Fi[i % 3]
                k.dma(k.sp, F_[:], D["Fr" + sfx][i], (), [F_])
                FB_ = FBD if isim else FBS
                for kk in range(ntl):
                    k.mm(ps[:, :], F_[:, kk * 128:(kk + 1) * 128], FB_[:, kk * 512:(kk + 1) * 512], kk == 0, kk == ntl - 1,
                         [F_, FB_], [ps])
            Re_ = ReH[:, ip * 512:(ip + 1) * 512]
            if not isim:
                k.tt(Re_, ps[:, :], rn[:], ALU.mult, [ps, rn], [ReH])
            else:
                B_ = Bt[ip % 2]
                k.tt(B_[:], ps[:, :], rn[:], ALU.mult, [ps, rn], [B_])
                hs_ = D["HS" + sfx][ip]
                hr = g.HSR[sfx][ip]
                if ip > 0:
                    k.dma(k.pool, hs_[0], Re_, [ReH], [hr])
                    k.dma(k.pool, hs_[1], B_[:], [B_], [hr])
                    k.dma(k.pool, hs_[2], Re_, [ReH], [hr])
                else:
                    k.ts(Ct[:], Re_, g.hmsk[:, 1:2], None, ALU.mult, reads=[ReH, g.hmsk], writes=[Ct])
                    k.stt(Ct[:], B_[:], g.hmsk[:, 2:3], Ct[:], ALU.mult, ALU.add, [B_, Ct, g.hmsk], [Ct])
                    k.ts(B_[:], B_[:], g.hmsk[:, 1:2], None, ALU.mult, reads=[B_, g.hmsk], writes=[B_])
                    k.dma(k.pool, hs_[0], Re_, [ReH], [hr])
                    k.dma(k.pool, hs_[1], B_[:], [B_], [hr])
                    k.dma(k.pool, hs_[2], Ct[:], [Ct], [hr])
        k.barrier()


def hyena_convs(g, l, n, sfx, seqs):
    k, nc, D = g.k, g.nc, g.D
    ntl = n // 128
    npair = n // 128
    nfc = 2 * npair
    tbw = min(512, n)
    ntb = n // tbw
    with ExitStack() as es:
        hs = k.sb(es, "hsT", [128, 18], F32)
        hb = k.sb(es, "hbT", [128, 4], F32)
        k.dma(k.sp, hs[:], D["hsT"][l], (), [hs])
        k.dma(k.sp, hb[:], D["hbT"][l], (), [hb])
        U = k.sb(es, "hU", [128, 2 * n], F32)
        X = k.sb(es, "hX", [128, 2 * n], F32)
        xin = k.sb(es, "hxin", [128, n], F32)
        ubf = k.sb(es, "hubf", [128, 2 * n], BF16)
        utm = k.sb(es, "hutm", [128, ntl * 256], BF16)
        Yt = k.sb(es, "hYt", [128, nfc * 256], BF16)
        Gts = [k.sb(es, "hGt%d" % i, [128, nfc * tbw], BF16) for i in range(2)]
        Fi = [k.sb(es, "hFi%d" % i, [128, n], BF16) for i in range(4)]
        Hs = [k.sb(es, "hHs%d" % i, [128, 768], F32) for i in range(2)]
        vre = k.sb(es, "hvre", [128, 256], F32)
        vim = k.sb(es, "hvim", [128, 256], F32)
        t1 = k.sb(es, "ht1", [128, 256], F32)
        t2 = k.sb(es, "ht2", [128, 256], F32)
        ob = [k.sb(es, "hob%d" % i, [128, 512], BF16) for i in range(2)]

        def short_conv(dst, ch0, t0):
            for cc in range(2):
                ap_, rr = pj_rows(g, R_CV + (ch0 + cc) * 128, 128, t0, n)
                k.dma(k.sp, xin[:], ap_, rr, [xin])
                c3 = (ch0 + cc) * 3
                d_ = dst[:, cc * n:(cc + 1) * n]
                k.ts(d_, xin[:], hs[:, c3 + 1:c3 + 2], None, ALU.mult, reads=[xin, hs], writes=[dst])
                k.stt(d_[:, 1:n], xin[:, 0:n - 1], hs[:, c3:c3 + 1], d_[:, 1:n], ALU.mult, ALU.add, [xin, hs, dst], [dst])
                k.stt(d_[:, 0:n - 1], xin[:, 1:n], hs[:, c3 + 2:c3 + 3], d_[:, 0:n - 1], ALU.mult, ALU.add, [xin, hs, dst], [dst])

        def long_conv(o, t0, last):
            k.copy(ubf[:], U[:], [U], [ubf], E=k.act)
            for tt in range(ntl):
                for cc in range(2):
                    pT = g.PS[(tt * 2 + cc) % 2]
                    k.mm(pT[:, 0:128], ubf[:, cc * n + tt * 128: cc * n + (tt + 1) * 128], g.ident128[:, :], True, True,
                         [ubf, g.ident128], [pT])
                    k.copy(utm[:, tt * 256 + cc * 128: tt * 256 + (cc + 1) * 128], pT[:, 0:128], [pT], [utm],
                           E=(k.act if cc else k.dve))
            for ip in range(npair):
                Fre = Fi[(ip % 2) * 2]
                Fim = Fi[(ip % 2) * 2 + 1]
                k.dma(k.sp, Fre[:], D["Fr" + sfx][ip], (), [Fre])
                k.dma(k.sp, Fim[:], D["Fr" + sfx][npair + ip], (), [Fim])
                H_ = Hs[ip % 2]
                k.dma(k.sp, H_[:].rearrange("p (a c) -> p a c", a=3),
                      D["HS" + sfx][ip].rearrange("a p c -> p a c")[:, :, o * 256:(o + 1) * 256], [g.HSR[sfx][ip]], [H_])
                pre = g.PS[2 + (ip % 2) * 2]
                pim = g.PS[3 + (ip % 2) * 2]
                for kk in range(ntl):
                    k.mm(pre[:, 0:256], Fre[:, kk * 128:(kk + 1) * 128], utm[:, kk * 256:(kk + 1) * 256], kk == 0, kk == ntl - 1,
                         [Fre, utm], [pre])
                for kk in range(ntl):
                    k.mm(pim[:, 0:256], Fim[:, kk * 128:(kk + 1) * 128], utm[:, kk * 256:(kk + 1) * 256], kk == 0, kk == ntl - 1,
                         [Fim, utm], [pim])
                k.copy(vre[:], pre[:, 0:256], [pre], [vre], E=k.act)
                k.copy(vim[:], pim[:, 0:256], [pim], [vim], E=k.act)
                A_, B_, C_ = H_[:, 0:256], H_[:, 256:512], H_[:, 512:768]
                k.tt(t1[:], vre[:], A_, ALU.mult, [vre, H_], [t1])
                k.tt(t2[:], vim[:], B_, ALU.mult, [vim, H_], [t2], E=k.pool)
                k.tt(Yt[:, ip * 256:(ip + 1) * 256], t1[:], t2[:], ALU.subtract, [t1, t2], [Yt])
                k.tt(t1[:], vre[:], B_, ALU.mult, [vre, H_], [t1])
                k.tt(t2[:], vim[:], C_, ALU.mult, [vim, H_], [t2], E=k.pool)
                k.tt(Yt[:, (npair + ip) * 256:(npair + ip + 1) * 256], t1[:], t2[:], ALU.add, [t1, t2], [Yt])
            oi = 0
            for tb in range(ntb):
                Gt = Gts[tb % 2]
                for q in range(4):
                    w_ = nfc * tbw // 4
                    k.dma(k.sp, Gt[:, q * w_:(q + 1) * w_], D["Gr" + sfx][tb][:, q * w_:(q + 1) * w_], (), [Gt])
                for cc in range(2):
                    py = g.PS[6 + cc]
                    for fc in range(nfc):
                        k.mm(py[:, 0:tbw], Yt[:, fc * 256 + cc * 128: fc * 256 + (cc + 1) * 128], Gt[:, fc * tbw:(fc + 1) * tbw],
                             fc == 0, fc == nfc - 1, [Yt, Gt], [py])
                    sl = slice(cc * n + tb * tbw, cc * n + (tb + 1) * tbw)
                    k.stt(U[:, sl], U[:, sl], hb[:, o * 2 + cc:o * 2 + cc + 1], py[:, 0:tbw], ALU.mult, ALU.add, [U, hb, py], [U])
                    if not last:
                        k.tt(U[:, sl], U[:, sl], X[:, sl], ALU.mult, [U, X], [U])
                    else:
                        o_ = ob[oi % 2]
                        oi += 1
                        k.tt(o_[:, 0:tbw], U[:, sl], X[:, sl], ALU.mult, [U, X], [o_])
                        k.dma(k.pool, D["O"][4 + cc, :, t0 + tb * tbw:t0 + (tb + 1) * tbw], o_[:, 0:tbw], [o_], [g.OR[4 + cc]])

        for (t0, _, ctx) in seqs:
            short_conv(U, 0, t0)
            short_conv(X, 2, t0)
            long_conv(0, t0, False)
            short_conv(X, 4, t0)
            long_conv(1, t0, True)
        k.barrier()


def phase_hyena(g, l):
    for n, sfx, seqs in HYN:
        hyena_filters(g, l, n, sfx)
        hyena_convs(g, l, n, sfx, seqs)

def build(depth=DEPTH, mixers=("a", "b", "c", "d"), debug=False):
    nc = bass.Bass("TRN2", target_bir_lowering=False)
    D = {}

    def din(name, shape, dt=F32):
        D[name] = nc.dram_tensor(name, list(shape), dt, kind="ExternalInput").ap()

    def dout(name, shape, dt=F32):
        D[name] = nc.dram_tensor(name, list(shape), dt, kind="ExternalOutput").ap()

    def dscr(name, shape, dt=F32):
        kind = "ExternalOutput" if debug else "Internal"
        D[name] = nc.dram_tensor(name, list(shape), dt, kind=kind).ap()

    for name, shape in input_shapes().items():
        din(name, shape, BF16 if name[:2] in ("Fr", "Gr", "Fq") else F32)
    dout("yT", [8, 128, NT])
    dout("mlaT", [DEPTH, 160, NPR])
    dout("dkT", [DEPTH, 256, NPR])
    dout("dvT", [DEPTH, 256, NPR])
    dout("hg", [DEPTH, 4, 2, 4, 64, 64])
    dscr("PJ", [NCC, 128, NT])
    dscr("PT", [NT, 1024])
    dscr("O", [8, 128, NT], BF16)
    for nm, shp, dt in extra_scratch():
        dscr(nm, shp, dt)

    with ExitStack() as es:
        k = K(nc, es)
        g = Ctx()
        g.k, g.nc, g.D = k, nc, D
        g.PS = [k.psum(es, "ps%d" % i, [128, 512]) for i in range(8)]
        g.modv = [k.sb(es, "modv%d" % l, [128, 144], F32) for l in range(DEPTH)]
        g.XR = [[Res() for _ in range(NTB)] for _ in range(8)]
        g.PJR = [Res() for _ in range(NCC)]
        g.PTR = Res()
        g.OR = [Res() for _ in range(8)]
        g.xwritten = set()
        g.ones_ms = k.sb(es, "ones_ms", [128, 128], BF16)
        k.memset(g.ones_ms[:], 1.0 / 1024.0, [g.ones_ms])
        g.eps_t = k.sb(es, "eps_t", [128, 1], F32)
        k.memset(g.eps_t[:], EPS, [g.eps_t])
        setup_consts(g, es)
        phase_mod(g)
        if debug == "mod":
            for l in range(DEPTH):
                k.dma(k.sp, D["PT"][l * 128:(l + 1) * 128, 0:144], g.modv[l][:], [g.modv[l]], [])
            depth = 0
        for l in range(depth):
            phase_ffn(g, l, 0)
            phase_proj(g, l)
            phase_mixers(g, l, mixers)
            phase_out(g, l)
            phase_ffn(g, l, 1)
        k.finish()
        g.stats = (k.n_inst, k.n_wait)
    return nc, g


def input_shapes():
    s = {
        "xT": [8, 128, NT], "cT": [128, 16], "w_mod": [DEPTH, 1024, 9216], "bmodT": [DEPTH, 128, 72],
        "normgT": [DEPTH, 128, 24], "wgu": [DEPTH, 2, NJ, 128, 2048], "wd": [DEPTH, 2, 8, 128, NJ * 128],
        "win": [DEPTH, NCC, 128, 1024], "wtm": [DEPTH, 128, 8192], "w_out": [DEPTH, 1024, 1024],
    }
    s.update(mixer_input_shapes())
    return s


def host_shared(inp):
    f = np.float32
    S = {}
    S["w_mod"] = np.ascontiguousarray(inp["w_mod"], dtype=f)
    S["bmodT"] = np.ascontiguousarray(inp["b_mod"].reshape(DEPTH, 72, 128).transpose(0, 2, 1), dtype=f)
    S["normgT"] = np.ascontiguousarray(inp["norm_g"].reshape(DEPTH, 24, 128).transpose(0, 2, 1), dtype=f)
    wgu = inp["ffn_w_gu"].reshape(DEPTH, 2, 8, 128, 2, NJ, 128)
    S["wgu"] = np.ascontiguousarray(wgu.transpose(0, 1, 5, 3, 4, 2, 6).reshape(DEPTH, 2, NJ, 128, 2048), dtype=f)
    wd = inp["ffn_w_down"].reshape(DEPTH, 2, NJ, 128, 8, 128)
    S["wd"] = np.ascontiguousarray(wd.transpose(0, 1, 4, 3, 2, 5).reshape(DEPTH, 2, 8, 128, NJ * 128), dtype=f)
    win = np.zeros((DEPTH, 1024, NCC * 128), f)
    win[:, :, :INW] = inp["w_in"]
    win = win.reshape(DEPTH, 8, 128, NCC, 128)
    S["win"] = np.ascontiguousarray(win.transpose(0, 3, 2, 1, 4).reshape(DEPTH, NCC, 128, 1024))
    wi = inp["w_in"]
    tm = np.concatenate([wi[:, :, R_AI:R_AI + 256], wi[:, :, R_AFF:R_AFF + 256], wi[:, :, R_AFB:R_AFB + 256],
                         wi[:, :, R_DV:R_DV + 256]], axis=2)
    tm = tm.reshape(DEPTH, 8, 128, 1024).transpose(0, 2, 1, 3)
    S["wtm"] = np.ascontiguousarray(tm.reshape(DEPTH, 128, 8192), dtype=f)
    S["w_out"] = np.ascontiguousarray(inp["w_out"], dtype=f)
    S.update(mixer_host_shared(inp))
    return S


def host_core(inp, core):
    f = np.float32
    C = {}
    xp = inp["x_prompt"][core * 4:(core + 1) * 4].reshape(NPR, 1024)
    xs = inp["x_sample"][core]
    x = np.concatenate([xp, xs], axis=0)
    C["xT"] = np.ascontiguousarray(x.T.reshape(8, 128, NT), dtype=f)
    cc = np.stack([inp["c_ctx"], inp["c"][core]], axis=1)
    C["cT"] = np.ascontiguousarray(cc.reshape(8, 128, 2).transpose(1, 0, 2).reshape(128, 16), dtype=f)
    C.update(mixer_host_core(inp, core))
    return C


_CACHE = {}


def kernel(**inp):
    inp = {k_: np.asarray(v) for k_, v in inp.items()}
    if "nc" not in _CACHE:
        _CACHE["nc"] = build()[0]
    nc = _CACHE["nc"]
    S = host_shared(inp)
    in_maps = []
    for core in range(8):
        m = dict(S)
        m.update(host_core(inp, core))
        in_maps.append(m)
    res = run_bass_kernel_spmd(nc, in_maps, core_ids=list(range(8)))
    R = res.results
    f = np.float32
    yp = np.zeros((32, 256, 1024), f)
    ys = np.zeros((8, 2048, 1024), f)
    mla = np.zeros((32, DEPTH, 256, 160), f)
    dk = np.zeros((32, DEPTH, 256, 4, 2, 32), f)
    dv = np.zeros((32, DEPTH, 256, 4, 64), f)
    hg = np.zeros((32, DEPTH, 2, 4, 64, 64), f)
    for core in range(8):
        r = R[core]
        y = np.asarray(r["yT"]).reshape(1024, NT).T
        yp[core * 4:(core + 1) * 4] = y[:NPR].reshape(4, 256, 1024)
        ys[core] = y[NPR:]
        m_ = np.asarray(r["mlaT"]).reshape(DEPTH, 160, 4, 256)
        mla[core * 4:(core + 1) * 4] = m_.transpose(2, 0, 3, 1)
        k_ = np.asarray(r["dkT"]).reshape(DEPTH, 256, 4, 256)
        dk[core * 4:(core + 1) * 4] = k_.transpose(2, 0, 3, 1).reshape(4, DEPTH, 256, 4, 2, 32)
        v_ = np.asarray(r["dvT"]).reshape(DEPTH, 256, 4, 256)
        dv[core * 4:(core + 1) * 4] = v_.transpose(2, 0, 3, 1).reshape(4, DEPTH, 256, 4, 64)
        h_ = np.asarray(r["hg"])
        hg[core * 4:(core + 1) * 4] = h_.transpose(1, 0, 2, 3, 4, 5)
    return (yp, ys, mla, dk, dv, hg)
```

```python
from concourse.bass_utils import run_bass_kernel_spmd
import ml_dtypes
import numpy as np
from contextlib import ExitStack
import concourse.bass as bass
import concourse.mybir as mybir

F32 = mybir.dt.float32
BF16 = mybir.dt.bfloat16
I32 = mybir.dt.int32
AF = mybir.ActivationFunctionType
ALU = mybir.AluOpType
AX = mybir.AxisListType


class Res:
    __slots__ = ("w", "rs")

    def __init__(self):
        self.w = None
        self.rs = {}


class T:
    __slots__ = ("t", "res")

    def __init__(self, t, res=None):
        self.t = t
        self.res = res if res is not None else Res()

    def __getitem__(self, idx):
        return self.t[idx]


class Eng:
    def __init__(self, k, eng, name, inorder):
        self.k = k
        self.eng = eng
        self.name = name
        self.inorder = inorder
        self.sem = k.new_sem("s_" + name)
        self.semid = k.semid(self.sem)
        self.cnt = 0
        self.pending = False
        self.seen = {}
        self.dsems = None
        self.dvals = None
        self.di = 0

    def init_dma(self, ns):
        self.dsems = [self.k.new_sem("d_%s_%d" % (self.name, i)) for i in range(ns)]
        self.dids = [self.k.semid(s) for s in self.dsems]
        self.dvals = [0] * ns


class K:
    def __init__(self, nc, es):
        self.nc = nc
        self.es = es
        self._sems = {}
        self._nsem = 0
        self.pe = Eng(self, nc.tensor, "pe", True)
        self.act = Eng(self, nc.scalar, "act", False)
        self.dve = Eng(self, nc.vector, "dve", False)
        self.pool = Eng(self, nc.gpsimd, "pool", False)
        self.sp = Eng(self, nc.sync, "sp", False)
        self.engs = [self.pe, self.act, self.dve, self.pool, self.sp]
        self.sp.init_dma(16)
        self.pool.init_dma(16)
        self.act.init_dma(8)
        self.all_dma_toks = []
        self.same_engine_sync = True
        self.n_inst = 0
        self.n_wait = 0

    def new_sem(self, name):
        s = self.es.enter_context(self.nc.semaphore(name))
        self._nsem += 1
        self._sems[id(s)] = self._nsem
        return s

    def semid(self, s):
        return self._sems[id(s)]

    def need(self, E, tok):
        if tok is None:
            return
        sem, val, sid = tok
        if sid == E.semid and (E.inorder or not self.same_engine_sync):
            return
        if E.seen.get(sid, 0) >= val:
            return
        E.eng.wait_ge(sem, val)
        self.n_wait += 1
        E.seen[sid] = val

    def _deps(self, E, reads, writes):
        for r in reads:
            r = r.res if isinstance(r, T) else r
            self.need(E, r.w)
        for w in writes:
            w = w.res if isinstance(w, T) else w
            self.need(E, w.w)
            for tok in w.rs.values():
                self.need(E, tok)

    def _post(self, tok, reads, writes):
        sid = tok[2]
        for r in reads:
            r = r.res if isinstance(r, T) else r
            old = r.rs.get(sid)
            if old is None or old[1] < tok[1]:
                r.rs[sid] = tok
        for w in writes:
            w = w.res if isinstance(w, T) else w
            w.w = tok
            w.rs = {}

    def op(self, E, fn, reads=(), writes=(), signal=True):
        self._deps(E, reads, writes)
        inst = fn()
        self.n_inst += 1
        tok = (E.sem, E.cnt + 1, E.semid)
        if signal:
            inst.then_inc(E.sem, 1)
            E.cnt += 1
            E.pending = False
        else:
            E.pending = True
        self._post(tok, reads, writes)
        return inst

    def dma(self, Q, out, in_, reads=(), writes=(), **kw):
        self._deps(Q, reads, writes)
        ns = len(Q.dsems)
        slot = Q.di % ns
        Q.di += 1
        sem = Q.dsems[slot]
        sid = Q.dids[slot]
        if Q.dvals[slot] > 0:
            self.need(Q, (sem, Q.dvals[slot], sid))
        inst = Q.eng.dma_start(out=out, in_=in_, **kw)
        inst.then_inc(sem, 16)
        self.n_inst += 1
        Q.dvals[slot] += 16
        tok = (sem, Q.dvals[slot], sid)
        self._post(tok, reads, writes)
        return inst

    def barrier(self):
        toks = []
        for F in self.engs:
            assert not F.pending, F.name
            if F.cnt > 0:
                toks.append((F.sem, F.cnt, F.semid))
            if F.dsems is not None:
                for s, v, i in zip(F.dsems, F.dvals, F.dids):
                    if v > 0:
                        toks.append((s, v, i))
        for E in self.engs:
            for tok in toks:
                if tok[2] == E.semid:
                    if E.inorder:
                        continue
                self.need(E, tok)

    def finish(self):
        self.barrier()

    def sb(self, es, name, shape, dtype):
        self._uid = getattr(self, "_uid", 0) + 1
        name = "sb%d_%s" % (self._uid, name)
        t = es.enter_context(self.nc.sbuf_tensor(name, shape, dtype))
        return T(t)

    def psum(self, es, name, shape, dtype=F32):
        self._uid = getattr(self, "_uid", 0) + 1
        name = "pp%d_%s" % (self._uid, name)
        t = es.enter_context(self.nc.psum_tensor(name, shape, dtype))
        return T(t)

    def mm(self, out, lhsT, rhs, start, stop, reads, writes, signal=None, **kw):
        if signal is None:
            signal = True
        return self.op(self.pe, lambda: self.nc.tensor.matmul(out, lhsT, rhs, start=start, stop=stop, **kw),
                       reads, writes, signal=signal)

    def transpose(self, out, in_, ident, reads, writes, signal=True):
        return self.op(self.pe, lambda: self.nc.tensor.transpose(out, in_, ident), reads, writes, signal=signal)

    def activation(self, out, in_, func, reads, writes, bias=None, scale=None, accum_out=None, E=None):
        E = E or self.act
        kw = {}
        if bias is not None:
            kw["bias"] = bias
        if scale is not None:
            kw["scale"] = scale
        if accum_out is not None:
            kw["accum_out"] = accum_out
        return self.op(E, lambda: E.eng.activation(out=out, in_=in_, func=func, **kw), reads, writes)

    def tt(self, out, in0, in1, op, reads, writes, E=None):
        E = E or self.dve
        return self.op(E, lambda: E.eng.tensor_tensor(out=out, in0=in0, in1=in1, op=op), reads, writes)

    def ts(self, out, in0, s1, s2, op0, op1=None, reads=(), writes=(), E=None, accum_out=None):
        E = E or self.dve
        kw = {}
        if op1 is not None:
            kw["op1"] = op1
        if accum_out is not None:
            kw["accum_out"] = accum_out
        return self.op(E, lambda: E.eng.tensor_scalar(out=out, in0=in0, scalar1=s1, scalar2=s2, op0=op0, **kw),
                       reads, writes)

    def stt(self, out, in0, scalar, in1, op0, op1, reads, writes):
        E = self.dve
        return self.op(E, lambda: E.eng.scalar_tensor_tensor(out=out, in0=in0, scalar=scalar, in1=in1, op0=op0, op1=op1),
                       reads, writes)

    def copy(self, out, in_, reads, writes, E=None):
        E = E or self.dve
        if E is self.act:
            return self.op(E, lambda: E.eng.copy(out=out, in_=in_), reads, writes)
        return self.op(E, lambda: E.eng.tensor_copy(out=out, in_=in_), reads, writes)

    def memset(self, ap, val, writes, E=None):
        E = E or self.dve
        return self.op(E, lambda: E.eng.memset(ap, val), (), writes)

DEPTH = 4
NT = 3072
NPR = 1024
NTB = 6
DFF = 2816
NJ = 22
EPS = 1e-6
INW = 3232
NCC = 26
R_AQ, R_AI, R_AFF, R_AFB, R_AOG = 0, 256, 512, 768, 1024
R_CQ, R_CKV, R_KR = 1280, 1536, 1664
R_CV, R_CX1, R_CX2 = 1696, 1952, 2208
R_DQ, R_DK, R_DV = 2464, 2720, 2976


class Ctx:
    pass


def mvec(g, l, kind, m, cond):
    i = (kind * 8 + m) * 2 + cond
    return g.modv[l][:, i:i + 1]


def xsrc_ap(g, m, tb):
    base = g.D["yT"] if (m, tb) in g.xwritten else g.D["xT"]
    return base[tb, :, m, :]


def phase_mod(g):
    k, nc, D = g.k, g.nc, g.D
    with ExitStack() as es:
        cT = k.sb(es, "cT", [128, 16], F32)
        sc = k.sb(es, "scT", [128, 16], BF16)
        k.dma(k.sp, cT[:], D["cT"], (), [cT])
        k.activation(sc[:], cT[:], AF.Silu, [cT], [sc])
        wm = [k.sb(es, "wm%d" % i, [128, 8 * 512], BF16) for i in range(3)]
        bm = [k.sb(es, "bm%d" % i, [128, 72], F32) for i in range(2)]
        ng = [k.sb(es, "ng%d" % i, [128, 24], F32) for i in range(2)]
        wsrc = D["w_mod"]
        for l in range(DEPTH):
            b_, n_ = bm[l % 2], ng[l % 2]
            k.dma(k.sp, b_[:], D["bmodT"][l], (), [b_])
            k.dma(k.sp, n_[:], D["normgT"][l], (), [n_])
            ps = g.PS[l % 2]
            wl = wsrc[l].rearrange("(k p) c -> p k c", p=128)
            for pc in range(18):
                w = wm[pc % 3]
                k.dma(k.pool, w[:].rearrange("p (k c) -> p k c", k=8), wl[:, :, pc * 512:(pc + 1) * 512], (), [w])
                for q4 in range(4):
                    q = pc * 4 + q4
                    for kk in range(8):
                        k.mm(ps[:, q * 2:q * 2 + 2], w[:, kk * 512 + q4 * 128: kk * 512 + (q4 + 1) * 128],
                             sc[:, kk * 2:kk * 2 + 2], kk == 0, kk == 7, [w, sc], [ps])
            mv = g.modv[l]
            mv3 = mv[:].rearrange("p (q c) -> p q c", c=2)
            ps3 = ps[:, 0:144].rearrange("p (q c) -> p q c", c=2)
            for cond in range(2):
                k.tt(mv3[:, :, cond], ps3[:, :, cond], b_[:], ALU.add, [ps, b_], [mv])
            for k3 in range(3):
                q0 = (3 * k3 + 1) * 8
                for cond in range(2):
                    k.stt(mv3[:, q0:q0 + 8, cond], mv3[:, q0:q0 + 8, cond], 1.0, n_[:, k3 * 8:(k3 + 1) * 8],
                          ALU.add, ALU.mult, [mv, n_], [mv])
            for k3 in (0, 2):
                q0 = (3 * k3 + 2) * 8
                k.ts(mv[:, q0 * 2:(q0 + 8) * 2], mv[:, q0 * 2:(q0 + 8) * 2], 0.5, None, ALU.mult, reads=[mv], writes=[mv])
        k.barrier()


def phase_norm(g, l, kn, h):
    k, nc, D = g.k, g.nc, g.D
    with ExitStack() as es:
        xt = [k.sb(es, "nx%d" % i, [128, 8 * 512], F32) for i in range(2)]
        sq = [k.sb(es, "nsq%d" % i, [128, 512], BF16) for i in range(4)]
        rts = [k.sb(es, "nrt%d" % i, [128, 512], F32) for i in range(2)]
        rss = [k.sb(es, "nrs%d" % i, [128, 512], F32) for i in range(2)]
        tmp = [k.sb(es, "ntmp%d" % i, [128, 512], F32) for i in range(4)]
        for tb in range(NTB):
            cond = 0 if tb < 2 else 1
            x = xt[tb % 2]
            xbase = g.D["yT"] if (0, tb) in g.xwritten else g.D["xT"]
            assert all((((m, tb) in g.xwritten) == ((0, tb) in g.xwritten)) for m in range(8))
            k.dma(k.sp, x[:].rearrange("p (m t) -> p m t", m=8), xbase[tb], [g.XR[m][tb] for m in range(8)], [x])
            ps = g.PS[6 + tb % 2]
            rt = rts[tb % 2]
            rs = rss[tb % 2]
            for m in range(8):
                s = sq[m % 4]
                xm = x[:, m * 512:(m + 1) * 512]
                if m % 2 == 0:
                    k.tt(s[:], xm, xm, ALU.mult, [x], [s], E=k.pool)
                else:
                    k.activation(s[:], xm, AF.Square, [x], [s])
                k.mm(ps[:], g.ones_ms[:], s[:], m == 0, m == 7, [s], [ps])
            k.activation(rt[:], ps[:], AF.Sqrt, [ps], [rt], bias=g.eps_t[:, 0:1])
            k.op(k.dve, lambda: nc.vector.reciprocal(out=rs[:], in_=rt[:]), [rt], [rs])
            for m in range(8):
                t = tmp[m % 4]
                k.tt(t[:], x[:, m * 512:(m + 1) * 512], rs[:], ALU.mult, [x, rs], [t], E=(k.pool if m % 4 == 3 else k.dve))
                k.activation(h[m][:, tb * 512:(tb + 1) * 512], t[:], AF.Identity, [t], [h[m]],
                             bias=mvec(g, l, 3 * kn, m, cond), scale=mvec(g, l, 3 * kn + 1, m, cond))
        k.barrier()


def resid_update(g, l, gate_kind, m, tb, po, xo, xn):
    k, D = g.k, g.D
    cond = 0 if tb < 2 else 1
    k.dma(k.sp, xo[:], xsrc_ap(g, m, tb), [g.XR[m][tb]], [xo])
    k.stt(xn[:], po[:], mvec(g, l, gate_kind, m, cond), xo[:], ALU.mult, ALU.add, [po, xo], [xn])
    g.xwritten.add((m, tb))
    k.dma(k.pool, D["yT"][tb, :, m, :], xn[:], [xn], [g.XR[m][tb]])


def phase_ffn(g, l, w):
    k, nc, D = g.k, g.nc, g.D
    kn = 0 if w == 0 else 2
    with ExitStack() as es:
        h = [k.sb(es, "h%d" % m, [128, NT], BF16) for m in range(8)]
        phase_norm(g, l, kn, h)
        act = [k.sb(es, "a%d" % j, [128, NT], BF16) for j in range(11)]
        wt = [k.sb(es, "fw%d" % i, [128, 2048], BF16) for i in range(2)]
        sa = [k.sb(es, "fsa%d" % i, [128, 512], F32) for i in range(2)]
        wdt = [k.sb(es, "fwd%d" % i, [128, 11 * 128], BF16) for i in range(2)]
        xo = [k.sb(es, "fxo%d" % i, [128, 512], F32) for i in range(4)]
        xn = [k.sb(es, "fxn%d" % i, [128, 512], F32) for i in range(4)]
        it = 0
        it2 = 0

        def load_wgu(j):
            k.dma(k.pool, wt[j % 2][:], D["wgu"][l, w, j], (), [wt[j % 2]])

        def load_wd(half, m):
            k.dma(k.pool, wdt[m % 2][:], D["wd"][l, w, m][:, half * 1408:(half + 1) * 1408], (), [wdt[m % 2]])

        load_wgu(0)
        for half in range(2):
            for jj in range(11):
                j = half * 11 + jj
                wj = wt[j % 2]
                if jj < 10:
                    load_wgu(j + 1)
                else:
                    load_wd(half, 0)
                for tb in range(NTB):
                    pa = g.PS[(it % 2) * 2]
                    pu = g.PS[(it % 2) * 2 + 1]
                    s = sa[it % 2]
                    it += 1
                    for kk in range(8):
                        k.mm(pa[:], wj[:, kk * 128:(kk + 1) * 128], h[kk][:, tb * 512:(tb + 1) * 512],
                             kk == 0, kk == 7, [wj, h[kk]], [pa])
                    for kk in range(8):
                        k.mm(pu[:], wj[:, 1024 + kk * 128:1024 + (kk + 1) * 128], h[kk][:, tb * 512:(tb + 1) * 512],
                             kk == 0, kk == 7, [wj, h[kk]], [pu])
                    k.activation(s[:], pa[:], AF.Silu, [pa], [s])
                    k.tt(act[jj][:, tb * 512:(tb + 1) * 512], s[:], pu[:], ALU.mult, [s, pu], [act[jj]])
            for m in range(8):
                wm_ = wdt[m % 2]
                if m < 7:
                    load_wd(half, m + 1)
                elif half == 0:
                    load_wgu(11)
                for tb in range(NTB):
                    po = g.PS[4 + it2 % 2]
                    for jj in range(11):
                        k.mm(po[:], wm_[:, jj * 128:(jj + 1) * 128], act[jj][:, tb * 512:(tb + 1) * 512],
                             jj == 0, jj == 10, [wm_, act[jj]], [po])
                    resid_update(g, l, 3 * kn + 2, m, tb, po, xo[it2 % 4], xn[it2 % 4])
                    it2 += 1
        k.barrier()


def phase_proj(g, l):
    k, nc, D = g.k, g.nc, g.D
    with ExitStack() as es:
        h = [k.sb(es, "h%d" % m, [128, NT], BF16) for m in range(8)]
        phase_norm(g, l, 1, h)
        wt = [k.sb(es, "pw%d" % i, [128, 1024], BF16) for i in range(2)]
        st = [k.sb(es, "pst%d" % i, [128, NT], F32) for i in range(2)]
        wtm = k.sb(es, "pwtm", [128, 8192], BF16)
        st2 = [k.sb(es, "pst2%d" % i, [128, 1024], F32) for i in range(2)]
        for q in range(4):
            k.dma(k.pool, wtm[:, q * 2048:(q + 1) * 2048], D["wtm"][l][:, q * 2048:(q + 1) * 2048], (), [wtm])
        it = 0
        for cc in range(NCC):
            wj = wt[cc % 2]
            k.dma(k.pool, wj[:], D["win"][l, cc], (), [wj])
            s = st[cc % 2]
            for tb in range(NTB):
                ps = g.PS[it % 4]
                for kk in range(8):
                    k.mm(ps[:], wj[:, kk * 128:(kk + 1) * 128], h[kk][:, tb * 512:(tb + 1) * 512],
                         kk == 0, kk == 7, [wj, h[kk]], [ps])
                k.copy(s[:, tb * 512:(tb + 1) * 512], ps[:], [ps], [s], E=(k.act if it % 2 == 0 else k.dve))
                it += 1
            k.dma(k.sp, D["PJ"][cc], s[:], [s], [g.PJR[cc]])
        for tt in range(NT // 128):
            s2 = st2[tt % 2]
            for hf in range(2):
                ps = g.PS[4 + it % 4]
                for kk in range(8):
                    k.mm(ps[:], h[kk][:, tt * 128:(tt + 1) * 128], wtm[:, kk * 1024 + hf * 512: kk * 1024 + (hf + 1) * 512],
                         kk == 0, kk == 7, [wtm, h[kk]], [ps])
                k.copy(s2[:, hf * 512:(hf + 1) * 512], ps[:], [ps], [s2], E=(k.act if it % 2 == 0 else k.dve))
                it += 1
            k.dma(k.sp, D["PT"][tt * 128:(tt + 1) * 128, :], s2[:], [s2], [g.PTR])
        k.barrier()


def phase_out(g, l):
    k, nc, D = g.k, g.nc, g.D
    with ExitStack() as es:
        wo = k.sb(es, "wo", [128, 8192], BF16)
        wsrc = D["w_out"][l].rearrange("(k p) c -> p k c", p=128)
        for kk in range(8):
            k.dma(k.pool, wo[:, kk * 1024:(kk + 1) * 1024], wsrc[:, kk, :], (), [wo])
        ot = [k.sb(es, "oo%d" % i, [128, 8 * 512], BF16) for i in range(2)]
        xo = [k.sb(es, "oxo%d" % i, [128, 512], F32) for i in range(4)]
        xn = [k.sb(es, "oxn%d" % i, [128, 512], F32) for i in range(4)]
        it = 0
        for tb in range(NTB):
            o = ot[tb % 2]
            for kk in range(8):
                k.dma(k.sp, o[:, kk * 512:(kk + 1) * 512], D["O"][kk, :, tb * 512:(tb + 1) * 512], [g.OR[kk]], [o])
            for m in range(8):
                po = g.PS[it % 4]
                for kk in range(8):
                    k.mm(po[:], wo[:, kk * 1024 + m * 128: kk * 1024 + (m + 1) * 128], o[:, kk * 512:(kk + 1) * 512],
                         kk == 0, kk == 7, [wo, o], [po])
                resid_update(g, l, 5, m, tb, po, xo[it % 4], xn[it % 4])
                it += 1
        k.barrier()

SEQS = [(0, 256, False), (256, 256, False), (512, 256, False), (768, 256, False), (1024, 2048, True)]
GROUPS = [(0, 1024, False, [(0, 256), (256, 256), (512, 256), (768, 256)]), (1024, 2048, True, [(0, 2048)])]
MAXKEY = 1.0 - 1e-6


def mixer_input_shapes():
    return {
        "ropeB": [2, 96, 2048], "ropeD": [2, 128, 2048], "rotB": [96, 96], "rotD": [128, 128], "bd32": [128, 128], "bd64": [128, 128], "esel": [32, 96],
        "wuq": [DEPTH, 128, 768], "wukv": [DEPTH, 128, 512], "gains": [DEPTH, 128, 8], "dlam": [DEPTH, 32, 4],
        "cmlaT": [DEPTH, 160, 512], "cdkT": [DEPTH, 256, 512], "cdv": [DEPTH, 512, 256],
        "s0": [DEPTH, 2, 4, 64, 64], "lbl": [64, 32], "gon": [DEPTH, 64, 1], "mask4": [2, 128, 256],
        "maskbd": [2, 128, 128], "ident": [64, 64],
        **hyena_input_shapes(),
    }


def extra_scratch():
    return hyena_scratch()


def _rope_tables(rot_dim, ngroups_before):
    n_f = rot_dim // 4
    inv = (10000.0 ** (-np.arange(n_f, dtype=np.float32) / n_f)).astype(np.float32)
    t = np.arange(2048)
    ang_r = (t // 64).astype(np.float32)[:, None] * inv
    ang_c = (t % 64).astype(np.float32)[:, None] * inv
    half = rot_dim // 2
    C = np.zeros((rot_dim, 2048), np.float32)
    S_ = np.zeros((rot_dim, 2048), np.float32)
    for d in range(rot_dim):
        ang = ang_r if d < half else ang_c
        i = (d % half) % n_f
        C[d] = np.cos(ang[:, i])
        S_[d] = np.sin(ang[:, i])
    R = np.zeros((rot_dim, rot_dim), np.float32)
    for d in range(rot_dim):
        dd = d % half
        if dd < n_f:
            R[d + n_f, d] = -1.0
        else:
            R[d - n_f, d] = 1.0
    return C, S_, R


def mixer_host_shared(inp):
    f = np.float32
    S = {}
    C, Sn, R = _rope_tables(32, 0)
    ropeB = np.zeros((2, 96, 2048), f)
    ropeB[0, :64] = 1.0
    ropeB[0, 64:] = C
    ropeB[1, 64:] = Sn
    S["ropeB"] = ropeB
    S["ropeD"] = np.ascontiguousarray(np.tile(np.stack([C, Sn]).astype(f), (1, 4, 1)))
    rotB = np.zeros((96, 96), f)
    rotB[64:, 64:] = R
    S["rotB"] = rotB
    S["rotD"] = np.kron(np.eye(4, dtype=f), R.astype(f))
    S["bd32"] = np.kron(np.eye(4, dtype=f), np.full((32, 32), 1.0 / 32.0, f))
    S["bd64"] = np.kron(np.eye(2, dtype=f), np.full((64, 64), 1.0 / 64.0, f))
    es_ = np.zeros((32, 96), f)
    es_[np.arange(32), 64 + np.arange(32)] = 1.0
    S["esel"] = es_
    S["wuq"] = np.ascontiguousarray(inp["mla_w_uq"].reshape(DEPTH, 2, 128, 384).transpose(0, 2, 1, 3).reshape(DEPTH, 128, 768), dtype=f)
    S["wukv"] = np.ascontiguousarray(inp["mla_w_ukv"], dtype=f)
    gains = np.zeros((DEPTH, 128, 8), f)
    gains[:, :, 0:2] = inp["mla_q_norm"].reshape(DEPTH, 2, 128).transpose(0, 2, 1)
    gains[:, :, 2] = inp["mla_kv_norm"]
    gains[:, :96, 3:5] = inp["mla_qk_norm"].transpose(0, 2, 1)
    gains[:, :, 5:7] = np.tile(inp["diff_qk_norm"].transpose(0, 2, 1), (1, 4, 1))
    gains[:, :64, 7] = inp["diff_subln"]
    S["gains"] = gains
    S["dlam"] = np.ascontiguousarray(inp["diff_lambda"].transpose(0, 2, 1), dtype=f)
    lb = inp["hgrn_lb_logits"].reshape(2, DEPTH, 4, 64)
    S["lbl"] = np.ascontiguousarray(lb.transpose(3, 0, 2, 1).reshape(64, 32), dtype=f)
    S["gon"] = np.ascontiguousarray(inp["hgrn_onorm"].reshape(DEPTH, 64, 1), dtype=f)
    s_ = np.arange(128)
    m4 = np.zeros((2, 128, 4, 64), f)
    for j in range(4):
        m4[0, s_ // 32 == j, j, :] = 1.0
        m4[1, s_ // 32 == 3 - j, j, :] = 1.0
    S["mask4"] = m4.reshape(2, 128, 256)
    same = (s_[:, None] // 32) == (s_[None, :] // 32)
    S["maskbd"] = np.stack([same & (s_[:, None] <= s_[None, :]), same & (s_[:, None] >= s_[None, :])]).astype(f)
    S["ident"] = np.eye(64, dtype=f)
    S.update(hyena_host_shared(inp))
    return S


def mixer_host_core(inp, core):
    f = np.float32
    C = {}
    C["cmlaT"] = np.ascontiguousarray(inp["cache_mla"][core].transpose(0, 2, 1), dtype=f)
    C["cdkT"] = np.ascontiguousarray(inp["cache_diff_k"][core].reshape(DEPTH, 512, 256).transpose(0, 2, 1), dtype=f)
    C["cdv"] = np.ascontiguousarray(inp["cache_diff_v"][core].reshape(DEPTH, 512, 256), dtype=f)
    C["s0"] = np.ascontiguousarray(inp["state_hgrn"][core], dtype=f)
    return C


def setup_consts(g, es):
    k, D = g.k, g.D
    g.ones = {}
    for gs in (256, 128, 96, 64, 32):
        t = k.sb(es, "ones%d" % gs, [128, 128], BF16)
        k.memset(t[:], 1.0 / gs, [t])
        g.ones[gs] = t
    g.rotB = k.sb(es, "rotB", [96, 96], BF16)
    g.rotD = k.sb(es, "rotD", [128, 128], BF16)
    g.bd32 = k.sb(es, "bd32", [128, 128], BF16)
    k.dma(k.pool, g.bd32[:], D["bd32"], (), [g.bd32])
    g.bd64 = k.sb(es, "bd64", [128, 128], BF16)
    k.dma(k.pool, g.bd64[:], D["bd64"], (), [g.bd64])
    g.esel = k.sb(es, "esel", [32, 96], BF16)
    k.dma(k.pool, g.rotB[:], D["rotB"], (), [g.rotB])
    k.dma(k.pool, g.rotD[:], D["rotD"], (), [g.rotD])
    k.dma(k.pool, g.esel[:], D["esel"], (), [g.esel])
    g.onesf = k.sb(es, "onesf", [32, 64], F32)
    k.memset(g.onesf[:], 1.0, [g.onesf])
    hgrn_setup(g, es)
    hyena_setup(g, es)


def pj_rows(g, r0, n, t0, nt):
    flat = g.D["PJ"].rearrange("c p t -> (c p) t")
    res = [g.PJR[c] for c in range(r0 // 128, (r0 + n - 1) // 128 + 1)]
    return flat[r0:r0 + n, t0:t0 + nt], res


def gnr_gen(g, W, src, srcres, gs, n, gain, rope, out, outres, rope_off=0, f32out=None, f32res=None, onesT=None,
                    gres=()):
    k, nc = g.k, g.nc
    sq, rs, t1, t2, pn, pr = W["sq"], W["rs"], W["t1"], W["t2"], W["pn"], W["pr"]
    k.activation(sq[0:gs, 0:n], src, AF.Square, srcres, [sq])
    yield
    oT = onesT if onesT is not None else g.ones[gs]
    k.mm(pn[0:gs, 0:n], oT[0:gs, 0:gs], sq[0:gs, 0:n], True, True, [sq, oT], [pn])
    yield
    k.activation(rs[0:gs, 0:n], pn[0:gs, 0:n], AF.Ln, [pn], [rs], bias=g.eps_t[0:gs, 0:1])
    yield
    k.activation(rs[0:gs, 0:n], rs[0:gs, 0:n], AF.Exp, [rs], [rs], scale=-0.5)
    yield
    if rope is None:
        if f32out is not None:
            k.stt(f32out, src, gain, rs[0:gs, 0:n], ALU.mult, ALU.mult, list(srcres) + [rs] + list(gres), f32res)
            yield
            k.copy(out, f32out, f32res, outres, E=k.pool)
            yield
        else:
            k.stt(out, src, gain, rs[0:gs, 0:n], ALU.mult, ALU.mult, list(srcres) + [rs] + list(gres), outres)
            yield
        return
    rot, Ct, St = rope
    k.stt(t1[0:gs, 0:n], src, gain, rs[0:gs, 0:n], ALU.mult, ALU.mult, list(srcres) + [rs] + list(gres), [t1])
    yield
    k.copy(t2[0:gs, 0:n], t1[0:gs, 0:n], [t1], [t2], E=k.act)
    yield
    k.mm(pr[0:gs, 0:n], rot[0:gs, 0:gs], t2[0:gs, 0:n], True, True, [t2], [pr])
    yield
    k.tt(t1[0:gs, 0:n], t1[0:gs, 0:n], Ct[0:gs, rope_off:rope_off + n], ALU.mult, [t1, Ct], [t1], E=k.pool)
    yield
    k.tt(rs[0:gs, 0:n], pr[0:gs, 0:n], St[0:gs, rope_off:rope_off + n], ALU.mult, [pr, St], [rs])
    yield
    k.tt(out, t1[0:gs, 0:n], rs[0:gs, 0:n], ALU.add, [t1, rs], outres, E=k.pool)
    yield


def group_norm_rope(*a, **kw):
    for _ in gnr_gen(*a, **kw):
        pass


def run_interleaved(gens):
    gens = list(gens)
    while gens:
        for g_ in list(gens):
            try:
                next(g_)
            except StopIteration:
                gens.remove(g_)


def attn_pipeline(g, W, streams, nk, nq):
    k = g.k
    nkt = nk // 128
    items = [(st, kt) for kt in range(nkt) for st in streams]
    NB = 3
    LA = 2

    def emit_sc(i):
        st, kt = items[i]
        sc = g.PS[i % NB]
        dk_ = st["dk"]
        kc = st.get("kt0", 0) + kt
        k.mm(sc[:, 0:nq], st["KT"][0:dk_, kc * 128:(kc + 1) * 128], st["QT"][0:dk_, st["q0"]:st["q0"] + nq], True, True,
             [st["KT"], st["QT"]], [sc])

    for i in range(min(LA, len(items))):
        emit_sc(i)
    for i in range(len(items)):
        if i + LA < len(items):
            emit_sc(i + LA)
        st, kt = items[i]
        sc = g.PS[i % NB]
        pt = W["pt"][i % NB]
        k.activation(pt[:, 0:nq], sc[:, 0:nq], AF.Exp, [sc], [pt], scale=st["scale"])
        h = st["hcol"]
        kc = st.get("kt0", 0) + kt
        k.mm(st["acc"][:, 0:nq], st["VA"][:, kc * 512 + h * 128: kc * 512 + (h + 1) * 128], pt[:, 0:nq], kt == 0, kt == nkt - 1,
             [st["VA"], pt], [st["acc"]])


def phase_mla(g, l):
    k, nc, D = g.k, g.nc, g.D
    scale = 96.0 ** -0.5
    with ExitStack() as es:
        g.ropeB = [k.sb(es, "ropeB%d" % i, [96, 2048], F32) for i in range(2)]
        for i in range(2):
            k.dma(k.sp, g.ropeB[i][:], D["ropeB"][i], (), [g.ropeB[i]])
        gn = k.sb(es, "gn", [128, 8], F32)
        k.dma(k.sp, gn[:], D["gains"][l], (), [gn])
        wuq = k.sb(es, "wuq", [128, 768], BF16)
        k.dma(k.pool, wuq[:], D["wuq"][l], (), [wuq])
        wkv = k.sb(es, "wkv", [128, 512], BF16)
        k.dma(k.pool, wkv[:], D["wukv"][l], (), [wkv])
        wkp = k.sb(es, "wkp", [128, 4 * 96], BF16)
        wv = k.sb(es, "wv", [128, 256], BF16)
        k.memset(wkp[:], 0.0, [wkp])
        for h in range(4):
            k.copy(wkp[:, h * 96:h * 96 + 64], wkv[:, h * 128:h * 128 + 64], [wkv], [wkp])
            k.copy(wv[:, h * 64:(h + 1) * 64], wkv[:, h * 128 + 64:(h + 1) * 128], [wkv], [wv])
        QT = [k.sb(es, "QT%d" % h, [96, 2048], BF16) for h in range(4)]
        KT = [k.sb(es, "KT%d" % h, [96, 2560], BF16) for h in range(4)]
        VA = k.sb(es, "VA", [128, 20 * 512], BF16)
        k.memset(VA[:], 1.0, [VA])
        W = {"sq": k.sb(es, "wsq", [128, 512], BF16), "rs": k.sb(es, "wrs", [128, 512], F32),
             "t1": k.sb(es, "wt1", [128, 512], F32), "t2": k.sb(es, "wt2", [128, 512], BF16),
             "pn": g.PS[3], "pr": g.PS[3], "pt": [k.sb(es, "wpt%d" % i, [128, 512], BF16) for i in range(3)]}
        WB = {"sq": k.sb(es, "wsqB", [128, 512], BF16), "rs": k.sb(es, "wrsB", [128, 512], F32),
              "t1": k.sb(es, "wt1B", [128, 512], F32), "t2": k.sb(es, "wt2B", [128, 512], BF16),
              "pn": g.PS[2], "pr": g.PS[2]}
        Ws = [W, WB]
        pqs = [g.PS[4], g.PS[6]]
        cq = k.sb(es, "cq", [128, 1024], F32)
        cqn = k.sb(es, "cqn", [128, 1024], BF16)
        ckv = k.sb(es, "ckv", [128, 512], F32)
        ckn = k.sb(es, "ckn", [128, 512], F32)
        ckb = k.sb(es, "ckb", [128, 512], BF16)
        kr = k.sb(es, "kr", [32, 512], F32)
        krb = k.sb(es, "krb", [32, 512], BF16)
        rd = [k.sb(es, "rd%d" % i, [64, 512], F32) for i in range(2)]
        ob = [k.sb(es, "ob%d" % i, [64, 512], BF16) for i in range(4)]
        pq = g.PS[4]
        pv = g.PS[5]
        oi = 0

        def keys_block(ckb_, krb_, n, kcol, rope_off):
            rope = None if rope_off is None else (g.rotB, g.ropeB[0], g.ropeB[1])

            def kchain(h, Wx, pq_):
                k.mm(pq_[0:96, 0:n], wkp[:, h * 96:(h + 1) * 96], ckb_[:, 0:n], True, False, [wkp, ckb_], [pq_])
                yield
                k.mm(pq_[0:96, 0:n], g.esel[:, :], krb_[:, 0:n], False, True, [krb_], [pq_])
                yield
                yield from gnr_gen(g, Wx, pq_[0:96, 0:n], [pq_], 96, n, gn[0:96, 4:5], rope, KT[h][:, kcol:kcol + n], [KT[h]],
                                   rope_off=rope_off or 0, gres=[gn])

            for hp in range(2):
                run_interleaved([kchain(2 * hp + j, Ws[j], pqs[j]) for j in range(2)])
            for tt in range(n // 128):
                k.mm(pv[:, 0:256], ckb_[:, tt * 128:(tt + 1) * 128], wv[:, :], True, True, [ckb_, wv], [pv])
                kt = kcol // 128 + tt
                k.copy(VA[:, kt * 512:(kt + 1) * 512].rearrange("p (h c) -> p h c", h=4)[:, :, 0:64],
                       pv[:, 0:256].rearrange("p (h c) -> p h c", h=4), [pv], [VA])

        for (t0, L, ctx, seqs) in GROUPS:
            koff = 512 if ctx else 0
            if ctx:
                k.dma(k.sp, ckn[:], D["cmlaT"][l, 0:128, :], (), [ckn])
                k.dma(k.sp, kr[:], D["cmlaT"][l, 128:160, :], (), [kr])
                k.copy(ckb[:], ckn[:], [ckn], [ckb])
                k.copy(krb[:], kr[:], [kr], [krb])
                keys_block(ckb, krb, 512, 0, None)
            for b0 in range(0, L, 512):
                n = min(512, L - b0)
                tg = t0 + b0
                for m in range(2):
                    ap_, rr = pj_rows(g, R_CQ + m * 128, 128, tg, n)
                    k.dma(k.sp, cq[:, m * 512:m * 512 + n], ap_, rr, [cq])
                ap_, rr = pj_rows(g, R_CKV, 128, tg, n)
                k.dma(k.sp, ckv[:, 0:n], ap_, rr, [ckv])
                ap_, rr = pj_rows(g, R_KR, 32, tg, n)
                k.dma(k.sp, kr[:, 0:n], ap_, rr, [kr])
                pn = W["pn"]
                for m in range(2):
                    k.activation(W["sq"][:, 0:n], cq[:, m * 512:m * 512 + n], AF.Square, [cq], [W["sq"]])
                    k.mm(pn[:, 0:n], g.ones[256][:, :], W["sq"][:, 0:n], m == 0, m == 1, [W["sq"]], [pn])
                k.activation(W["rs"][:, 0:n], pn[:, 0:n], AF.Sqrt, [pn], [W["rs"]], bias=g.eps_t[:, 0:1])
                k.op(k.dve, lambda: nc.vector.reciprocal(out=W["rs"][:, 0:n], in_=W["rs"][:, 0:n]), [W["rs"]], [W["rs"]])
                for m in range(2):
                    k.stt(cqn[:, m * 512:m * 512 + n], cq[:, m * 512:m * 512 + n], gn[:, m:m + 1], W["rs"][:, 0:n],
                          ALU.mult, ALU.mult, [cq, W["rs"], gn], [cqn])
                group_norm_rope(g, W, ckv[:, 0:n], [ckv], 128, n, gn[:, 2:3], None, ckb[:, 0:n], [ckb],
                                f32out=ckn[:, 0:n], f32res=[ckn], gres=[gn])
                k.copy(krb[:, 0:n], kr[:, 0:n], [kr], [krb])
                if not ctx:
                    k.dma(k.sp, D["mlaT"][l, 0:128, tg:tg + n], ckn[:, 0:n], [ckn], [])
                    k.dma(k.sp, D["mlaT"][l, 128:160, tg:tg + n], kr[:, 0:n], [kr], [])
                rope_q = (g.rotB, g.ropeB[0], g.ropeB[1]) if ctx else None

                def qchain(h, Wx, pq_):
                    for m in range(2):
                        k.mm(pq_[0:96, 0:n], wuq[:, m * 384 + h * 96: m * 384 + (h + 1) * 96], cqn[:, m * 512:m * 512 + n],
                             m == 0, m == 1, [wuq, cqn], [pq_])
                        yield
                    yield from gnr_gen(g, Wx, pq_[0:96, 0:n], [pq_], 96, n, gn[0:96, 3:4], rope_q, QT[h][:, b0:b0 + n], [QT[h]],
                                       rope_off=b0, gres=[gn])

                for hp in range(2):
                    run_interleaved([qchain(2 * hp + j, Ws[j], pqs[j]) for j in range(2)])
                keys_block(ckb, krb, n, koff + b0, b0 if ctx else None)
            for (s0, Ls) in seqs:
                nk = Ls + koff
                kt0 = 0 if ctx else s0 // 128
                for q0 in range(0, Ls, 512):
                    nq = min(512, Ls - q0)
                    for hp in range(2):
                        base = 4 + 2 * (oi % 2)
                        oi += 1
                        sts = []
                        for j in range(2):
                            h = hp * 2 + j
                            sts.append(dict(KT=KT[h], QT=QT[h], dk=96, q0=s0 + q0, kt0=kt0, VA=VA, hcol=h, scale=scale,
                                            acc=g.PS[base + j]))
                        attn_pipeline(g, W, sts, nk, nq)
                        for j in range(2):
                            h = hp * 2 + j
                            acc = sts[j]["acc"]
                            o_ = ob[(oi * 2 + j) % 4]
                            k.op(k.dve, lambda: nc.vector.reciprocal(out=rd[j][:, 0:nq], in_=acc[64:128, 0:nq]), [acc], [rd[j]])
                            k.tt(o_[:, 0:nq], acc[0:64, 0:nq], rd[j][:, 0:nq], ALU.mult, [acc, rd[j]], [o_])
                            r0 = 256 + h * 64
                            tq = t0 + s0 + q0
                            k.dma(k.sp, D["O"][r0 // 128, r0 % 128:r0 % 128 + 64, tq:tq + nq], o_[:, 0:nq], [o_],
                                  [g.OR[r0 // 128]])
        k.barrier()


def phase_diff(g, l):
    k, nc, D = g.k, g.nc, g.D
    scale = 32.0 ** -0.5
    lam_init = 0.8 - 0.6 * float(np.exp(-0.3 * l))
    with ExitStack() as es:
        g.ropeD = [k.sb(es, "ropeD%d" % i, [128, 2048], F32) for i in range(2)]
        for i in range(2):
            k.dma(k.sp, g.ropeD[i][:], D["ropeD"][i], (), [g.ropeD[i]])
        gn = k.sb(es, "gn", [128, 8], F32)
        k.dma(k.sp, gn[:], D["gains"][l], (), [gn])
        dl = k.sb(es, "dl", [32, 4], F32)
        k.dma(k.sp, dl[:], D["dlam"][l], (), [dl])
        pr2 = k.sb(es, "pr2", [32, 2], F32)
        k.tt(pr2[:, 0:1], dl[:, 0:1], dl[:, 1:2], ALU.mult, [dl], [pr2])
        k.tt(pr2[:, 1:2], dl[:, 2:3], dl[:, 3:4], ALU.mult, [dl], [pr2])
        pl = g.PS[2]
        k.mm(pl[0:64, 0:2], g.onesf[:, :], pr2[:, :], True, True, [pr2, g.onesf], [pl])
        lam = k.sb(es, "lam", [64, 4], F32)
        k.activation(lam[:, 0:2], pl[0:64, 0:2], AF.Exp, [pl], [lam])
        k.stt(lam[:, 2:3], lam[:, 1:2], -lam_init, lam[:, 0:1], ALU.add, ALU.subtract, [lam], [lam])
        k.ts(lam[:, 3:4], gn[0:64, 7:8], 1.0 - lam_init, None, ALU.mult, reads=[gn], writes=[lam])
        QT = [k.sb(es, "dQ%d" % i, [128, 2048], BF16) for i in range(2)]
        KT = [k.sb(es, "dK%d" % i, [128, 2560], BF16) for i in range(2)]
        QP = [k.sb(es, "dQP%d" % i, [128, 2048], BF16) for i in range(8)]
        for i in range(8):
            k.memset(QP[i][:], 0.0, [QP[i]], E=(k.pool if i % 2 else k.dve))
        VA = k.sb(es, "dVA", [128, 20 * 512], BF16)
        k.memset(VA[:], 1.0, [VA])
        W = {"sq": k.sb(es, "wsq", [128, 512], BF16), "rs": k.sb(es, "wrs", [128, 512], F32),
             "t1": k.sb(es, "wt1", [128, 512], F32), "t2": k.sb(es, "wt2", [128, 512], BF16),
             "pn": g.PS[3], "pr": g.PS[3], "pt": [k.sb(es, "wpt%d" % i, [128, 512], BF16) for i in range(8)]}
        xin128 = [k.sb(es, "dxin128%d" % i, [128, 512], F32) for i in range(2)]
        WB = {"sq": k.sb(es, "wsqB", [128, 512], BF16), "rs": k.sb(es, "wrsB", [128, 512], F32),
              "t1": k.sb(es, "wt1B", [128, 512], F32), "t2": k.sb(es, "wt2B", [128, 512], BF16),
              "pn": g.PS[2], "pr": g.PS[2]}
        Ws = [W, WB]
        kf = k.sb(es, "dkf", [128, 512], F32)
        vin = [k.sb(es, "dvin%d" % i, [128, 256], F32) for i in range(2)]
        rd = [k.sb(es, "rd%d" % i, [64, 512], F32) for i in range(4)]
        oo = [k.sb(es, "doo%d" % i, [64, 512], F32) for i in range(4)]
        ob = [k.sb(es, "dob%d" % i, [64, 512], BF16) for i in range(2)]
        oi = 0
        xi = 0
        pti = 0
        for (t0, L, ctx, seqs) in GROUPS:
            koff = 512 if ctx else 0
            rope = (g.rotD, g.ropeD[0], g.ropeD[1]) if ctx else None
            if ctx:
                for c2 in range(2):
                    x_ = xin128[xi % 2]
                    xi += 1
                    k.dma(k.sp, x_[:], D["cdkT"][l, c2 * 128:(c2 + 1) * 128, :], (), [x_])
                    k.copy(KT[c2][:, 0:512], x_[:], [x_], [KT[c2]])
                for tt in range(4):
                    v_ = vin[tt % 2]
                    k.dma(k.sp, v_[:], D["cdv"][l, tt * 128:(tt + 1) * 128, :], (), [v_])
                    k.copy(VA[:, tt * 512:(tt + 1) * 512].rearrange("p (h c) -> p h c", h=4)[:, :, 0:64],
                           v_[:].rearrange("p (h c) -> p h c", h=4), [v_], [VA])
            for b0 in range(0, L, 512):
                n = min(512, L - b0)
                tg = t0 + b0
                def dchain(c2, isk):
                    x_ = xin128[isk]
                    Wx = Ws[isk]
                    ap_, rr = pj_rows(g, (R_DK if isk else R_DQ) + c2 * 128, 128, tg, n)
                    k.dma(k.sp, x_[:, 0:n], ap_, rr, [x_])
                    yield
                    dst = KT[c2] if isk else QT[c2]
                    off = (koff + b0) if isk else b0
                    if isk and not ctx:
                        yield from gnr_gen(g, Wx, x_[:, 0:n], [x_], 128, n, gn[:, 6:7], None, dst[:, off:off + n], [dst],
                                           f32out=kf[:, 0:n], f32res=[kf], onesT=g.bd32, gres=[gn])
                        k.dma(k.sp, D["dkT"][l, c2 * 128:(c2 + 1) * 128, tg:tg + n], kf[:, 0:n], [kf], [])
                        yield
                    else:
                        yield from gnr_gen(g, Wx, x_[:, 0:n], [x_], 128, n, gn[:, 5 + isk:6 + isk], rope, dst[:, off:off + n], [dst],
                                           rope_off=b0, onesT=g.bd32, gres=[gn])
                    if not isk:
                        for j in range(4):
                            k.copy(QP[c2 * 4 + j][32 * j:32 * j + 32, b0:b0 + n], QT[c2][32 * j:32 * j + 32, b0:b0 + n],
                                   [QT[c2]], [QP[c2 * 4 + j]], E=(k.pool if j % 2 else k.act))
                            yield

                for c2 in range(2):
                    run_interleaved([dchain(c2, 0), dchain(c2, 1)])
                for tt in range(n // 128):
                    v_ = vin[tt % 2]
                    k.dma(k.sp, v_[:], D["PT"][tg + tt * 128:tg + (tt + 1) * 128, 768:1024], [g.PTR], [v_])
                    kt = (koff + b0) // 128 + tt
                    k.copy(VA[:, kt * 512:(kt + 1) * 512].rearrange("p (h c) -> p h c", h=4)[:, :, 0:64],
                           v_[:].rearrange("p (h c) -> p h c", h=4), [v_], [VA])
            if not ctx:
                ap_, rr = pj_rows(g, R_DV, 256, t0, L)
                k.dma(k.sp, D["dvT"][l, :, t0:t0 + L], ap_, rr, [])
            for (s0, Ls) in seqs:
              nkt = (Ls + koff) // 128
              kt0 = 0 if ctx else s0 // 128
              for h in range(4):
                c2 = h // 2
                rb = 64 * (h % 2)
                for q0 in range(0, Ls, 512):
                    nq = min(512, Ls - q0)
                    qc = s0 + q0
                    base = 4 + 2 * (oi % 2)
                    accs = [g.PS[base], g.PS[base + 1]]

                    def emit_sc(kt):
                        kc = kt0 + kt
                        for m in range(2):
                            sc = g.PS[(kt % 2) * 2 + m]
                            qp = QP[h * 2 + m]
                            k.mm(sc[:, 0:nq], KT[c2][:, kc * 128:(kc + 1) * 128], qp[:, qc:qc + nq],
                                 True, True, [KT[c2], qp], [sc])

                    emit_sc(0)
                    for kt in range(nkt):
                        if kt + 1 < nkt:
                            emit_sc(kt + 1)
                        kc = kt0 + kt
                        for m in range(2):
                            sc = g.PS[(kt % 2) * 2 + m]
                            pt = W["pt"][pti % 8]
                            pti += 1
                            k.activation(pt[:, 0:nq], sc[:, 0:nq], AF.Exp, [sc], [pt], scale=scale)
                            k.mm(accs[m][:, 0:nq], VA[:, kc * 512 + h * 128: kc * 512 + (h + 1) * 128], pt[:, 0:nq],
                                 kt == 0, kt == nkt - 1, [VA, pt], [accs[m]])
                    for m in range(2):
                        k.op(k.dve, lambda: nc.vector.reciprocal(out=rd[m][:, 0:nq], in_=accs[m][64:128, 0:nq]), [accs[m]], [rd[m]])
                        k.tt(oo[m][:, 0:nq], accs[m][0:64, 0:nq], rd[m][:, 0:nq], ALU.mult, [accs[m], rd[m]], [oo[m]])
                    o0, o1 = oo[0], oo[1]
                    o_ = ob[oi % 2]
                    oi += 1
                    k.stt(o0[:, 0:nq], o1[:, 0:nq], lam[:, 2:3], o0[:, 0:nq], ALU.mult, ALU.add, [o0, o1, lam], [o0])
                    group_norm_rope(g, W, o0[:, 0:nq], [o0], 64, nq, lam[:, 3:4], None, o_[:, 0:nq], [o_], gres=[lam])
                    r0 = 768 + h * 64
                    tq = t0 + s0 + q0
                    k.dma(k.sp, D["O"][r0 // 128, r0 % 128:r0 % 128 + 64, tq:tq + nq], o_[:, 0:nq], [o_],
                          [g.OR[r0 // 128]])
        k.barrier()


def phase_zero(g, chunks):
    k, D = g.k, g.D
    with ExitStack() as es:
        z = k.sb(es, "z", [128, NT], BF16)
        k.memset(z[:], 0.0, [z])
        for kk in chunks:
            k.dma(k.sp, D["O"][kk], z[:], [z], [g.OR[kk]])
        k.barrier()


def phase_mixers(g, l, mixers):
    if "a" in mixers:
        phase_hgrn(g, l)
    else:
        phase_zero(g, [0, 1])
    if "b" in mixers:
        phase_mla(g, l)
    else:
        phase_zero(g, [2, 3])
    if "c" in mixers:
        phase_hyena(g, l)
    else:
        phase_zero(g, [4, 5])
    if "d" in mixers:
        phase_diff(g, l)
    else:
        phase_zero(g, [6, 7])


def hgrn_setup(g, es):
    k, nc, D = g.k, g.nc, g.D
    lbl = k.sb(es, "lbl", [64, 32], F32)
    k.dma(k.sp, lbl[:], D["lbl"], (), [lbl])
    e = k.sb(es, "lbe", [64, 32], F32)
    k.activation(e[:], lbl[:], AF.Exp, [lbl], [e])
    sm = k.sb(es, "lbs", [64, 8], F32)
    e3 = e[:].rearrange("p (a l) -> p a l", l=4)
    k.op(k.dve, lambda: nc.vector.tensor_reduce(out=sm[:], in_=e3, axis=AX.X, op=ALU.add), [e], [sm])
    k.op(k.dve, lambda: nc.vector.reciprocal(out=sm[:], in_=sm[:]), [sm], [sm])
    k.tt(e3, e3, sm[:].unsqueeze(2).to_broadcast([64, 8, 4]), ALU.mult, [e, sm], [e])
    g.oml = k.sb(es, "oml", [64, 32], F32)
    o3 = g.oml[:].rearrange("p (a l) -> p a l", l=4)
    k.memset(g.oml[:], 1.0, [g.oml])
    for l in range(1, 4):
        k.tt(o3[:, :, l], o3[:, :, l - 1], e3[:, :, l], ALU.subtract, [g.oml, e], [g.oml])
    g.mask4 = [k.sb(es, "mask4%d" % i, [128, 256], BF16) for i in range(2)]
    g.maskbd = [k.sb(es, "maskbd%d" % i, [128, 128], F32) for i in range(2)]
    for i in range(2):
        k.dma(k.pool, g.mask4[i][:], D["mask4"][i], (), [g.mask4[i]])
        k.dma(k.sp, g.maskbd[i][:], D["maskbd"][i], (), [g.maskbd[i]])
    g.ident = k.sb(es, "ident", [64, 64], BF16)
    k.dma(k.pool, g.ident[:], D["ident"], (), [g.ident])
    g.rmask = k.sb(es, "rmask", [128, 2048], F32)
    k.memset(g.rmask[:], 1.0, [g.rmask])
    k.memset(g.rmask[:].rearrange("p (a b) -> p a b", b=32)[:, :, 0:1], 0.0, [g.rmask])
    g.one_t = k.sb(es, "one_t", [128, 1], F32)
    k.memset(g.one_t[:], 1.0, [g.one_t])


def phase_hgrn(g, l):
    k, nc, D = g.k, g.nc, g.D
    for (t0, nseq, Ls, ctx) in ((0, 4, 256, False), (1024, 1, 2048, True)):
        L = nseq * Ls
        nch = L // 32
        nchs = Ls // 32
        ntl = L // 128
        nsl = nseq * (nchs + 1)

        def slot_of(i):
            return (i // nchs) * (nchs + 1) + 1 + (i % nchs)

        with ExitStack() as es:
            f32t = lambda nm: k.sb(es, nm, [128, L], F32)
            af, og, qs, kk, cum, e1, of = [f32t(n_) for n_ in ("af", "og", "qs", "kk", "cum", "e1", "of")]
            aq = e1
            gl = af
            qt, kt, k2 = [k.sb(es, n_, [128, L], BF16) for n_ in ("qt", "kt", "k2")]
            el = k.sb(es, "el", [128, nch], F32)
            k2T = k.sb(es, "k2T", [128, ntl * 128], BF16)
            vin = [k.sb(es, "hvin%d" % i, [128, 256], F32) for i in range(2)]
            vt = k.sb(es, "vt", [128, ntl * 256], BF16)
            V4 = [k.sb(es, "V4%d" % i, [128, 256], BF16) for i in range(4)]
            U3, A3, S3 = [k.sb(es, n_, [128, 64 * nsl], F32) for n_ in ("U3", "A3", "S3")]
            S16 = k.sb(es, "S16", [128, 64 * nsl], BF16)
            AT = [k.sb(es, "AT%d" % i, [128, 128], BF16) for i in range(4)]
            s0t = k.sb(es, "s0t", [128, 64], F32)
            fin = [k.sb(es, "fin%d" % i, [128, 64], F32) for i in range(2)]
            gon = k.sb(es, "gon", [128, 1], F32)
            k.dma(k.sp, gon[0:64, :], D["gon"][l], (), [gon])
            k.dma(k.sp, gon[64:128, :], D["gon"][l], (), [gon])
            omlp = k.sb(es, "omlp", [128, 4], F32)
            for hp in range(2):
                for d in range(2):
                    for j in range(2):
                        col = (d * 4 + hp * 2 + j) * 4 + l
                        k.copy(omlp[64 * j:64 * j + 64, hp * 2 + d:hp * 2 + d + 1], g.oml[:, col:col + 1], [g.oml], [omlp])
            obf = [k.sb(es, "hob%d" % i, [128, 512], BF16) for i in range(2)]
            W = {"sq": k.sb(es, "wsq", [128, 512], BF16), "rs": k.sb(es, "wrs", [128, 512], F32),
                 "t1": k.sb(es, "wt1", [128, 512], F32), "t2": k.sb(es, "wt2", [128, 512], BF16),
                 "pn": g.PS[2], "pr": g.PS[3]}
            v3 = lambda t_: t_[:].rearrange("p (d s) -> p d s", s=nsl)
            c3 = lambda t_: t_[:].rearrange("p (c j) -> p c j", j=32)
            cnt = 0
            for tt in range(ntl):
                v_ = vin[tt % 2]
                k.dma(k.sp, v_[:], D["PT"][t0 + tt * 128:t0 + (tt + 1) * 128, 0:256], [g.PTR], [v_])
                k.copy(vt[:, tt * 256:(tt + 1) * 256], v_[:], [v_], [vt], E=(k.act if tt % 2 else k.dve))
            for hp in range(2):
                ap_, rr = pj_rows(g, R_AQ + hp * 128, 128, t0, L)
                k.dma(k.sp, aq[:], ap_, rr, [aq])
                ap_, rr = pj_rows(g, R_AOG + hp * 128, 128, t0, L)
                k.dma(k.sp, og[:], ap_, rr, [og])
                k.activation(qs[:], aq[:], AF.Silu, [aq], [qs])
                for d in range(2):
                    ap_, rr = pj_rows(g, (R_AFF if d == 0 else R_AFB) + hp * 128, 128, t0, L)
                    k.dma(k.sp, af[:], ap_, rr, [af])
                    k.activation(kk[:], af[:], AF.Sigmoid, [af], [kk], scale=-1.0)
                    k.ts(kk[:], kk[:], omlp[:, hp * 2 + d:hp * 2 + d + 1], MAXKEY, ALU.mult, ALU.min, reads=[kk, omlp], writes=[kk])
                    k.activation(gl[:], kk[:], AF.Ln, [kk], [gl], scale=-1.0, bias=g.one_t[:, 0:1])
                    if d == 0:
                        k.op(k.dve, lambda: nc.vector.tensor_tensor_scan(out=cum[:], data0=g.rmask[:, 0:L], data1=gl[:],
                             initial=0.0, op0=ALU.mult, op1=ALU.add), [gl, g.rmask], [cum])
                        totv = c3(cum)[:, :, 31]
                    else:
                        k.op(k.dve, lambda: nc.vector.tensor_tensor_scan(out=cum[:, ::-1], data0=g.rmask[:, 0:L],
                             data1=gl[:, ::-1], initial=0.0, op0=ALU.mult, op1=ALU.add), [gl, g.rmask], [cum])
                        totv = c3(cum)[:, :, 0]
                    k.activation(e1[:], cum[:], AF.Exp, [cum], [e1])
                    k.stt(qt[:], e1[:], 0.125, qs[:], ALU.mult, ALU.mult, [e1, qs], [qt])
                    k.activation(e1[:], cum[:], AF.Exp, [cum], [e1], scale=-1.0)
                    k.tt(kt[:], kk[:], e1[:], ALU.mult, [kk, e1], [kt], E=k.pool)
                    k.activation(el[:], totv, AF.Exp, [cum], [el])
                    k.tt(c3(e1), totv.unsqueeze(2).to_broadcast([128, nch, 32]), c3(cum), ALU.subtract, [cum], [e1])
                    k.activation(e1[:], e1[:], AF.Exp, [e1], [e1])
                    k.tt(k2[:], kk[:], e1[:], ALU.mult, [kk, e1], [k2], E=k.pool)
                    for tt in range(ntl):
                        pT = g.PS[tt % 2]
                        k.mm(pT[:, 0:128], k2[:, tt * 128:(tt + 1) * 128], g.ident128[:, :], True, True, [k2, g.ident128], [pT])
                        k.copy(k2T[:, tt * 128:(tt + 1) * 128], pT[:, 0:128], [pT], [k2T], E=(k.act if tt % 2 else k.dve))
                    init_sl = slice(0, nsl, nchs + 1)
                    if ctx:
                        for j in range(2):
                            k.dma(k.sp, s0t[64 * j:64 * j + 64, :], D["s0"][l, d, hp * 2 + j], (), [s0t])
                        k.copy(v3(U3)[:, :, 0], s0t[:], [s0t], [U3])
                    else:
                        k.memset(v3(U3)[:, :, init_sl], 0.0, [U3])
                    k.memset(v3(A3)[:, :, init_sl], 0.0, [A3])
                    elv = el[:] if d == 0 else el[:, ::-1]
                    for sp_ in range(nseq):
                        s_lo = sp_ * (nchs + 1) + 1
                        k.copy(v3(A3)[:, :, s_lo:s_lo + nchs],
                               elv[:, sp_ * nchs:(sp_ + 1) * nchs].unsqueeze(1).to_broadcast([128, 64, nchs]), [el], [A3],
                               E=k.pool)
                    for tt in range(ntl):
                        i0 = 4 * tt if d == 0 else nch - 4 - 4 * tt
                        s_lo = slot_of(i0)
                        for j in range(2):
                            h = hp * 2 + j
                            v4 = V4[(tt * 2 + j) % 4]
                            k.tt(v4[:].rearrange("p (j c) -> p j c", j=4),
                                 vt[:, tt * 256 + h * 64: tt * 256 + (h + 1) * 64].unsqueeze(1).to_broadcast([128, 4, 64]),
                                 g.mask4[d][:].rearrange("p (j c) -> p j c", j=4), ALU.mult, [vt, g.mask4[d]], [v4],
                                 E=k.pool)
                            pu = g.PS[4 + j]
                            k.mm(pu[:, 0:256], k2T[:, tt * 128:(tt + 1) * 128], v4[:], True, True, [k2T, v4], [pu])
                            k.copy(v3(U3)[64 * j:64 * j + 64, :, s_lo:s_lo + 4].rearrange("p d s -> p s d"),
                                   pu[64 * j:64 * j + 64, 0:256].rearrange("p (s d) -> p s d", s=4), [pu], [U3],
                                   E=(k.act if j else k.dve))
                    k.op(k.dve, lambda: nc.vector.tensor_tensor_scan(out=S3[:], data0=A3[:], data1=U3[:], initial=0.0,
                         op0=ALU.mult, op1=ALU.add), [A3, U3], [S3])
                    k.copy(S16[:], S3[:], [S3], [S16], E=k.act)
                    if not ctx:
                        for sp_ in range(nseq):
                            si = sp_ if d == 0 else nseq - 1 - sp_
                            f_ = fin[cnt % 2]
                            cnt += 1
                            k.copy(f_[:], v3(S3)[:, :, sp_ * (nchs + 1) + nchs], [S3], [f_])
                            for j in range(2):
                                k.dma(k.sp, D["hg"][l, si, d, hp * 2 + j], f_[64 * j:64 * j + 64, :], [f_], [])
                    for tt in range(ntl):
                        for j in range(2):
                            h = hp * 2 + j
                            hb = 64 * j
                            pa = g.PS[j]
                            at = AT[(tt * 2 + j) % 4]
                            k.mm(pa[:, 0:128], kt[hb:hb + 64, tt * 128:(tt + 1) * 128], qt[hb:hb + 64, tt * 128:(tt + 1) * 128],
                                 True, True, [kt, qt], [pa])
                            k.tt(at[:], pa[:, 0:128], g.maskbd[d][:], ALU.mult, [pa, g.maskbd[d]], [at])
                        for j in range(2):
                            h = hp * 2 + j
                            hb = 64 * j
                            at = AT[(tt * 2 + j) % 4]
                            po = g.PS[6 + j]
                            cb = (tt % 4) * 128
                            k.mm(po[0:64, cb:cb + 128], vt[:, tt * 256 + h * 64: tt * 256 + (h + 1) * 64], at[:], True, False,
                                 [vt, at], [po])
                            for jj in range(4):
                                c = 4 * tt + jj
                                i = c if d == 0 else nch - 1 - c
                                k.mm(po[0:64, cb + 32 * jj:cb + 32 * jj + 32], v3(S16)[hb:hb + 64, :, slot_of(i) - 1],
                                     qt[hb:hb + 64, c * 32:(c + 1) * 32], False, jj == 3, [S16, qt], [po])
                            if tt % 4 == 3 or tt == ntl - 1:
                                b0 = (tt // 4) * 512
                                n = cb + 128
                                if d == 0:
                                    k.copy(of[hb:hb + 64, b0:b0 + n], po[0:64, 0:n], [po], [of], E=k.act)
                                else:
                                    k.tt(of[hb:hb + 64, b0:b0 + n], of[hb:hb + 64, b0:b0 + n], po[0:64, 0:n], ALU.add, [of, po], [of])
                k.activation(og[:], og[:], AF.Silu, [og], [og])
                for b0 in range(0, L, 512):
                    n = min(512, L - b0)
                    o_ = obf[(b0 // 512) % 2]
                    group_norm_rope(g, W, of[:, b0:b0 + n], [of], 128, n, gon[:, 0:1], None, W["t2"][:, 0:n], [W["t2"]],
                                    f32out=e1[:, b0:b0 + n], f32res=[e1], onesT=g.bd64, gres=[gon])
                    k.tt(o_[:, 0:n], e1[:, b0:b0 + n], og[:, b0:b0 + n], ALU.mult, [e1, og], [o_])
                    k.dma(k.sp, D["O"][hp, :, t0 + b0:t0 + b0 + n], o_[:, 0:n], [o_], [g.OR[hp]])
            k.barrier()


HYN = ((256, "P", SEQS[0:4]), (2048, "S", SEQS[4:5]))
TWO_PI = 6.283185307179586


def hyena_input_shapes():
    s = {"hw1": [DEPTH, 17, 64], "hb1": [DEPTH, 64, 1], "hw2": [DEPTH, 64, 64], "hb2": [DEPTH, 64, 1],
         "hw3": [DEPTH, 64, 1024], "hld": [DEPTH, 128, 1024], "hsT": [DEPTH, 128, 18], "hbT": [DEPTH, 128, 4],
         "hmsk": [128, 4], "ident128": [128, 128]}
    for n, sfx, _ in HYN:
        tbw = min(512, n)
        s["feats" + sfx] = [17, n]
        s["negtn" + sfx] = [128, n // 128]
        s["Fr" + sfx] = [2 * n // 128, 128, n]
        s["Gr" + sfx] = [n // tbw, 128, (2 * n // 128) * tbw]
        s["Fq" + sfx] = [2, 128, n]
    return s


def hyena_scratch():
    return [("HS" + sfx, [n // 128, 3, 128, 512], F32) for n, sfx, _ in HYN]


def hyena_host_shared(inp):
    f = np.float32
    S = {}
    S["hw1"] = np.ascontiguousarray(inp["hy_w1"], dtype=f)
    S["hb1"] = np.ascontiguousarray(inp["hy_b1"].reshape(DEPTH, 64, 1), dtype=f)
    S["hw2"] = np.ascontiguousarray(inp["hy_w2"], dtype=f)
    S["hb2"] = np.ascontiguousarray(inp["hy_b2"].reshape(DEPTH, 64, 1), dtype=f)
    S["hw3"] = np.ascontiguousarray(inp["hy_w3"], dtype=f)
    S["hld"] = np.ascontiguousarray(np.broadcast_to(inp["hy_log_decay"].reshape(DEPTH, 1, 1024), (DEPTH, 128, 1024)), dtype=f)
    hs = inp["hy_short"].reshape(DEPTH, 3, 6, 128)
    S["hsT"] = np.ascontiguousarray(hs.transpose(0, 3, 2, 1).reshape(DEPTH, 128, 18), dtype=f)
    hb = inp["hy_bias"].reshape(DEPTH, 2, 2, 128)
    S["hbT"] = np.ascontiguousarray(hb.transpose(0, 3, 1, 2).reshape(DEPTH, 128, 4), dtype=f)
    msk = np.ones((128, 4), f)
    msk[0, 1] = 0.0
    msk[:, 2] = 0.0
    msk[0, 2] = 1.0
    msk[:, 3] = -1.0
    S["hmsk"] = msk
    S["ident128"] = np.eye(128, dtype=f)
    for n, sfx, _ in HYN:
        tn = (np.arange(n, dtype=f) / f(n)).astype(f)
        ang = (f(2.0 * np.pi) * tn[:, None] * np.arange(1, 9, dtype=f)).astype(f)
        feats = np.concatenate([tn[:, None], np.cos(ang), np.sin(ang)], axis=-1).astype(f)
        S["feats" + sfx] = np.ascontiguousarray(feats.T)
        S["negtn" + sfx] = np.ascontiguousarray((-tn).reshape(n // 128, 128).T)
        t = np.arange(n, dtype=np.float64)[:, None]
        fr = np.arange(n, dtype=np.float64)[None, :]
        th = np.pi * t * fr / n
        Fm = np.concatenate([np.cos(th), -np.sin(th)], axis=1)
        Fm[:, n] = (-1.0) ** np.arange(n)
        Fr = Fm.reshape(n // 128, 128, 2 * n // 128, 128).transpose(2, 1, 0, 3).reshape(2 * n // 128, 128, n)
        S["Fr" + sfx] = np.ascontiguousarray(Fr.astype(f).astype(ml_dtypes.bfloat16))
        fq = np.zeros((2, 128, n // 128, 128), np.float64)
        fq[0] = Fr[n // 128].reshape(128, n // 128, 128)
        fq[1, :, :, 0] = fq[0, :, :, 0]
        fq[0, :, :, 0] = 0.0
        S["Fq" + sfx] = np.ascontiguousarray(fq.reshape(2, 128, n).astype(f).astype(ml_dtypes.bfloat16))
        Gm = np.concatenate([np.cos(th.T), -np.sin(th.T)], axis=0) * (2.0 / (2 * n))
        Gm[0, :] = 1.0 / (2 * n)
        Gm[n, :] = ((-1.0) ** np.arange(n)) / (2 * n)
        tbw = min(512, n)
        Gr = Gm.reshape(2 * n // 128, 128, n // tbw, tbw).transpose(2, 1, 0, 3).reshape(n // tbw, 128, (2 * n // 128) * tbw)
        S["Gr" + sfx] = np.ascontiguousarray(Gr.astype(f).astype(ml_dtypes.bfloat16))
    return S


def hyena_setup(g, es):
    k, D = g.k, g.D
    g.hmsk = k.sb(es, "hmsk", [128, 4], F32)
    k.dma(k.sp, g.hmsk[:], D["hmsk"], (), [g.hmsk])
    g.ident128 = k.sb(es, "ident128", [128, 128], BF16)
    k.dma(k.pool, g.ident128[:], D["ident128"], (), [g.ident128])
    g.onesF = k.sb(es, "onesF", [128, 128], F32)
    k.memset(g.onesF[:], 1.0, [g.onesF])
    g.HSR = {sfx: [Res() for _ in range(n // 128)] for n, sfx, _ in HYN}


def hyena_filters(g, l, n, sfx):
    k, nc, D = g.k, g.nc, g.D
    ntl = n // 128
    npair = n // 128
    nfc = 2 * npair
    with ExitStack() as es:
        w1 = k.sb(es, "hw1", [17, 64], F32)
        b1 = k.sb(es, "hb1", [64, 1], F32)
        w2 = k.sb(es, "hw2", [64, 64], F32)
        b2 = k.sb(es, "hb2", [64, 1], F32)
        w3 = k.sb(es, "hw3", [64, 1024], F32)
        eld = k.sb(es, "eld", [128, 1024], F32)
        ft = k.sb(es, "feat", [17, n], F32)
        ntn = k.sb(es, "ntn", [128, ntl], F32)
        for t_, nm in ((w1, "hw1"), (b1, "hb1"), (w2, "hw2"), (b2, "hb2"), (w3, "hw3"), (eld, "hld")):
            k.dma(k.sp, t_[:], D[nm][l], (), [t_])
        k.dma(k.sp, ft[:], D["feats" + sfx], (), [ft])
        k.dma(k.sp, ntn[:], D["negtn" + sfx], (), [ntn])
        k.activation(eld[:], eld[:], AF.Exp, [eld], [eld])
        h1 = k.sb(es, "h1", [64, n], F32)
        h2 = k.sb(es, "h2", [64, n], F32)
        y = k.sb(es, "hy", [64, 512], F32)
        kq = k.sb(es, "hkq", [64, 512], F32)
        ti = k.sb(es, "hti", [64, 512], I32)
        FBS = k.sb(es, "FBS", [128, ntl * 512], BF16)
        FBD = k.sb(es, "FBD", [128, ntl * 512], BF16)
        dec = k.sb(es, "dec", [128, 512], F32)
        fls = [k.sb(es, "fl%d" % i, [128, 512], F32) for i in range(2)]
        ab = k.sb(es, "ab", [128, 512], F32)
        rn = k.sb(es, "rn", [128, 512], F32)
        ReH = k.sb(es, "ReH", [128, npair * 512], F32)
        Fi = [k.sb(es, "Fi%d" % i, [128, n], BF16) for i in range(3)]
        Bt = [k.sb(es, "Bt%d" % i, [128, 512], F32) for i in range(2)]
        Ct = k.sb(es, "Ct", [128, 512], F32)

        def sin_layer(w, K_, rhs, bias, out):
            for b0 in range(0, n, 512):
                nb = min(512, n - b0)
                ps = g.PS[(b0 // 512) % 2]
                k.mm(ps[0:64, 0:nb], w[0:K_, :], rhs[0:K_, b0:b0 + nb], True, True, [w, rhs], [ps])
                k.ts(y[:, 0:nb], ps[0:64, 0:nb], bias[:, 0:1], None, ALU.add, reads=[ps, bias], writes=[y])
                k.ts(kq[:, 0:nb], y[:, 0:nb], 1.0 / TWO_PI, None, ALU.mult, reads=[y], writes=[kq])
                k.copy(ti[:, 0:nb], kq[:, 0:nb], [kq], [ti])
                k.copy(kq[:, 0:nb], ti[:, 0:nb], [ti], [kq])
                k.stt(y[:, 0:nb], kq[:, 0:nb], -TWO_PI, y[:, 0:nb], ALU.mult, ALU.add, [kq, y], [y])
                k.ts(y[:, 0:nb], y[:, 0:nb], -3.141592, 3.141592, ALU.max, ALU.min, reads=[y], writes=[y])
                k.activation(out[:, b0:b0 + nb], y[:, 0:nb], AF.Sin, [y], [out])

        sin_layer(w1, 17, ft, b1, h1)
        sin_layer(w2, 64, h1, b2, h2)
        accs = [g.PS[4], g.PS[5]]
        for tt in range(ntl):
            for hf in range(2):
                ps = g.PS[2 + hf]
                fl = fls[hf]
                k.mm(ps[:, :], h2[:, tt * 128:(tt + 1) * 128], w3[:, hf * 512:(hf + 1) * 512], True, True, [h2, w3], [ps])
                k.activation(dec[:], eld[:, hf * 512:(hf + 1) * 512], AF.Exp, [eld, ntn], [dec], scale=ntn[:, tt:tt + 1])
                k.tt(fl[:], ps[:, :], dec[:], ALU.mult, [ps, dec], [fl])
                if hf == 1 and tt == 0:
                    k.ts(fl[:], fl[:], g.hmsk[:, 1:2], None, ALU.mult, reads=[fl, g.hmsk], writes=[fl])
                k.activation(ab[:], fl[:], AF.Abs, [fl], [ab])
                k.mm(accs[hf][:, :], g.onesF[:, :], ab[:], tt == 0, tt == ntl - 1, [ab, g.onesF], [accs[hf]])
            k.tt(FBS[:, tt * 512:(tt + 1) * 512], fls[0][:], fls[1][:], ALU.add, [fls[0], fls[1]], [FBS], E=k.pool)
            k.tt(FBD[:, tt * 512:(tt + 1) * 512], fls[0][:], fls[1][:], ALU.subtract, [fls[0], fls[1]], [FBD])
        k.copy(rn[:], accs[0][:, :], [accs[0]], [rn], E=k.act)
        k.tt(rn[:], rn[:], accs[1][:, :], ALU.add, [rn, accs[1]], [rn])
        k.ts(rn[:], rn[:], EPS, None, ALU.add, reads=[rn], writes=[rn])
        k.op(k.dve, lambda: nc.vector.reciprocal(out=rn[:], in_=rn[:]), [rn], [rn])
        Fq = [k.sb(es, "Fq%d" % i, [128, n], BF16) for i in range(2)]
        for i in range(2):
            k.dma(k.sp, Fq[i][:], D["Fq" + sfx][i], (), [Fq[i]])
        for i in range(nfc):
            ip = i % npair
            isim = i >= npair
            ps = g.PS[i % 2]
            if i == npair:
                for kk in range(ntl):
                    k.mm(ps[:, :], Fq[0][:, kk * 128:(kk + 1) * 128], FBD[:, kk * 512:(kk + 1) * 512], kk == 0, False,
                         [Fq[0], FBD], [ps])
                for kk in range(ntl):
                    k.mm(ps[:, :], Fq[1][:, kk * 128:(kk + 1) * 128], FBS[:, kk * 512:(kk + 1) * 512], False, kk == ntl - 1,
                         [Fq[1], FBS], [ps])
            else:
                F_ = Fi[i % 3]
                k.dma(k.sp, F_[:], D["Fr" + sfx][i], (), [F_])
                FB_ = FBD if isim else FBS
                for kk in range(ntl):
                    k.mm(ps[:, :], F_[:, kk * 128:(kk + 1) * 128], FB_[:, kk * 512:(kk + 1) * 512], kk == 0, kk == ntl - 1,
                         [F_, FB_], [ps])
            Re_ = ReH[:, ip * 512:(ip + 1) * 512]
            if not isim:
                k.tt(Re_, ps[:, :], rn[:], ALU.mult, [ps, rn], [ReH])
            else:
                B_ = Bt[ip % 2]
                k.tt(B_[:], ps[:, :], rn[:], ALU.mult, [ps, rn], [B_])
                hs_ = D["HS" + sfx][ip]
                hr = g.HSR[sfx][ip]
                if ip > 0:
                    k.dma(k.pool, hs_[0], Re_, [ReH], [hr])
                    k.dma(k.pool, hs_[1], B_[:], [B_], [hr])
                    k.dma(k.pool, hs_[2], Re_, [ReH], [hr])
                else:
                    k.ts(Ct[:], Re_, g.hmsk[:, 1:2], None, ALU.mult, reads=[ReH, g.hmsk], writes=[Ct])
                    k.stt(Ct[:], B_[:], g.hmsk[:, 2:3], Ct[:], ALU.mult, ALU.add, [B_, Ct, g.hmsk], [Ct])
                    k.ts(B_[:], B_[:], g.hmsk[:, 1:2], None, ALU.mult, reads=[B_, g.hmsk], writes=[B_])
                    k.dma(k.pool, hs_[0], Re_, [ReH], [hr])
                    k.dma(k.pool, hs_[1], B_[:], [B_], [hr])
                    k.dma(k.pool, hs_[2], Ct[:], [Ct], [hr])
        k.barrier()


def hyena_convs(g, l, n, sfx, seqs):
    k, nc, D = g.k, g.nc, g.D
    ntl = n // 128
    npair = n // 128
    nfc = 2 * npair
    tbw = min(512, n)
    ntb = n // tbw
    with ExitStack() as es:
        hs = k.sb(es, "hsT", [128, 18], F32)
        hb = k.sb(es, "hbT", [128, 4], F32)
        k.dma(k.sp, hs[:], D["hsT"][l], (), [hs])
        k.dma(k.sp, hb[:], D["hbT"][l], (), [hb])
        U = k.sb(es, "hU", [128, 2 * n], F32)
        X = k.sb(es, "hX", [128, 2 * n], F32)
        xin = k.sb(es, "hxin", [128, n], F32)
        ubf = k.sb(es, "hubf", [128, 2 * n], BF16)
        utm = k.sb(es, "hutm", [128, ntl * 256], BF16)
        Yt = k.sb(es, "hYt", [128, nfc * 256], BF16)
        Gts = [k.sb(es, "hGt%d" % i, [128, nfc * tbw], BF16) for i in range(2)]
        Fi = [k.sb(es, "hFi%d" % i, [128, n], BF16) for i in range(4)]
        Hs = [k.sb(es, "hHs%d" % i, [128, 768], F32) for i in range(2)]
        vre = k.sb(es, "hvre", [128, 256], F32)
        vim = k.sb(es, "hvim", [128, 256], F32)
        t1 = k.sb(es, "ht1", [128, 256], F32)
        t2 = k.sb(es, "ht2", [128, 256], F32)
        ob = [k.sb(es, "hob%d" % i, [128, 512], BF16) for i in range(2)]

        def short_conv(dst, ch0, t0):
            for cc in range(2):
                ap_, rr = pj_rows(g, R_CV + (ch0 + cc) * 128, 128, t0, n)
                k.dma(k.sp, xin[:], ap_, rr, [xin])
                c3 = (ch0 + cc) * 3
                d_ = dst[:, cc * n:(cc + 1) * n]
                k.ts(d_, xin[:], hs[:, c3 + 1:c3 + 2], None, ALU.mult, reads=[xin, hs], writes=[dst])
                k.stt(d_[:, 1:n], xin[:, 0:n - 1], hs[:, c3:c3 + 1], d_[:, 1:n], ALU.mult, ALU.add, [xin, hs, dst], [dst])
                k.stt(d_[:, 0:n - 1], xin[:, 1:n], hs[:, c3 + 2:c3 + 3], d_[:, 0:n - 1], ALU.mult, ALU.add, [xin, hs, dst], [dst])

        def long_conv(o, t0, last):
            k.copy(ubf[:], U[:], [U], [ubf], E=k.act)
            for tt in range(ntl):
                for cc in range(2):
                    pT = g.PS[(tt * 2 + cc) % 2]
                    k.mm(pT[:, 0:128], ubf[:, cc * n + tt * 128: cc * n + (tt + 1) * 128], g.ident128[:, :], True, True,
                         [ubf, g.ident128], [pT])
                    k.copy(utm[:, tt * 256 + cc * 128: tt * 256 + (cc + 1) * 128], pT[:, 0:128], [pT], [utm],
                           E=(k.act if cc else k.dve))
            for ip in range(npair):
                Fre = Fi[(ip % 2) * 2]
                Fim = Fi[(ip % 2) * 2 + 1]
                k.dma(k.sp, Fre[:], D["Fr" + sfx][ip], (), [Fre])
                k.dma(k.sp, Fim[:], D["Fr" + sfx][npair + ip], (), [Fim])
                H_ = Hs[ip % 2]
                k.dma(k.sp, H_[:].rearrange("p (a c) -> p a c", a=3),
                      D["HS" + sfx][ip].rearrange("a p c -> p a c")[:, :, o * 256:(o + 1) * 256], [g.HSR[sfx][ip]], [H_])
                pre = g.PS[2 + (ip % 2) * 2]
                pim = g.PS[3 + (ip % 2) * 2]
                for kk in range(ntl):
                    k.mm(pre[:, 0:256], Fre[:, kk * 128:(kk + 1) * 128], utm[:, kk * 256:(kk + 1) * 256], kk == 0, kk == ntl - 1,
                         [Fre, utm], [pre])
                for kk in range(ntl):
                    k.mm(pim[:, 0:256], Fim[:, kk * 128:(kk + 1) * 128], utm[:, kk * 256:(kk + 1) * 256], kk == 0, kk == ntl - 1,
                         [Fim, utm], [pim])
                k.copy(vre[:], pre[:, 0:256], [pre], [vre], E=k.act)
                k.copy(vim[:], pim[:, 0:256], [pim], [vim], E=k.act)
                A_, B_, C_ = H_[:, 0:256], H_[:, 256:512], H_[:, 512:768]
                k.tt(t1[:], vre[:], A_, ALU.mult, [vre, H_], [t1])
                k.tt(t2[:], vim[:], B_, ALU.mult, [vim, H_], [t2], E=k.pool)
                k.tt(Yt[:, ip * 256:(ip + 1) * 256], t1[:], t2[:], ALU.subtract, [t1, t2], [Yt])
                k.tt(t1[:], vre[:], B_, ALU.mult, [vre, H_], [t1])
                k.tt(t2[:], vim[:], C_, ALU.mult, [vim, H_], [t2], E=k.pool)
                k.tt(Yt[:, (npair + ip) * 256:(npair + ip + 1) * 256], t1[:], t2[:], ALU.add, [t1, t2], [Yt])
            oi = 0
            for tb in range(ntb):
                Gt = Gts[tb % 2]
                for q in range(4):
                    w_ = nfc * tbw // 4
                    k.dma(k.sp, Gt[:, q * w_:(q + 1) * w_], D["Gr" + sfx][tb][:, q * w_:(q + 1) * w_], (), [Gt])
                for cc in range(2):
                    py = g.PS[6 + cc]
                    for fc in range(nfc):
                        k.mm(py[:, 0:tbw], Yt[:, fc * 256 + cc * 128: fc * 256 + (cc + 1) * 128], Gt[:, fc * tbw:(fc + 1) * tbw],
                             fc == 0, fc == nfc - 1, [Yt, Gt], [py])
                    sl = slice(cc * n + tb * tbw, cc * n + (tb + 1) * tbw)
                    k.stt(U[:, sl], U[:, sl], hb[:, o * 2 + cc:o * 2 + cc + 1], py[:, 0:tbw], ALU.mult, ALU.add, [U, hb, py], [U])
                    if not last:
                        k.tt(U[:, sl], U[:, sl], X[:, sl], ALU.mult, [U, X], [U])
                    else:
                        o_ = ob[oi % 2]
                        oi += 1
                        k.tt(o_[:, 0:tbw], U[:, sl], X[:, sl], ALU.mult, [U, X], [o_])
                        k.dma(k.pool, D["O"][4 + cc, :, t0 + tb * tbw:t0 + (tb + 1) * tbw], o_[:, 0:tbw], [o_], [g.OR[4 + cc]])

        for (t0, _, ctx) in seqs:
            short_conv(U, 0, t0)
            short_conv(X, 2, t0)
            long_conv(0, t0, False)
            short_conv(X, 4, t0)
            long_conv(1, t0, True)
        k.barrier()


def phase_hyena(g, l):
    for n, sfx, seqs in HYN:
        hyena_filters(g, l, n, sfx)
        hyena_convs(g, l, n, sfx, seqs)

def build(depth=DEPTH, mixers=("a", "b", "c", "d"), debug=False):
    nc = bass.Bass("TRN2", target_bir_lowering=False)
    D = {}

    def din(name, shape, dt=F32):
        D[name] = nc.dram_tensor(name, list(shape), dt, kind="ExternalInput").ap()

    def dout(name, shape, dt=F32):
        D[name] = nc.dram_tensor(name, list(shape), dt, kind="ExternalOutput").ap()

    def dscr(name, shape, dt=F32):
        kind = "ExternalOutput" if debug else "Internal"
        D[name] = nc.dram_tensor(name, list(shape), dt, kind=kind).ap()

    for name, shape in input_shapes().items():
        din(name, shape, BF16 if name[:2] in ("Fr", "Gr", "Fq") else F32)
    dout("yT", [NTB, 128, 8, 512])
    dout("mlaT", [DEPTH, 160, NPR])
    dout("dkT", [DEPTH, 256, NPR])
    dout("dvT", [DEPTH, 256, NPR])
    dout("hg", [DEPTH, 4, 2, 4, 64, 64])
    dscr("PJ", [NCC, 128, NT])
    dscr("PT", [NT, 1024])
    dscr("O", [8, 128, NT], BF16)
    for nm, shp, dt in extra_scratch():
        dscr(nm, shp, dt)

    with ExitStack() as es:
        k = K(nc, es)
        g = Ctx()
        g.k, g.nc, g.D = k, nc, D
        g.PS = [k.psum(es, "ps%d" % i, [128, 512]) for i in range(8)]
        g.modv = [k.sb(es, "modv%d" % l, [128, 144], F32) for l in range(DEPTH)]
        g.XR = [[Res() for _ in range(NTB)] for _ in range(8)]
        g.PJR = [Res() for _ in range(NCC)]
        g.PTR = Res()
        g.OR = [Res() for _ in range(8)]
        g.xwritten = set()
        g.ones_ms = k.sb(es, "ones_ms", [128, 128], BF16)
        k.memset(g.ones_ms[:], 1.0 / 1024.0, [g.ones_ms])
        g.eps_t = k.sb(es, "eps_t", [128, 1], F32)
        k.memset(g.eps_t[:], EPS, [g.eps_t])
        setup_consts(g, es)
        phase_mod(g)
        if debug == "mod":
            for l in range(DEPTH):
                k.dma(k.sp, D["PT"][l * 128:(l + 1) * 128, 0:144], g.modv[l][:], [g.modv[l]], [])
            depth = 0
        for l in range(depth):
            phase_ffn(g, l, 0)
            phase_proj(g, l)
            phase_mixers(g, l, mixers)
            phase_out(g, l)
            phase_ffn(g, l, 1)
        k.finish()
        g.stats = (k.n_inst, k.n_wait)
    return nc, g


def input_shapes():
    s = {
        "xT": [NTB, 128, 8, 512], "cT": [128, 16], "w_mod": [DEPTH, 1024, 9216], "bmodT": [DEPTH, 128, 72],
        "normgT": [DEPTH, 128, 24], "wgu": [DEPTH, 2, NJ, 128, 2048], "wd": [DEPTH, 2, 8, 128, NJ * 128],
        "win": [DEPTH, NCC, 128, 1024], "wtm": [DEPTH, 128, 8192], "w_out": [DEPTH, 1024, 1024],
    }
    s.update(mixer_input_shapes())
    return s


def host_shared(inp):
    f = np.float32
    S = {}
    S["w_mod"] = np.ascontiguousarray(inp["w_mod"], dtype=f)
    S["bmodT"] = np.ascontiguousarray(inp["b_mod"].reshape(DEPTH, 72, 128).transpose(0, 2, 1), dtype=f)
    S["normgT"] = np.ascontiguousarray(inp["norm_g"].reshape(DEPTH, 24, 128).transpose(0, 2, 1), dtype=f)
    wgu = inp["ffn_w_gu"].reshape(DEPTH, 2, 8, 128, 2, NJ, 128)
    S["wgu"] = np.ascontiguousarray(wgu.transpose(0, 1, 5, 3, 4, 2, 6).reshape(DEPTH, 2, NJ, 128, 2048), dtype=f)
    wd = inp["ffn_w_down"].reshape(DEPTH, 2, NJ, 128, 8, 128)
    S["wd"] = np.ascontiguousarray(wd.transpose(0, 1, 4, 3, 2, 5).reshape(DEPTH, 2, 8, 128, NJ * 128), dtype=f)
    win = np.zeros((DEPTH, 1024, NCC * 128), f)
    win[:, :, :INW] = inp["w_in"]
    win = win.reshape(DEPTH, 8, 128, NCC, 128)
    S["win"] = np.ascontiguousarray(win.transpose(0, 3, 2, 1, 4).reshape(DEPTH, NCC, 128, 1024))
    wi = inp["w_in"]
    tm = np.concatenate([wi[:, :, R_AI:R_AI + 256], wi[:, :, R_AFF:R_AFF + 256], wi[:, :, R_AFB:R_AFB + 256],
                         wi[:, :, R_DV:R_DV + 256]], axis=2)
    tm = tm.reshape(DEPTH, 8, 128, 1024).transpose(0, 2, 1, 3)
    S["wtm"] = np.ascontiguousarray(tm.reshape(DEPTH, 128, 8192), dtype=f)
    S["w_out"] = np.ascontiguousarray(inp["w_out"], dtype=f)
    S.update(mixer_host_shared(inp))
    return S


def host_core(inp, core):
    f = np.float32
    C = {}
    xp = inp["x_prompt"][core * 4:(core + 1) * 4].reshape(NPR, 1024)
    xs = inp["x_sample"][core]
    x = np.concatenate([xp, xs], axis=0)
    C["xT"] = np.ascontiguousarray(x.T.reshape(8, 128, NTB, 512).transpose(2, 1, 0, 3), dtype=f)
    cc = np.stack([inp["c_ctx"], inp["c"][core]], axis=1)
    C["cT"] = np.ascontiguousarray(cc.reshape(8, 128, 2).transpose(1, 0, 2).reshape(128, 16), dtype=f)
    C.update(mixer_host_core(inp, core))
    return C


_CACHE = {}


def kernel(**inp):
    inp = {k_: np.asarray(v) for k_, v in inp.items()}
    if "nc" not in _CACHE:
        _CACHE["nc"] = build()[0]
    nc = _CACHE["nc"]
    S = host_shared(inp)
    in_maps = []
    for core in range(8):
        m = dict(S)
        m.update(host_core(inp, core))
        in_maps.append(m)
    res = run_bass_kernel_spmd(nc, in_maps, core_ids=list(range(8)))
    R = res.results
    f = np.float32
    yp = np.zeros((32, 256, 1024), f)
    ys = np.zeros((8, 2048, 1024), f)
    mla = np.zeros((32, DEPTH, 256, 160), f)
    dk = np.zeros((32, DEPTH, 256, 4, 2, 32), f)
    dv = np.zeros((32, DEPTH, 256, 4, 64), f)
    hg = np.zeros((32, DEPTH, 2, 4, 64, 64), f)
    for core in range(8):
        r = R[core]
        y = np.asarray(r["yT"]).reshape(NTB, 128, 8, 512).transpose(2, 1, 0, 3).reshape(1024, NT).T
        yp[core * 4:(core + 1) * 4] = y[:NPR].reshape(4, 256, 1024)
        ys[core] = y[NPR:]
        m_ = np.asarray(r["mlaT"]).reshape(DEPTH, 160, 4, 256)
        mla[core * 4:(core + 1) * 4] = m_.transpose(2, 0, 3, 1)
        k_ = np.asarray(r["dkT"]).reshape(DEPTH, 256, 4, 256)
        dk[core * 4:(core + 1) * 4] = k_.transpose(2, 0, 3, 1).reshape(4, DEPTH, 256, 4, 2, 32)
        v_ = np.asarray(r["dvT"]).reshape(DEPTH, 256, 4, 256)
        dv[core * 4:(core + 1) * 4] = v_.transpose(2, 0, 3, 1).reshape(4, DEPTH, 256, 4, 64)
        h_ = np.asarray(r["hg"])
        hg[core * 4:(core + 1) * 4] = h_.transpose(1, 0, 2, 3, 4, 5)
    return (yp, ys, mla, dk, dv, hg)
```

```python
from concourse.bass_utils import run_bass_kernel_spmd
import ml_dtypes
import numpy as np
from contextlib import ExitStack
import concourse.bass as bass
import concourse.mybir as mybir

F32 = mybir.dt.float32
BF16 = mybir.dt.bfloat16
I32 = mybir.dt.int32
AF = mybir.ActivationFunctionType
ALU = mybir.AluOpType
AX = mybir.AxisListType


class Res:
    __slots__ = ("w", "rs")

    def __init__(self):
        self.w = None
        self.rs = {}


class T:
    __slots__ = ("t", "res")

    def __init__(self, t, res=None):
        self.t = t
        self.res = res if res is not None else Res()

    def __getitem__(self, idx):
        return self.t[idx]


class Eng:
    def __init__(self, k, eng, name, inorder):
        self.k = k
        self.eng = eng
        self.name = name
        self.inorder = inorder
        self.sem = k.new_sem("s_" + name)
        self.semid = k.semid(self.sem)
        self.cnt = 0
        self.pending = False
        self.seen = {}
        self.dsems = None
        self.dvals = None
        self.di = 0

    def init_dma(self, ns):
        self.dsems = [self.k.new_sem("d_%s_%d" % (self.name, i)) for i in range(ns)]
        self.dids = [self.k.semid(s) for s in self.dsems]
        self.dvals = [0] * ns


class K:
    def __init__(self, nc, es):
        self.nc = nc
        self.es = es
        self._sems = {}
        self._nsem = 0
        self.pe = Eng(self, nc.tensor, "pe", True)
        self.act = Eng(self, nc.scalar, "act", False)
        self.dve = Eng(self, nc.vector, "dve", False)
        self.pool = Eng(self, nc.gpsimd, "pool", False)
        self.sp = Eng(self, nc.sync, "sp", False)
        self.engs = [self.pe, self.act, self.dve, self.pool, self.sp]
        self.sp.init_dma(16)
        self.pool.init_dma(16)
        self.act.init_dma(8)
        self.all_dma_toks = []
        self.same_engine_sync = True
        self.n_inst = 0
        self.n_wait = 0

    def new_sem(self, name):
        s = self.es.enter_context(self.nc.semaphore(name))
        self._nsem += 1
        self._sems[id(s)] = self._nsem
        return s

    def semid(self, s):
        return self._sems[id(s)]

    def need(self, E, tok):
        if tok is None:
            return
        sem, val, sid = tok
        if sid == E.semid and (E.inorder or not self.same_engine_sync):
            return
        if E.seen.get(sid, 0) >= val:
            return
        E.eng.wait_ge(sem, val)
        self.n_wait += 1
        E.seen[sid] = val

    def _deps(self, E, reads, writes):
        for r in reads:
            r = r.res if isinstance(r, T) else r
            self.need(E, r.w)
        for w in writes:
            w = w.res if isinstance(w, T) else w
            self.need(E, w.w)
            for tok in w.rs.values():
                self.need(E, tok)

    def _post(self, tok, reads, writes):
        sid = tok[2]
        for r in reads:
            r = r.res if isinstance(r, T) else r
            old = r.rs.get(sid)
            if old is None or old[1] < tok[1]:
                r.rs[sid] = tok
        for w in writes:
            w = w.res if isinstance(w, T) else w
            w.w = tok
            w.rs = {}

    def op(self, E, fn, reads=(), writes=(), signal=True):
        self._deps(E, reads, writes)
        inst = fn()
        self.n_inst += 1
        tok = (E.sem, E.cnt + 1, E.semid)
        if signal:
            inst.then_inc(E.sem, 1)
            E.cnt += 1
            E.pending = False
        else:
            E.pending = True
        self._post(tok, reads, writes)
        return inst

    def dma(self, Q, out, in_, reads=(), writes=(), **kw):
        self._deps(Q, reads, writes)
        ns = len(Q.dsems)
        slot = Q.di % ns
        Q.di += 1
        sem = Q.dsems[slot]
        sid = Q.dids[slot]
        if Q.dvals[slot] > 0:
            self.need(Q, (sem, Q.dvals[slot], sid))
        inst = Q.eng.dma_start(out=out, in_=in_, **kw)
        inst.then_inc(sem, 16)
        self.n_inst += 1
        Q.dvals[slot] += 16
        tok = (sem, Q.dvals[slot], sid)
        self._post(tok, reads, writes)
        return inst

    def barrier(self):
        toks = []
        for F in self.engs:
            assert not F.pending, F.name
            if F.cnt > 0:
                toks.append((F.sem, F.cnt, F.semid))
            if F.dsems is not None:
                for s, v, i in zip(F.dsems, F.dvals, F.dids):
                    if v > 0:
                        toks.append((s, v, i))
        for E in self.engs:
            for tok in toks:
                if tok[2] == E.semid:
                    if E.inorder:
                        continue
                self.need(E, tok)

    def finish(self):
        self.barrier()

    def sb(self, es, name, shape, dtype):
        self._uid = getattr(self, "_uid", 0) + 1
        name = "sb%d_%s" % (self._uid, name)
        t = es.enter_context(self.nc.sbuf_tensor(name, shape, dtype))
        return T(t)

    def psum(self, es, name, shape, dtype=F32):
        self._uid = getattr(self, "_uid", 0) + 1
        name = "pp%d_%s" % (self._uid, name)
        t = es.enter_context(self.nc.psum_tensor(name, shape, dtype))
        return T(t)

    def mm(self, out, lhsT, rhs, start, stop, reads, writes, signal=None, **kw):
        if signal is None:
            signal = True
        return self.op(self.pe, lambda: self.nc.tensor.matmul(out, lhsT, rhs, start=start, stop=stop, **kw),
                       reads, writes, signal=signal)

    def transpose(self, out, in_, ident, reads, writes, signal=True):
        return self.op(self.pe, lambda: self.nc.tensor.transpose(out, in_, ident), reads, writes, signal=signal)

    def activation(self, out, in_, func, reads, writes, bias=None, scale=None, accum_out=None, E=None):
        E = E or self.act
        kw = {}
        if bias is not None:
            kw["bias"] = bias
        if scale is not None:
            kw["scale"] = scale
        if accum_out is not None:
            kw["accum_out"] = accum_out
        return self.op(E, lambda: E.eng.activation(out=out, in_=in_, func=func, **kw), reads, writes)

    def tt(self, out, in0, in1, op, reads, writes, E=None):
        E = E or self.dve
        return self.op(E, lambda: E.eng.tensor_tensor(out=out, in0=in0, in1=in1, op=op), reads, writes)

    def ts(self, out, in0, s1, s2, op0, op1=None, reads=(), writes=(), E=None, accum_out=None):
        E = E or self.dve
        kw = {}
        if op1 is not None:
            kw["op1"] = op1
        if accum_out is not None:
            kw["accum_out"] = accum_out
        return self.op(E, lambda: E.eng.tensor_scalar(out=out, in0=in0, scalar1=s1, scalar2=s2, op0=op0, **kw),
                       reads, writes)

    def stt(self, out, in0, scalar, in1, op0, op1, reads, writes):
        E = self.dve
        return self.op(E, lambda: E.eng.scalar_tensor_tensor(out=out, in0=in0, scalar=scalar, in1=in1, op0=op0, op1=op1),
                       reads, writes)

    def copy(self, out, in_, reads, writes, E=None):
        E = E or self.dve
        if E is self.act:
            return self.op(E, lambda: E.eng.copy(out=out, in_=in_), reads, writes)
        return self.op(E, lambda: E.eng.tensor_copy(out=out, in_=in_), reads, writes)

    def memset(self, ap, val, writes, E=None):
        E = E or self.dve
        return self.op(E, lambda: E.eng.memset(ap, val), (), writes)

DEPTH = 4
NT = 3072
NPR = 1024
NTB = 6
DFF = 2816
NJ = 22
EPS = 1e-6
INW = 3232
NCC = 26
R_AQ, R_AI, R_AFF, R_AFB, R_AOG = 0, 256, 512, 768, 1024
R_CQ, R_CKV, R_KR = 1280, 1536, 1664
R_CV, R_CX1, R_CX2 = 1696, 1952, 2208
R_DQ, R_DK, R_DV = 2464, 2720, 2976


class Ctx:
    pass


def mvec(g, l, kind, m, cond):
    i = (kind * 8 + m) * 2 + cond
    return g.modv[l][:, i:i + 1]


def xsrc_ap(g, m, tb):
    base = g.D["yT"] if (m, tb) in g.xwritten else g.D["xT"]
    return base[tb, :, m, :]


def phase_mod(g):
    k, nc, D = g.k, g.nc, g.D
    with ExitStack() as es:
        cT = k.sb(es, "cT", [128, 16], F32)
        sc = k.sb(es, "scT", [128, 16], BF16)
        k.dma(k.sp, cT[:], D["cT"], (), [cT])
        k.activation(sc[:], cT[:], AF.Silu, [cT], [sc])
        wm = [k.sb(es, "wm%d" % i, [128, 8 * 512], BF16) for i in range(3)]
        bm = [k.sb(es, "bm%d" % i, [128, 72], F32) for i in range(2)]
        ng = [k.sb(es, "ng%d" % i, [128, 24], F32) for i in range(2)]
        wsrc = D["w_mod"]
        for l in range(DEPTH):
            b_, n_ = bm[l % 2], ng[l % 2]
            k.dma(k.sp, b_[:], D["bmodT"][l], (), [b_])
            k.dma(k.sp, n_[:], D["normgT"][l], (), [n_])
            ps = g.PS[l % 2]
            wl = wsrc[l].rearrange("(k p) c -> p k c", p=128)
            for pc in range(18):
                w = wm[pc % 3]
                k.dma(k.pool, w[:].rearrange("p (k c) -> p k c", k=8), wl[:, :, pc * 512:(pc + 1) * 512], (), [w])
                for q4 in range(4):
                    q = pc * 4 + q4
                    for kk in range(8):
                        k.mm(ps[:, q * 2:q * 2 + 2], w[:, kk * 512 + q4 * 128: kk * 512 + (q4 + 1) * 128],
                             sc[:, kk * 2:kk * 2 + 2], kk == 0, kk == 7, [w, sc], [ps])
            mv = g.modv[l]
            mv3 = mv[:].rearrange("p (q c) -> p q c", c=2)
            ps3 = ps[:, 0:144].rearrange("p (q c) -> p q c", c=2)
            for cond in range(2):
                k.tt(mv3[:, :, cond], ps3[:, :, cond], b_[:], ALU.add, [ps, b_], [mv])
            for k3 in range(3):
                q0 = (3 * k3 + 1) * 8
                for cond in range(2):
                    k.stt(mv3[:, q0:q0 + 8, cond], mv3[:, q0:q0 + 8, cond], 1.0, n_[:, k3 * 8:(k3 + 1) * 8],
                          ALU.add, ALU.mult, [mv, n_], [mv])
            for k3 in (0, 2):
                q0 = (3 * k3 + 2) * 8
                k.ts(mv[:, q0 * 2:(q0 + 8) * 2], mv[:, q0 * 2:(q0 + 8) * 2], 0.5, None, ALU.mult, reads=[mv], writes=[mv])
        k.barrier()


def phase_norm(g, l, kn, h):
    k, nc, D = g.k, g.nc, g.D
    with ExitStack() as es:
        xt = [k.sb(es, "nx%d" % i, [128, 8 * 512], F32) for i in range(2)]
        sq = [k.sb(es, "nsq%d" % i, [128, 512], BF16) for i in range(4)]
        rts = [k.sb(es, "nrt%d" % i, [128, 512], F32) for i in range(2)]
        rss = [k.sb(es, "nrs%d" % i, [128, 512], F32) for i in range(2)]
        tmp = [k.sb(es, "ntmp%d" % i, [128, 512], F32) for i in range(4)]
        for tb in range(NTB):
            cond = 0 if tb < 2 else 1
            x = xt[tb % 2]
            xbase = g.D["yT"] if (0, tb) in g.xwritten else g.D["xT"]
            assert all((((m, tb) in g.xwritten) == ((0, tb) in g.xwritten)) for m in range(8))
            k.dma(k.sp, x[:].rearrange("p (m t) -> p m t", m=8), xbase[tb], [g.XR[m][tb] for m in range(8)], [x])
            ps = g.PS[6 + tb % 2]
            rt = rts[tb % 2]
            rs = rss[tb % 2]
            for m in range(8):
                s = sq[m % 4]
                xm = x[:, m * 512:(m + 1) * 512]
                if m % 2 == 0:
                    k.tt(s[:], xm, xm, ALU.mult, [x], [s], E=k.pool)
                else:
                    k.activation(s[:], xm, AF.Square, [x], [s])
                k.mm(ps[:], g.ones_ms[:], s[:], m == 0, m == 7, [s], [ps])
            k.activation(rt[:], ps[:], AF.Ln, [ps], [rt], bias=g.eps_t[:, 0:1])
            k.activation(rs[:], rt[:], AF.Exp, [rt], [rs], scale=-0.5)
            for m in range(8):
                t = tmp[m % 4]
                k.tt(t[:], x[:, m * 512:(m + 1) * 512], rs[:], ALU.mult, [x, rs], [t], E=(k.pool if m % 4 == 3 else k.dve))
                k.activation(h[m][:, tb * 512:(tb + 1) * 512], t[:], AF.Identity, [t], [h[m]],
                             bias=mvec(g, l, 3 * kn, m, cond), scale=mvec(g, l, 3 * kn + 1, m, cond))
        k.barrier()


def resid_update(g, l, gate_kind, m, tb, po, xo, xn):
    k, D = g.k, g.D
    cond = 0 if tb < 2 else 1
    k.dma(k.sp, xo[:], xsrc_ap(g, m, tb), [g.XR[m][tb]], [xo])
    k.stt(xn[:], po[:], mvec(g, l, gate_kind, m, cond), xo[:], ALU.mult, ALU.add, [po, xo], [xn])
    g.xwritten.add((m, tb))
    k.dma(k.pool, D["yT"][tb, :, m, :], xn[:], [xn], [g.XR[m][tb]])


def phase_ffn(g, l, w):
    k, nc, D = g.k, g.nc, g.D
    kn = 0 if w == 0 else 2
    with ExitStack() as es:
        h = [k.sb(es, "h%d" % m, [128, NT], BF16) for m in range(8)]
        phase_norm(g, l, kn, h)
        act = [k.sb(es, "a%d" % j, [128, NT], BF16) for j in range(11)]
        wt = [k.sb(es, "fw%d" % i, [128, 2048], BF16) for i in range(2)]
        sa = [k.sb(es, "fsa%d" % i, [128, 512], F32) for i in range(2)]
        wdt = [k.sb(es, "fwd%d" % i, [128, 11 * 128], BF16) for i in range(2)]
        xo = [k.sb(es, "fxo%d" % i, [128, 512], F32) for i in range(4)]
        xn = [k.sb(es, "fxn%d" % i, [128, 512], F32) for i in range(4)]
        it = 0
        it2 = 0

        def load_wgu(j):
            k.dma(k.pool, wt[j % 2][:], D["wgu"][l, w, j], (), [wt[j % 2]])

        def load_wd(half, m):
            k.dma(k.pool, wdt[m % 2][:], D["wd"][l, w, m][:, half * 1408:(half + 1) * 1408], (), [wdt[m % 2]])

        load_wgu(0)
        for half in range(2):
            for jj in range(11):
                j = half * 11 + jj
                wj = wt[j % 2]
                if jj < 10:
                    load_wgu(j + 1)
                else:
                    load_wd(half, 0)
                for tb in range(NTB):
                    pa = g.PS[(it % 2) * 2]
                    pu = g.PS[(it % 2) * 2 + 1]
                    s = sa[it % 2]
                    it += 1
                    for kk in range(8):
                        k.mm(pa[:], wj[:, kk * 128:(kk + 1) * 128], h[kk][:, tb * 512:(tb + 1) * 512],
                             kk == 0, kk == 7, [wj, h[kk]], [pa])
                    for kk in range(8):
                        k.mm(pu[:], wj[:, 1024 + kk * 128:1024 + (kk + 1) * 128], h[kk][:, tb * 512:(tb + 1) * 512],
                             kk == 0, kk == 7, [wj, h[kk]], [pu])
                    k.activation(s[:], pa[:], AF.Silu, [pa], [s])
                    k.tt(act[jj][:, tb * 512:(tb + 1) * 512], s[:], pu[:], ALU.mult, [s, pu], [act[jj]])
            for m in range(8):
                wm_ = wdt[m % 2]
                if m < 7:
                    load_wd(half, m + 1)
                elif half == 0:
                    load_wgu(11)
                for tb in range(NTB):
                    po = g.PS[4 + it2 % 2]
                    for jj in range(11):
                        k.mm(po[:], wm_[:, jj * 128:(jj + 1) * 128], act[jj][:, tb * 512:(tb + 1) * 512],
                             jj == 0, jj == 10, [wm_, act[jj]], [po])
                    resid_update(g, l, 3 * kn + 2, m, tb, po, xo[it2 % 4], xn[it2 % 4])
                    it2 += 1
        k.barrier()


def phase_proj(g, l):
    k, nc, D = g.k, g.nc, g.D
    with ExitStack() as es:
        h = [k.sb(es, "h%d" % m, [128, NT], BF16) for m in range(8)]
        phase_norm(g, l, 1, h)
        wt = [k.sb(es, "pw%d" % i, [128, 1024], BF16) for i in range(2)]
        st = [k.sb(es, "pst%d" % i, [128, NT], F32) for i in range(2)]
        wtm = k.sb(es, "pwtm", [128, 8192], BF16)
        st2 = [k.sb(es, "pst2%d" % i, [128, 1024], F32) for i in range(2)]
        for q in range(4):
            k.dma(k.pool, wtm[:, q * 2048:(q + 1) * 2048], D["wtm"][l][:, q * 2048:(q + 1) * 2048], (), [wtm])
        it = 0
        for cc in range(NCC):
            wj = wt[cc % 2]
            k.dma(k.pool, wj[:], D["win"][l, cc], (), [wj])
            s = st[cc % 2]
            for tb in range(NTB):
                ps = g.PS[it % 4]
                for kk in range(8):
                    k.mm(ps[:], wj[:, kk * 128:(kk + 1) * 128], h[kk][:, tb * 512:(tb + 1) * 512],
                         kk == 0, kk == 7, [wj, h[kk]], [ps])
                k.copy(s[:, tb * 512:(tb + 1) * 512], ps[:], [ps], [s], E=(k.act if it % 2 == 0 else k.dve))
                it += 1
            k.dma(k.sp, D["PJ"][cc], s[:], [s], [g.PJR[cc]])
        for tt in range(NT // 128):
            s2 = st2[tt % 2]
            for hf in range(2):
                ps = g.PS[4 + it % 4]
                for kk in range(8):
                    k.mm(ps[:], h[kk][:, tt * 128:(tt + 1) * 128], wtm[:, kk * 1024 + hf * 512: kk * 1024 + (hf + 1) * 512],
                         kk == 0, kk == 7, [wtm, h[kk]], [ps])
                k.copy(s2[:, hf * 512:(hf + 1) * 512], ps[:], [ps], [s2], E=(k.act if it % 2 == 0 else k.dve))
                it += 1
            k.dma(k.sp, D["PT"][tt * 128:(tt + 1) * 128, :], s2[:], [s2], [g.PTR])
        k.barrier()


def phase_out(g, l):
    k, nc, D = g.k, g.nc, g.D
    with ExitStack() as es:
        wo = k.sb(es, "wo", [128, 8192], BF16)
        wsrc = D["w_out"][l].rearrange("(k p) c -> p k c", p=128)
        for kk in range(8):
            k.dma(k.pool, wo[:, kk * 1024:(kk + 1) * 1024], wsrc[:, kk, :], (), [wo])
        ot = [k.sb(es, "oo%d" % i, [128, 8 * 512], BF16) for i in range(2)]
        xo = [k.sb(es, "oxo%d" % i, [128, 512], F32) for i in range(4)]
        xn = [k.sb(es, "oxn%d" % i, [128, 512], F32) for i in range(4)]
        it = 0
        for tb in range(NTB):
            o = ot[tb % 2]
            for kk in range(8):
                k.dma(k.sp, o[:, kk * 512:(kk + 1) * 512], D["O"][kk, :, tb * 512:(tb + 1) * 512], [g.OR[kk]], [o])
            for m in range(8):
                po = g.PS[it % 4]
                for kk in range(8):
                    k.mm(po[:], wo[:, kk * 1024 + m * 128: kk * 1024 + (m + 1) * 128], o[:, kk * 512:(kk + 1) * 512],
                         kk == 0, kk == 7, [wo, o], [po])
                resid_update(g, l, 5, m, tb, po, xo[it % 4], xn[it % 4])
                it += 1
        k.barrier()

SEQS = [(0, 256, False), (256, 256, False), (512, 256, False), (768, 256, False), (1024, 2048, True)]
GROUPS = [(0, 1024, False, [(0, 256), (256, 256), (512, 256), (768, 256)]), (1024, 2048, True, [(0, 2048)])]
MAXKEY = 1.0 - 1e-6


def mixer_input_shapes():
    return {
        "ropeB": [2, 96, 2048], "ropeD": [2, 128, 2048], "rotB": [96, 96], "rotD": [128, 128], "bd32": [128, 128], "bd64": [128, 128], "esel": [32, 96],
        "wuq": [DEPTH, 128, 768], "wukv": [DEPTH, 128, 512], "gains": [DEPTH, 128, 8], "dlam": [DEPTH, 32, 4],
        "cmlaT": [DEPTH, 160, 512], "cdkT": [DEPTH, 256, 512], "cdv": [DEPTH, 512, 256],
        "s0": [DEPTH, 2, 4, 64, 64], "lbl": [64, 32], "gon": [DEPTH, 64, 1], "mask4": [2, 128, 256],
        "maskbd": [2, 128, 128], "ident": [64, 64],
        **hyena_input_shapes(),
    }


def extra_scratch():
    return hyena_scratch()


def _rope_tables(rot_dim, ngroups_before):
    n_f = rot_dim // 4
    inv = (10000.0 ** (-np.arange(n_f, dtype=np.float32) / n_f)).astype(np.float32)
    t = np.arange(2048)
    ang_r = (t // 64).astype(np.float32)[:, None] * inv
    ang_c = (t % 64).astype(np.float32)[:, None] * inv
    half = rot_dim // 2
    C = np.zeros((rot_dim, 2048), np.float32)
    S_ = np.zeros((rot_dim, 2048), np.float32)
    for d in range(rot_dim):
        ang = ang_r if d < half else ang_c
        i = (d % half) % n_f
        C[d] = np.cos(ang[:, i])
        S_[d] = np.sin(ang[:, i])
    R = np.zeros((rot_dim, rot_dim), np.float32)
    for d in range(rot_dim):
        dd = d % half
        if dd < n_f:
            R[d + n_f, d] = -1.0
        else:
            R[d - n_f, d] = 1.0
    return C, S_, R


def mixer_host_shared(inp):
    f = np.float32
    S = {}
    C, Sn, R = _rope_tables(32, 0)
    ropeB = np.zeros((2, 96, 2048), f)
    ropeB[0, :64] = 1.0
    ropeB[0, 64:] = C
    ropeB[1, 64:] = Sn
    S["ropeB"] = ropeB
    S["ropeD"] = np.ascontiguousarray(np.tile(np.stack([C, Sn]).astype(f), (1, 4, 1)))
    rotB = np.zeros((96, 96), f)
    rotB[64:, 64:] = R
    S["rotB"] = rotB
    S["rotD"] = np.kron(np.eye(4, dtype=f), R.astype(f))
    S["bd32"] = np.kron(np.eye(4, dtype=f), np.full((32, 32), 1.0 / 32.0, f))
    S["bd64"] = np.kron(np.eye(2, dtype=f), np.full((64, 64), 1.0 / 64.0, f))
    es_ = np.zeros((32, 96), f)
    es_[np.arange(32), 64 + np.arange(32)] = 1.0
    S["esel"] = es_
    S["wuq"] = np.ascontiguousarray(inp["mla_w_uq"].reshape(DEPTH, 2, 128, 384).transpose(0, 2, 1, 3).reshape(DEPTH, 128, 768), dtype=f)
    S["wukv"] = np.ascontiguousarray(inp["mla_w_ukv"], dtype=f)
    gains = np.zeros((DEPTH, 128, 8), f)
    gains[:, :, 0:2] = inp["mla_q_norm"].reshape(DEPTH, 2, 128).transpose(0, 2, 1)
    gains[:, :, 2] = inp["mla_kv_norm"]
    gains[:, :96, 3:5] = inp["mla_qk_norm"].transpose(0, 2, 1)
    gains[:, :, 5:7] = np.tile(inp["diff_qk_norm"].transpose(0, 2, 1), (1, 4, 1))
    gains[:, :64, 7] = inp["diff_subln"]
    S["gains"] = gains
    S["dlam"] = np.ascontiguousarray(inp["diff_lambda"].transpose(0, 2, 1), dtype=f)
    lb = inp["hgrn_lb_logits"].reshape(2, DEPTH, 4, 64)
    S["lbl"] = np.ascontiguousarray(lb.transpose(3, 0, 2, 1).reshape(64, 32), dtype=f)
    S["gon"] = np.ascontiguousarray(inp["hgrn_onorm"].reshape(DEPTH, 64, 1), dtype=f)
    s_ = np.arange(128)
    m4 = np.zeros((2, 128, 4, 64), f)
    for j in range(4):
        m4[0, s_ // 32 == j, j, :] = 1.0
        m4[1, s_ // 32 == 3 - j, j, :] = 1.0
    S["mask4"] = m4.reshape(2, 128, 256)
    same = (s_[:, None] // 32) == (s_[None, :] // 32)
    S["maskbd"] = np.stack([same & (s_[:, None] <= s_[None, :]), same & (s_[:, None] >= s_[None, :])]).astype(f)
    S["ident"] = np.eye(64, dtype=f)
    S.update(hyena_host_shared(inp))
    return S


def mixer_host_core(inp, core):
    f = np.float32
    C = {}
    C["cmlaT"] = np.ascontiguousarray(inp["cache_mla"][core].transpose(0, 2, 1), dtype=f)
    C["cdkT"] = np.ascontiguousarray(inp["cache_diff_k"][core].reshape(DEPTH, 512, 256).transpose(0, 2, 1), dtype=f)
    C["cdv"] = np.ascontiguousarray(inp["cache_diff_v"][core].reshape(DEPTH, 512, 256), dtype=f)
    C["s0"] = np.ascontiguousarray(inp["state_hgrn"][core], dtype=f)
    return C


def setup_consts(g, es):
    k, D = g.k, g.D
    g.ones = {}
    for gs in (256, 128, 96, 64, 32):
        t = k.sb(es, "ones%d" % gs, [128, 128], BF16)
        k.memset(t[:], 1.0 / gs, [t])
        g.ones[gs] = t
    g.rotB = k.sb(es, "rotB", [96, 96], BF16)
    g.rotD = k.sb(es, "rotD", [128, 128], BF16)
    g.bd32 = k.sb(es, "bd32", [128, 128], BF16)
    k.dma(k.pool, g.bd32[:], D["bd32"], (), [g.bd32])
    g.bd64 = k.sb(es, "bd64", [128, 128], BF16)
    k.dma(k.pool, g.bd64[:], D["bd64"], (), [g.bd64])
    g.esel = k.sb(es, "esel", [32, 96], BF16)
    k.dma(k.pool, g.rotB[:], D["rotB"], (), [g.rotB])
    k.dma(k.pool, g.rotD[:], D["rotD"], (), [g.rotD])
    k.dma(k.pool, g.esel[:], D["esel"], (), [g.esel])
    g.onesf = k.sb(es, "onesf", [32, 64], F32)
    k.memset(g.onesf[:], 1.0, [g.onesf])
    hgrn_setup(g, es)
    hyena_setup(g, es)


def pj_rows(g, r0, n, t0, nt):
    flat = g.D["PJ"].rearrange("c p t -> (c p) t")
    res = [g.PJR[c] for c in range(r0 // 128, (r0 + n - 1) // 128 + 1)]
    return flat[r0:r0 + n, t0:t0 + nt], res


def gnr_gen(g, W, src, srcres, gs, n, gain, rope, out, outres, rope_off=0, f32out=None, f32res=None, onesT=None,
                    gres=()):
    k, nc = g.k, g.nc
    sq, rs, t1, t2, pn, pr = W["sq"], W["rs"], W["t1"], W["t2"], W["pn"], W["pr"]
    k.activation(sq[0:gs, 0:n], src, AF.Square, srcres, [sq])
    yield
    oT = onesT if onesT is not None else g.ones[gs]
    k.mm(pn[0:gs, 0:n], oT[0:gs, 0:gs], sq[0:gs, 0:n], True, True, [sq, oT], [pn])
    yield
    k.activation(rs[0:gs, 0:n], pn[0:gs, 0:n], AF.Ln, [pn], [rs], bias=g.eps_t[0:gs, 0:1])
    yield
    k.activation(rs[0:gs, 0:n], rs[0:gs, 0:n], AF.Exp, [rs], [rs], scale=-0.5)
    yield
    if rope is None:
        if f32out is not None:
            k.stt(f32out, src, gain, rs[0:gs, 0:n], ALU.mult, ALU.mult, list(srcres) + [rs] + list(gres), f32res)
            yield
            k.copy(out, f32out, f32res, outres, E=k.pool)
            yield
        else:
            k.stt(out, src, gain, rs[0:gs, 0:n], ALU.mult, ALU.mult, list(srcres) + [rs] + list(gres), outres)
            yield
        return
    rot, Ct, St = rope
    k.stt(t1[0:gs, 0:n], src, gain, rs[0:gs, 0:n], ALU.mult, ALU.mult, list(srcres) + [rs] + list(gres), [t1])
    yield
    k.copy(t2[0:gs, 0:n], t1[0:gs, 0:n], [t1], [t2], E=k.act)
    yield
    k.mm(pr[0:gs, 0:n], rot[0:gs, 0:gs], t2[0:gs, 0:n], True, True, [t2], [pr])
    yield
    k.tt(t1[0:gs, 0:n], t1[0:gs, 0:n], Ct[0:gs, rope_off:rope_off + n], ALU.mult, [t1, Ct], [t1], E=k.pool)
    yield
    k.tt(rs[0:gs, 0:n], pr[0:gs, 0:n], St[0:gs, rope_off:rope_off + n], ALU.mult, [pr, St], [rs])
    yield
    k.tt(out, t1[0:gs, 0:n], rs[0:gs, 0:n], ALU.add, [t1, rs], outres, E=k.pool)
    yield


def group_norm_rope(*a, **kw):
    for _ in gnr_gen(*a, **kw):
        pass


def run_interleaved(gens):
    gens = list(gens)
    while gens:
        for g_ in list(gens):
            try:
                next(g_)
            except StopIteration:
                gens.remove(g_)


def attn_pipeline(g, W, streams, nk, nq):
    k = g.k
    nkt = nk // 128
    items = [(st, kt) for kt in range(nkt) for st in streams]
    NB = 3
    LA = 2

    def emit_sc(i):
        st, kt = items[i]
        sc = g.PS[i % NB]
        dk_ = st["dk"]
        kc = st.get("kt0", 0) + kt
        k.mm(sc[:, 0:nq], st["KT"][0:dk_, kc * 128:(kc + 1) * 128], st["QT"][0:dk_, st["q0"]:st["q0"] + nq], True, True,
             [st["KT"], st["QT"]], [sc])

    for i in range(min(LA, len(items))):
        emit_sc(i)
    for i in range(len(items)):
        if i + LA < len(items):
            emit_sc(i + LA)
        st, kt = items[i]
        sc = g.PS[i % NB]
        pt = W["pt"][i % NB]
        k.activation(pt[:, 0:nq], sc[:, 0:nq], AF.Exp, [sc], [pt], scale=st["scale"])
        h = st["hcol"]
        kc = st.get("kt0", 0) + kt
        k.mm(st["acc"][:, 0:nq], st["VA"][:, kc * 512 + h * 128: kc * 512 + (h + 1) * 128], pt[:, 0:nq], kt == 0, kt == nkt - 1,
             [st["VA"], pt], [st["acc"]])


def phase_mla(g, l):
    k, nc, D = g.k, g.nc, g.D
    scale = 96.0 ** -0.5
    with ExitStack() as es:
        g.ropeB = [k.sb(es, "ropeB%d" % i, [96, 2048], F32) for i in range(2)]
        for i in range(2):
            k.dma(k.sp, g.ropeB[i][:], D["ropeB"][i], (), [g.ropeB[i]])
        gn = k.sb(es, "gn", [128, 8], F32)
        k.dma(k.sp, gn[:], D["gains"][l], (), [gn])
        wuq = k.sb(es, "wuq", [128, 768], BF16)
        k.dma(k.pool, wuq[:], D["wuq"][l], (), [wuq])
        wkv = k.sb(es, "wkv", [128, 512], BF16)
        k.dma(k.pool, wkv[:], D["wukv"][l], (), [wkv])
        wkp = k.sb(es, "wkp", [128, 4 * 96], BF16)
        wv = k.sb(es, "wv", [128, 256], BF16)
        k.memset(wkp[:], 0.0, [wkp])
        for h in range(4):
            k.copy(wkp[:, h * 96:h * 96 + 64], wkv[:, h * 128:h * 128 + 64], [wkv], [wkp])
            k.copy(wv[:, h * 64:(h + 1) * 64], wkv[:, h * 128 + 64:(h + 1) * 128], [wkv], [wv])
        QT = [k.sb(es, "QT%d" % h, [96, 2048], BF16) for h in range(4)]
        KT = [k.sb(es, "KT%d" % h, [96, 2560], BF16) for h in range(4)]
        VA = k.sb(es, "VA", [128, 20 * 512], BF16)
        k.memset(VA[:], 1.0, [VA])
        W = {"sq": k.sb(es, "wsq", [128, 512], BF16), "rs": k.sb(es, "wrs", [128, 512], F32),
             "t1": k.sb(es, "wt1", [128, 512], F32), "t2": k.sb(es, "wt2", [128, 512], BF16),
             "pn": g.PS[3], "pr": g.PS[3], "pt": [k.sb(es, "wpt%d" % i, [128, 512], BF16) for i in range(3)]}
        WB = {"sq": k.sb(es, "wsqB", [128, 512], BF16), "rs": k.sb(es, "wrsB", [128, 512], F32),
              "t1": k.sb(es, "wt1B", [128, 512], F32), "t2": k.sb(es, "wt2B", [128, 512], BF16),
              "pn": g.PS[2], "pr": g.PS[2]}
        Ws = [W, WB]
        pqs = [g.PS[4], g.PS[6]]
        cq = k.sb(es, "cq", [128, 1024], F32)
        cqn = k.sb(es, "cqn", [128, 1024], BF16)
        ckv = k.sb(es, "ckv", [128, 512], F32)
        ckn = k.sb(es, "ckn", [128, 512], F32)
        ckb = k.sb(es, "ckb", [128, 512], BF16)
        kr = k.sb(es, "kr", [32, 512], F32)
        krb = k.sb(es, "krb", [32, 512], BF16)
        rd = [k.sb(es, "rd%d" % i, [64, 512], F32) for i in range(2)]
        ob = [k.sb(es, "ob%d" % i, [64, 512], BF16) for i in range(4)]
        pq = g.PS[4]
        pv = g.PS[5]
        oi = 0

        def keys_block(ckb_, krb_, n, kcol, rope_off):
            rope = None if rope_off is None else (g.rotB, g.ropeB[0], g.ropeB[1])

            def kchain(h, Wx, pq_):
                k.mm(pq_[0:96, 0:n], wkp[:, h * 96:(h + 1) * 96], ckb_[:, 0:n], True, False, [wkp, ckb_], [pq_])
                yield
                k.mm(pq_[0:96, 0:n], g.esel[:, :], krb_[:, 0:n], False, True, [krb_], [pq_])
                yield
                yield from gnr_gen(g, Wx, pq_[0:96, 0:n], [pq_], 96, n, gn[0:96, 4:5], rope, KT[h][:, kcol:kcol + n], [KT[h]],
                                   rope_off=rope_off or 0, gres=[gn])

            for hp in range(2):
                run_interleaved([kchain(2 * hp + j, Ws[j], pqs[j]) for j in range(2)])
            for tt in range(n // 128):
                k.mm(pv[:, 0:256], ckb_[:, tt * 128:(tt + 1) * 128], wv[:, :], True, True, [ckb_, wv], [pv])
                kt = kcol // 128 + tt
                k.copy(VA[:, kt * 512:(kt + 1) * 512].rearrange("p (h c) -> p h c", h=4)[:, :, 0:64],
                       pv[:, 0:256].rearrange("p (h c) -> p h c", h=4), [pv], [VA])

        for (t0, L, ctx, seqs) in GROUPS:
            koff = 512 if ctx else 0
            if ctx:
                k.dma(k.sp, ckn[:], D["cmlaT"][l, 0:128, :], (), [ckn])
                k.dma(k.sp, kr[:], D["cmlaT"][l, 128:160, :], (), [kr])
                k.copy(ckb[:], ckn[:], [ckn], [ckb])
                k.copy(krb[:], kr[:], [kr], [krb])
                keys_block(ckb, krb, 512, 0, None)
            for b0 in range(0, L, 512):
                n = min(512, L - b0)
                tg = t0 + b0
                for m in range(2):
                    ap_, rr = pj_rows(g, R_CQ + m * 128, 128, tg, n)
                    k.dma(k.sp, cq[:, m * 512:m * 512 + n], ap_, rr, [cq])
                ap_, rr = pj_rows(g, R_CKV, 128, tg, n)
                k.dma(k.sp, ckv[:, 0:n], ap_, rr, [ckv])
                ap_, rr = pj_rows(g, R_KR, 32, tg, n)
                k.dma(k.sp, kr[:, 0:n], ap_, rr, [kr])
                pn = W["pn"]
                for m in range(2):
                    k.activation(W["sq"][:, 0:n], cq[:, m * 512:m * 512 + n], AF.Square, [cq], [W["sq"]])
                    k.mm(pn[:, 0:n], g.ones[256][:, :], W["sq"][:, 0:n], m == 0, m == 1, [W["sq"]], [pn])
                k.activation(W["rs"][:, 0:n], pn[:, 0:n], AF.Ln, [pn], [W["rs"]], bias=g.eps_t[:, 0:1])
                k.activation(W["rs"][:, 0:n], W["rs"][:, 0:n], AF.Exp, [W["rs"]], [W["rs"]], scale=-0.5)
                for m in range(2):
                    k.stt(cqn[:, m * 512:m * 512 + n], cq[:, m * 512:m * 512 + n], gn[:, m:m + 1], W["rs"][:, 0:n],
                          ALU.mult, ALU.mult, [cq, W["rs"], gn], [cqn])
                group_norm_rope(g, W, ckv[:, 0:n], [ckv], 128, n, gn[:, 2:3], None, ckb[:, 0:n], [ckb],
                                f32out=ckn[:, 0:n], f32res=[ckn], gres=[gn])
                k.copy(krb[:, 0:n], kr[:, 0:n], [kr], [krb])
                if not ctx:
                    k.dma(k.sp, D["mlaT"][l, 0:128, tg:tg + n], ckn[:, 0:n], [ckn], [])
                    k.dma(k.sp, D["mlaT"][l, 128:160, tg:tg + n], kr[:, 0:n], [kr], [])
                rope_q = (g.rotB, g.ropeB[0], g.ropeB[1]) if ctx else None

                def qchain(h, Wx, pq_):
                    for m in range(2):
                        k.mm(pq_[0:96, 0:n], wuq[:, m * 384 + h * 96: m * 384 + (h + 1) * 96], cqn[:, m * 512:m * 512 + n],
                             m == 0, m == 1, [wuq, cqn], [pq_])
                        yield
                    yield from gnr_gen(g, Wx, pq_[0:96, 0:n], [pq_], 96, n, gn[0:96, 3:4], rope_q, QT[h][:, b0:b0 + n], [QT[h]],
                                       rope_off=b0, gres=[gn])

                for hp in range(2):
                    run_interleaved([qchain(2 * hp + j, Ws[j], pqs[j]) for j in range(2)])
                keys_block(ckb, krb, n, koff + b0, b0 if ctx else None)
            for (s0, Ls) in seqs:
                nk = Ls + koff
                kt0 = 0 if ctx else s0 // 128
                for q0 in range(0, Ls, 512):
                    nq = min(512, Ls - q0)
                    for hp in range(2):
                        base = 4 + 2 * (oi % 2)
                        oi += 1
                        sts = []
                        for j in range(2):
                            h = hp * 2 + j
                            sts.append(dict(KT=KT[h], QT=QT[h], dk=96, q0=s0 + q0, kt0=kt0, VA=VA, hcol=h, scale=scale,
                                            acc=g.PS[base + j]))
                        attn_pipeline(g, W, sts, nk, nq)
                        for j in range(2):
                            h = hp * 2 + j
                            acc = sts[j]["acc"]
                            o_ = ob[(oi * 2 + j) % 4]
                            k.op(k.dve, lambda: nc.vector.reciprocal(out=rd[j][:, 0:nq], in_=acc[64:128, 0:nq]), [acc], [rd[j]])
                            k.tt(o_[:, 0:nq], acc[0:64, 0:nq], rd[j][:, 0:nq], ALU.mult, [acc, rd[j]], [o_])
                            r0 = 256 + h * 64
                            tq = t0 + s0 + q0
                            k.dma(k.sp, D["O"][r0 // 128, r0 % 128:r0 % 128 + 64, tq:tq + nq], o_[:, 0:nq], [o_],
                                  [g.OR[r0 // 128]])
        k.barrier()


def phase_diff(g, l):
    k, nc, D = g.k, g.nc, g.D
    scale = 32.0 ** -0.5
    lam_init = 0.8 - 0.6 * float(np.exp(-0.3 * l))
    with ExitStack() as es:
        g.ropeD = [k.sb(es, "ropeD%d" % i, [128, 2048], F32) for i in range(2)]
        for i in range(2):
            k.dma(k.sp, g.ropeD[i][:], D["ropeD"][i], (), [g.ropeD[i]])
        gn = k.sb(es, "gn", [128, 8], F32)
        k.dma(k.sp, gn[:], D["gains"][l], (), [gn])
        dl = k.sb(es, "dl", [32, 4], F32)
        k.dma(k.sp, dl[:], D["dlam"][l], (), [dl])
        pr2 = k.sb(es, "pr2", [32, 2], F32)
        k.tt(pr2[:, 0:1], dl[:, 0:1], dl[:, 1:2], ALU.mult, [dl], [pr2])
        k.tt(pr2[:, 1:2], dl[:, 2:3], dl[:, 3:4], ALU.mult, [dl], [pr2])
        pl = g.PS[2]
        k.mm(pl[0:64, 0:2], g.onesf[:, :], pr2[:, :], True, True, [pr2, g.onesf], [pl])
        lam = k.sb(es, "lam", [64, 4], F32)
        k.activation(lam[:, 0:2], pl[0:64, 0:2], AF.Exp, [pl], [lam])
        k.stt(lam[:, 2:3], lam[:, 1:2], -lam_init, lam[:, 0:1], ALU.add, ALU.subtract, [lam], [lam])
        k.ts(lam[:, 3:4], gn[0:64, 7:8], 1.0 - lam_init, None, ALU.mult, reads=[gn], writes=[lam])
        QT = [k.sb(es, "dQ%d" % i, [128, 2048], BF16) for i in range(2)]
        KT = [k.sb(es, "dK%d" % i, [128, 2560], BF16) for i in range(2)]
        QP = [k.sb(es, "dQP%d" % i, [128, 2048], BF16) for i in range(8)]
        for i in range(8):
            k.memset(QP[i][:], 0.0, [QP[i]], E=(k.pool if i % 2 else k.dve))
        VA = k.sb(es, "dVA", [128, 20 * 512], BF16)
        k.memset(VA[:], 1.0, [VA])
        W = {"sq": k.sb(es, "wsq", [128, 512], BF16), "rs": k.sb(es, "wrs", [128, 512], F32),
             "t1": k.sb(es, "wt1", [128, 512], F32), "t2": k.sb(es, "wt2", [128, 512], BF16),
             "pn": g.PS[3], "pr": g.PS[3], "pt": [k.sb(es, "wpt%d" % i, [128, 512], BF16) for i in range(8)]}
        xin128 = [k.sb(es, "dxin128%d" % i, [128, 512], F32) for i in range(2)]
        WB = {"sq": k.sb(es, "wsqB", [128, 512], BF16), "rs": k.sb(es, "wrsB", [128, 512], F32),
              "t1": k.sb(es, "wt1B", [128, 512], F32), "t2": k.sb(es, "wt2B", [128, 512], BF16),
              "pn": g.PS[2], "pr": g.PS[2]}
        Ws = [W, WB]
        kf = k.sb(es, "dkf", [128, 512], F32)
        vin = [k.sb(es, "dvin%d" % i, [128, 256], F32) for i in range(2)]
        rd = [k.sb(es, "rd%d" % i, [64, 512], F32) for i in range(4)]
        oo = [k.sb(es, "doo%d" % i, [64, 512], F32) for i in range(4)]
        ob = [k.sb(es, "dob%d" % i, [64, 512], BF16) for i in range(2)]
        oi = 0
        xi = 0
        pti = 0
        for (t0, L, ctx, seqs) in GROUPS:
            koff = 512 if ctx else 0
            rope = (g.rotD, g.ropeD[0], g.ropeD[1]) if ctx else None
            if ctx:
                for c2 in range(2):
                    x_ = xin128[xi % 2]
                    xi += 1
                    k.dma(k.sp, x_[:], D["cdkT"][l, c2 * 128:(c2 + 1) * 128, :], (), [x_])
                    k.copy(KT[c2][:, 0:512], x_[:], [x_], [KT[c2]])
                for tt in range(4):
                    v_ = vin[tt % 2]
                    k.dma(k.sp, v_[:], D["cdv"][l, tt * 128:(tt + 1) * 128, :], (), [v_])
                    k.copy(VA[:, tt * 512:(tt + 1) * 512].rearrange("p (h c) -> p h c", h=4)[:, :, 0:64],
                           v_[:].rearrange("p (h c) -> p h c", h=4), [v_], [VA])
            for b0 in range(0, L, 512):
                n = min(512, L - b0)
                tg = t0 + b0
                def dchain(c2, isk):
                    x_ = xin128[isk]
                    Wx = Ws[isk]
                    ap_, rr = pj_rows(g, (R_DK if isk else R_DQ) + c2 * 128, 128, tg, n)
                    k.dma(k.sp, x_[:, 0:n], ap_, rr, [x_])
                    yield
                    dst = KT[c2] if isk else QT[c2]
                    off = (koff + b0) if isk else b0
                    if isk and not ctx:
                        yield from gnr_gen(g, Wx, x_[:, 0:n], [x_], 128, n, gn[:, 6:7], None, dst[:, off:off + n], [dst],
                                           f32out=kf[:, 0:n], f32res=[kf], onesT=g.bd32, gres=[gn])
                        k.dma(k.sp, D["dkT"][l, c2 * 128:(c2 + 1) * 128, tg:tg + n], kf[:, 0:n], [kf], [])
                        yield
                    else:
                        yield from gnr_gen(g, Wx, x_[:, 0:n], [x_], 128, n, gn[:, 5 + isk:6 + isk], rope, dst[:, off:off + n], [dst],
                                           rope_off=b0, onesT=g.bd32, gres=[gn])
                    if not isk:
                        for j in range(4):
                            k.copy(QP[c2 * 4 + j][32 * j:32 * j + 32, b0:b0 + n], QT[c2][32 * j:32 * j + 32, b0:b0 + n],
                                   [QT[c2]], [QP[c2 * 4 + j]], E=(k.pool if j % 2 else k.act))
                            yield

                for c2 in range(2):
                    run_interleaved([dchain(c2, 0), dchain(c2, 1)])
                for tt in range(n // 128):
                    v_ = vin[tt % 2]
                    k.dma(k.sp, v_[:], D["PT"][tg + tt * 128:tg + (tt + 1) * 128, 768:1024], [g.PTR], [v_])
                    kt = (koff + b0) // 128 + tt
                    k.copy(VA[:, kt * 512:(kt + 1) * 512].rearrange("p (h c) -> p h c", h=4)[:, :, 0:64],
                           v_[:].rearrange("p (h c) -> p h c", h=4), [v_], [VA])
            if not ctx:
                ap_, rr = pj_rows(g, R_DV, 256, t0, L)
                k.dma(k.sp, D["dvT"][l, :, t0:t0 + L], ap_, rr, [])
            for (s0, Ls) in seqs:
              nkt = (Ls + koff) // 128
              kt0 = 0 if ctx else s0 // 128
              for h in range(4):
                c2 = h // 2
                rb = 64 * (h % 2)
                for q0 in range(0, Ls, 512):
                    nq = min(512, Ls - q0)
                    qc = s0 + q0
                    base = 4 + 2 * (oi % 2)
                    accs = [g.PS[base], g.PS[base + 1]]

                    def emit_sc(kt):
                        kc = kt0 + kt
                        for m in range(2):
                            sc = g.PS[(kt % 2) * 2 + m]
                            qp = QP[h * 2 + m]
                            k.mm(sc[:, 0:nq], KT[c2][:, kc * 128:(kc + 1) * 128], qp[:, qc:qc + nq],
                                 True, True, [KT[c2], qp], [sc])

                    emit_sc(0)
                    for kt in range(nkt):
                        if kt + 1 < nkt:
                            emit_sc(kt + 1)
                        kc = kt0 + kt
                        for m in range(2):
                            sc = g.PS[(kt % 2) * 2 + m]
                            pt = W["pt"][pti % 8]
                            pti += 1
                            k.activation(pt[:, 0:nq], sc[:, 0:nq], AF.Exp, [sc], [pt], scale=scale)
                            k.mm(accs[m][:, 0:nq], VA[:, kc * 512 + h * 128: kc * 512 + (h + 1) * 128], pt[:, 0:nq],
                                 kt == 0, kt == nkt - 1, [VA, pt], [accs[m]])
                    for m in range(2):
                        k.op(k.dve, lambda: nc.vector.reciprocal(out=rd[m][:, 0:nq], in_=accs[m][64:128, 0:nq]), [accs[m]], [rd[m]])
                        k.tt(oo[m][:, 0:nq], accs[m][0:64, 0:nq], rd[m][:, 0:nq], ALU.mult, [accs[m], rd[m]], [oo[m]])
                    o0, o1 = oo[0], oo[1]
                    o_ = ob[oi % 2]
                    oi += 1
                    k.stt(o0[:, 0:nq], o1[:, 0:nq], lam[:, 2:3], o0[:, 0:nq], ALU.mult, ALU.add, [o0, o1, lam], [o0])
                    group_norm_rope(g, W, o0[:, 0:nq], [o0], 64, nq, lam[:, 3:4], None, o_[:, 0:nq], [o_], gres=[lam])
                    r0 = 768 + h * 64
                    tq = t0 + s0 + q0
                    k.dma(k.sp, D["O"][r0 // 128, r0 % 128:r0 % 128 + 64, tq:tq + nq], o_[:, 0:nq], [o_],
                          [g.OR[r0 // 128]])
        k.barrier()


def phase_zero(g, chunks):
    k, D = g.k, g.D
    with ExitStack() as es:
        z = k.sb(es, "z", [128, NT], BF16)
        k.memset(z[:], 0.0, [z])
        for kk in chunks:
            k.dma(k.sp, D["O"][kk], z[:], [z], [g.OR[kk]])
        k.barrier()


def phase_mixers(g, l, mixers):
    if "a" in mixers:
        phase_hgrn(g, l)
    else:
        phase_zero(g, [0, 1])
    if "b" in mixers:
        phase_mla(g, l)
    else:
        phase_zero(g, [2, 3])
    if "c" in mixers:
        phase_hyena(g, l)
    else:
        phase_zero(g, [4, 5])
    if "d" in mixers:
        phase_diff(g, l)
    else:
        phase_zero(g, [6, 7])


def hgrn_setup(g, es):
    k, nc, D = g.k, g.nc, g.D
    lbl = k.sb(es, "lbl", [64, 32], F32)
    k.dma(k.sp, lbl[:], D["lbl"], (), [lbl])
    e = k.sb(es, "lbe", [64, 32], F32)
    k.activation(e[:], lbl[:], AF.Exp, [lbl], [e])
    sm = k.sb(es, "lbs", [64, 8], F32)
    e3 = e[:].rearrange("p (a l) -> p a l", l=4)
    k.op(k.dve, lambda: nc.vector.tensor_reduce(out=sm[:], in_=e3, axis=AX.X, op=ALU.add), [e], [sm])
    k.op(k.dve, lambda: nc.vector.reciprocal(out=sm[:], in_=sm[:]), [sm], [sm])
    k.tt(e3, e3, sm[:].unsqueeze(2).to_broadcast([64, 8, 4]), ALU.mult, [e, sm], [e])
    g.oml = k.sb(es, "oml", [64, 32], F32)
    o3 = g.oml[:].rearrange("p (a l) -> p a l", l=4)
    k.memset(g.oml[:], 1.0, [g.oml])
    for l in range(1, 4):
        k.tt(o3[:, :, l], o3[:, :, l - 1], e3[:, :, l], ALU.subtract, [g.oml, e], [g.oml])
    g.mask4 = [k.sb(es, "mask4%d" % i, [128, 256], BF16) for i in range(2)]
    g.maskbd = [k.sb(es, "maskbd%d" % i, [128, 128], F32) for i in range(2)]
    for i in range(2):
        k.dma(k.pool, g.mask4[i][:], D["mask4"][i], (), [g.mask4[i]])
        k.dma(k.sp, g.maskbd[i][:], D["maskbd"][i], (), [g.maskbd[i]])
    g.ident = k.sb(es, "ident", [64, 64], BF16)
    k.dma(k.pool, g.ident[:], D["ident"], (), [g.ident])
    g.rmask = k.sb(es, "rmask", [128, 2048], F32)
    k.memset(g.rmask[:], 1.0, [g.rmask])
    k.memset(g.rmask[:].rearrange("p (a b) -> p a b", b=32)[:, :, 0:1], 0.0, [g.rmask])
    g.one_t = k.sb(es, "one_t", [128, 1], F32)
    k.memset(g.one_t[:], 1.0, [g.one_t])


def phase_hgrn(g, l):
    k, nc, D = g.k, g.nc, g.D
    for (t0, nseq, Ls, ctx) in ((0, 4, 256, False), (1024, 1, 2048, True)):
        L = nseq * Ls
        nch = L // 32
        nchs = Ls // 32
        ntl = L // 128
        nsl = nseq * (nchs + 1)

        def slot_of(i):
            return (i // nchs) * (nchs + 1) + 1 + (i % nchs)

        with ExitStack() as es:
            f32t = lambda nm: k.sb(es, nm, [128, L], F32)
            af, og, qs, kk, cum, e1, of = [f32t(n_) for n_ in ("af", "og", "qs", "kk", "cum", "e1", "of")]
            aq = e1
            gl = af
            qt, kt, k2 = [k.sb(es, n_, [128, L], BF16) for n_ in ("qt", "kt", "k2")]
            el = k.sb(es, "el", [128, nch], F32)
            k2T = k.sb(es, "k2T", [128, ntl * 128], BF16)
            vin = [k.sb(es, "hvin%d" % i, [128, 256], F32) for i in range(2)]
            vt = k.sb(es, "vt", [128, ntl * 256], BF16)
            V4 = [k.sb(es, "V4%d" % i, [128, 256], BF16) for i in range(4)]
            U3, A3, S3 = [k.sb(es, n_, [128, 64 * nsl], F32) for n_ in ("U3", "A3", "S3")]
            S16 = k.sb(es, "S16", [128, 64 * nsl], BF16)
            AT = [k.sb(es, "AT%d" % i, [128, 128], BF16) for i in range(4)]
            s0t = k.sb(es, "s0t", [128, 64], F32)
            fin = [k.sb(es, "fin%d" % i, [128, 64], F32) for i in range(2)]
            gon = k.sb(es, "gon", [128, 1], F32)
            k.dma(k.sp, gon[0:64, :], D["gon"][l], (), [gon])
            k.dma(k.sp, gon[64:128, :], D["gon"][l], (), [gon])
            omlp = k.sb(es, "omlp", [128, 4], F32)
            for hp in range(2):
                for d in range(2):
                    for j in range(2):
                        col = (d * 4 + hp * 2 + j) * 4 + l
                        k.copy(omlp[64 * j:64 * j + 64, hp * 2 + d:hp * 2 + d + 1], g.oml[:, col:col + 1], [g.oml], [omlp])
            obf = [k.sb(es, "hob%d" % i, [128, 512], BF16) for i in range(2)]
            W = {"sq": k.sb(es, "wsq", [128, 512], BF16), "rs": k.sb(es, "wrs", [128, 512], F32),
                 "t1": k.sb(es, "wt1", [128, 512], F32), "t2": k.sb(es, "wt2", [128, 512], BF16),
                 "pn": g.PS[2], "pr": g.PS[3]}
            v3 = lambda t_: t_[:].rearrange("p (d s) -> p d s", s=nsl)
            c3 = lambda t_: t_[:].rearrange("p (c j) -> p c j", j=32)
            cnt = 0
            for tt in range(ntl):
                v_ = vin[tt % 2]
                k.dma(k.sp, v_[:], D["PT"][t0 + tt * 128:t0 + (tt + 1) * 128, 0:256], [g.PTR], [v_])
                k.copy(vt[:, tt * 256:(tt + 1) * 256], v_[:], [v_], [vt], E=(k.act if tt % 2 else k.dve))
            for hp in range(2):
                ap_, rr = pj_rows(g, R_AQ + hp * 128, 128, t0, L)
                k.dma(k.sp, aq[:], ap_, rr, [aq])
                ap_, rr = pj_rows(g, R_AOG + hp * 128, 128, t0, L)
                k.dma(k.sp, og[:], ap_, rr, [og])
                k.activation(qs[:], aq[:], AF.Silu, [aq], [qs])
                for d in range(2):
                    ap_, rr = pj_rows(g, (R_AFF if d == 0 else R_AFB) + hp * 128, 128, t0, L)
                    k.dma(k.sp, af[:], ap_, rr, [af])
                    k.activation(kk[:], af[:], AF.Sigmoid, [af], [kk], scale=-1.0)
                    k.ts(kk[:], kk[:], omlp[:, hp * 2 + d:hp * 2 + d + 1], MAXKEY, ALU.mult, ALU.min, reads=[kk, omlp], writes=[kk])
                    k.activation(gl[:], kk[:], AF.Ln, [kk], [gl], scale=-1.0, bias=g.one_t[:, 0:1])
                    if d == 0:
                        k.op(k.dve, lambda: nc.vector.tensor_tensor_scan(out=cum[:], data0=g.rmask[:, 0:L], data1=gl[:],
                             initial=0.0, op0=ALU.mult, op1=ALU.add), [gl, g.rmask], [cum])
                        totv = c3(cum)[:, :, 31]
                    else:
                        k.op(k.dve, lambda: nc.vector.tensor_tensor_scan(out=cum[:, ::-1], data0=g.rmask[:, 0:L],
                             data1=gl[:, ::-1], initial=0.0, op0=ALU.mult, op1=ALU.add), [gl, g.rmask], [cum])
                        totv = c3(cum)[:, :, 0]
                    k.activation(e1[:], cum[:], AF.Exp, [cum], [e1])
                    k.stt(qt[:], e1[:], 0.125, qs[:], ALU.mult, ALU.mult, [e1, qs], [qt])
                    k.activation(e1[:], cum[:], AF.Exp, [cum], [e1], scale=-1.0)
                    k.tt(kt[:], kk[:], e1[:], ALU.mult, [kk, e1], [kt], E=k.pool)
                    k.activation(el[:], totv, AF.Exp, [cum], [el])
                    k.tt(c3(e1), totv.unsqueeze(2).to_broadcast([128, nch, 32]), c3(cum), ALU.subtract, [cum], [e1])
                    k.activation(e1[:], e1[:], AF.Exp, [e1], [e1])
                    k.tt(k2[:], kk[:], e1[:], ALU.mult, [kk, e1], [k2], E=k.pool)
                    for tt in range(ntl):
                        pT = g.PS[tt % 2]
                        k.mm(pT[:, 0:128], k2[:, tt * 128:(tt + 1) * 128], g.ident128[:, :], True, True, [k2, g.ident128], [pT])
                        k.copy(k2T[:, tt * 128:(tt + 1) * 128], pT[:, 0:128], [pT], [k2T], E=(k.act if tt % 2 else k.dve))
                    init_sl = slice(0, nsl, nchs + 1)
                    if ctx:
                        for j in range(2):
                            k.dma(k.sp, s0t[64 * j:64 * j + 64, :], D["s0"][l, d, hp * 2 + j], (), [s0t])
                        k.copy(v3(U3)[:, :, 0], s0t[:], [s0t], [U3])
                    else:
                        k.memset(v3(U3)[:, :, init_sl], 0.0, [U3])
                    k.memset(v3(A3)[:, :, init_sl], 0.0, [A3])
                    elv = el[:] if d == 0 else el[:, ::-1]
                    for sp_ in range(nseq):
                        s_lo = sp_ * (nchs + 1) + 1
                        k.copy(v3(A3)[:, :, s_lo:s_lo + nchs],
                               elv[:, sp_ * nchs:(sp_ + 1) * nchs].unsqueeze(1).to_broadcast([128, 64, nchs]), [el], [A3],
                               E=k.pool)
                    for tt in range(ntl):
                        i0 = 4 * tt if d == 0 else nch - 4 - 4 * tt
                        s_lo = slot_of(i0)
                        for j in range(2):
                            h = hp * 2 + j
                            v4 = V4[(tt * 2 + j) % 4]
                            k.tt(v4[:].rearrange("p (j c) -> p j c", j=4),
                                 vt[:, tt * 256 + h * 64: tt * 256 + (h + 1) * 64].unsqueeze(1).to_broadcast([128, 4, 64]),
                                 g.mask4[d][:].rearrange("p (j c) -> p j c", j=4), ALU.mult, [vt, g.mask4[d]], [v4],
                                 E=k.pool)
                            pu = g.PS[4 + j]
                            k.mm(pu[:, 0:256], k2T[:, tt * 128:(tt + 1) * 128], v4[:], True, True, [k2T, v4], [pu])
                            k.copy(v3(U3)[64 * j:64 * j + 64, :, s_lo:s_lo + 4].rearrange("p d s -> p s d"),
                                   pu[64 * j:64 * j + 64, 0:256].rearrange("p (s d) -> p s d", s=4), [pu], [U3],
                                   E=(k.act if j else k.dve))
                    k.op(k.dve, lambda: nc.vector.tensor_tensor_scan(out=S3[:], data0=A3[:], data1=U3[:], initial=0.0,
                         op0=ALU.mult, op1=ALU.add), [A3, U3], [S3])
                    k.copy(S16[:], S3[:], [S3], [S16], E=k.act)
                    if not ctx:
                        for sp_ in range(nseq):
                            si = sp_ if d == 0 else nseq - 1 - sp_
                            f_ = fin[cnt % 2]
                            cnt += 1
                            k.copy(f_[:], v3(S3)[:, :, sp_ * (nchs + 1) + nchs], [S3], [f_])
                            for j in range(2):
                                k.dma(k.sp, D["hg"][l, si, d, hp * 2 + j], f_[64 * j:64 * j + 64, :], [f_], [])
                    for tt in range(ntl):
                        for j in range(2):
                            h = hp * 2 + j
                            hb = 64 * j
                            pa = g.PS[j]
                            at = AT[(tt * 2 + j) % 4]
                            k.mm(pa[:, 0:128], kt[hb:hb + 64, tt * 128:(tt + 1) * 128], qt[hb:hb + 64, tt * 128:(tt + 1) * 128],
                                 True, True, [kt, qt], [pa])
                            k.tt(at[:], pa[:, 0:128], g.maskbd[d][:], ALU.mult, [pa, g.maskbd[d]], [at])
                        for j in range(2):
                            h = hp * 2 + j
                            hb = 64 * j
                            at = AT[(tt * 2 + j) % 4]
                            po = g.PS[6 + j]
                            cb = (tt % 4) * 128
                            k.mm(po[0:64, cb:cb + 128], vt[:, tt * 256 + h * 64: tt * 256 + (h + 1) * 64], at[:], True, False,
                                 [vt, at], [po])
                            for jj in range(4):
                                c = 4 * tt + jj
                                i = c if d == 0 else nch - 1 - c
                                k.mm(po[0:64, cb + 32 * jj:cb + 32 * jj + 32], v3(S16)[hb:hb + 64, :, slot_of(i) - 1],
                                     qt[hb:hb + 64, c * 32:(c + 1) * 32], False, jj == 3, [S16, qt], [po])
                            if tt % 4 == 3 or tt == ntl - 1:
                                b0 = (tt // 4) * 512
                                n = cb + 128
                                if d == 0:
                                    k.copy(of[hb:hb + 64, b0:b0 + n], po[0:64, 0:n], [po], [of], E=k.act)
                                else:
                                    k.tt(of[hb:hb + 64, b0:b0 + n], of[hb:hb + 64, b0:b0 + n], po[0:64, 0:n], ALU.add, [of, po], [of])
                k.activation(og[:], og[:], AF.Silu, [og], [og])
                for b0 in range(0, L, 512):
                    n = min(512, L - b0)
                    o_ = obf[(b0 // 512) % 2]
                    group_norm_rope(g, W, of[:, b0:b0 + n], [of], 128, n, gon[:, 0:1], None, W["t2"][:, 0:n], [W["t2"]],
                                    f32out=e1[:, b0:b0 + n], f32res=[e1], onesT=g.bd64, gres=[gon])
                    k.tt(o_[:, 0:n], e1[:, b0:b0 + n], og[:, b0:b0 + n], ALU.mult, [e1, og], [o_])
                    k.dma(k.sp, D["O"][hp, :, t0 + b0:t0 + b0 + n], o_[:, 0:n], [o_], [g.OR[hp]])
            k.barrier()


HYN = ((256, "P", SEQS[0:4]), (2048, "S", SEQS[4:5]))
TWO_PI = 6.283185307179586


def hyena_input_shapes():
    s = {"hw1": [DEPTH, 17, 64], "hb1": [DEPTH, 64, 1], "hw2": [DEPTH, 64, 64], "hb2": [DEPTH, 64, 1],
         "hw3": [DEPTH, 64, 1024], "hld": [DEPTH, 128, 1024], "hsT": [DEPTH, 128, 18], "hbT": [DEPTH, 128, 4],
         "hmsk": [128, 4], "ident128": [128, 128]}
    for n, sfx, _ in HYN:
        tbw = min(512, n)
        s["feats" + sfx] = [17, n]
        s["negtn" + sfx] = [128, n // 128]
        s["Fr" + sfx] = [2 * n // 128, 128, n]
        s["Gr" + sfx] = [n // tbw, 128, (2 * n // 128) * tbw]
        s["Fq" + sfx] = [2, 128, n]
    return s


def hyena_scratch():
    return [("HS" + sfx, [n // 128, 3, 128, 512], F32) for n, sfx, _ in HYN]


def hyena_host_shared(inp):
    f = np.float32
    S = {}
    S["hw1"] = np.ascontiguousarray(inp["hy_w1"], dtype=f)
    S["hb1"] = np.ascontiguousarray(inp["hy_b1"].reshape(DEPTH, 64, 1), dtype=f)
    S["hw2"] = np.ascontiguousarray(inp["hy_w2"], dtype=f)
    S["hb2"] = np.ascontiguousarray(inp["hy_b2"].reshape(DEPTH, 64, 1), dtype=f)
    S["hw3"] = np.ascontiguousarray(inp["hy_w3"], dtype=f)
    S["hld"] = np.ascontiguousarray(np.broadcast_to(inp["hy_log_decay"].reshape(DEPTH, 1, 1024), (DEPTH, 128, 1024)), dtype=f)
    hs = inp["hy_short"].reshape(DEPTH, 3, 6, 128)
    S["hsT"] = np.ascontiguousarray(hs.transpose(0, 3, 2, 1).reshape(DEPTH, 128, 18), dtype=f)
    hb = inp["hy_bias"].reshape(DEPTH, 2, 2, 128)
    S["hbT"] = np.ascontiguousarray(hb.transpose(0, 3, 1, 2).reshape(DEPTH, 128, 4), dtype=f)
    msk = np.ones((128, 4), f)
    msk[0, 1] = 0.0
    msk[:, 2] = 0.0
    msk[0, 2] = 1.0
    msk[:, 3] = -1.0
    S["hmsk"] = msk
    S["ident128"] = np.eye(128, dtype=f)
    for n, sfx, _ in HYN:
        tn = (np.arange(n, dtype=f) / f(n)).astype(f)
        ang = (f(2.0 * np.pi) * tn[:, None] * np.arange(1, 9, dtype=f)).astype(f)
        feats = np.concatenate([tn[:, None], np.cos(ang), np.sin(ang)], axis=-1).astype(f)
        S["feats" + sfx] = np.ascontiguousarray(feats.T)
        S["negtn" + sfx] = np.ascontiguousarray((-tn).reshape(n // 128, 128).T)
        t = np.arange(n, dtype=np.float64)[:, None]
        fr = np.arange(n, dtype=np.float64)[None, :]
        th = np.pi * t * fr / n
        Fm = np.concatenate([np.cos(th), -np.sin(th)], axis=1)
        Fm[:, n] = (-1.0) ** np.arange(n)
        Fr = Fm.reshape(n // 128, 128, 2 * n // 128, 128).transpose(2, 1, 0, 3).reshape(2 * n // 128, 128, n)
        S["Fr" + sfx] = np.ascontiguousarray(Fr.astype(f).astype(ml_dtypes.bfloat16))
        fq = np.zeros((2, 128, n // 128, 128), np.float64)
        fq[0] = Fr[n // 128].reshape(128, n // 128, 128)
        fq[1, :, :, 0] = fq[0, :, :, 0]
        fq[0, :, :, 0] = 0.0
        S["Fq" + sfx] = np.ascontiguousarray(fq.reshape(2, 128, n).astype(f).astype(ml_dtypes.bfloat16))
        Gm = np.concatenate([np.cos(th.T), -np.sin(th.T)], axis=0) * (2.0 / (2 * n))
        Gm[0, :] = 1.0 / (2 * n)
        Gm[n, :] = ((-1.0) ** np.arange(n)) / (2 * n)
        tbw = min(512, n)
        Gr = Gm.reshape(2 * n // 128, 128, n // tbw, tbw).transpose(2, 1, 0, 3).reshape(n // tbw, 128, (2 * n // 128) * tbw)
        S["Gr" + sfx] = np.ascontiguousarray(Gr.astype(f).astype(ml_dtypes.bfloat16))
    return S


def hyena_setup(g, es):
    k, D = g.k, g.D
    g.hmsk = k.sb(es, "hmsk", [128, 4], F32)
    k.dma(k.sp, g.hmsk[:], D["hmsk"], (), [g.hmsk])
    g.ident128 = k.sb(es, "ident128", [128, 128], BF16)
    k.dma(k.pool, g.ident128[:], D["ident128"], (), [g.ident128])
    g.onesF = k.sb(es, "onesF", [128, 128], F32)
    k.memset(g.onesF[:], 1.0, [g.onesF])
    g.HSR = {sfx: [Res() for _ in range(n // 128)] for n, sfx, _ in HYN}


def hyena_filters(g, l, n, sfx):
    k, nc, D = g.k, g.nc, g.D
    ntl = n // 128
    npair = n // 128
    nfc = 2 * npair
    with ExitStack() as es:
        w1 = k.sb(es, "hw1", [17, 64], F32)
        b1 = k.sb(es, "hb1", [64, 1], F32)
        w2 = k.sb(es, "hw2", [64, 64], F32)
        b2 = k.sb(es, "hb2", [64, 1], F32)
        w3 = k.sb(es, "hw3", [64, 1024], F32)
        eld = k.sb(es, "eld", [128, 1024], F32)
        ft = k.sb(es, "feat", [17, n], F32)
        ntn = k.sb(es, "ntn", [128, ntl], F32)
        for t_, nm in ((w1, "hw1"), (b1, "hb1"), (w2, "hw2"), (b2, "hb2"), (w3, "hw3"), (eld, "hld")):
            k.dma(k.sp, t_[:], D[nm][l], (), [t_])
        k.dma(k.sp, ft[:], D["feats" + sfx], (), [ft])
        k.dma(k.sp, ntn[:], D["negtn" + sfx], (), [ntn])
        k.activation(eld[:], eld[:], AF.Exp, [eld], [eld])
        h1 = k.sb(es, "h1", [64, n], F32)
        h2 = k.sb(es, "h2", [64, n], F32)
        y = k.sb(es, "hy", [64, 512], F32)
        kq = k.sb(es, "hkq", [64, 512], F32)
        ti = k.sb(es, "hti", [64, 512], I32)
        FBS = k.sb(es, "FBS", [128, ntl * 512], BF16)
        FBD = k.sb(es, "FBD", [128, ntl * 512], BF16)
        dec = k.sb(es, "dec", [128, 512], F32)
        fls = [k.sb(es, "fl%d" % i, [128, 512], F32) for i in range(2)]
        ab = k.sb(es, "ab", [128, 512], F32)
        rn = k.sb(es, "rn", [128, 512], F32)
        ReH = k.sb(es, "ReH", [128, npair * 512], F32)
        Fi = [k.sb(es, "Fi%d" % i, [128, n], BF16) for i in range(3)]
        Bt = [k.sb(es, "Bt%d" % i, [128, 512], F32) for i in range(2)]
        Ct = k.sb(es, "Ct", [128, 512], F32)

        def sin_layer(w, K_, rhs, bias, out):
            for b0 in range(0, n, 512):
                nb = min(512, n - b0)
                ps = g.PS[(b0 // 512) % 2]
                k.mm(ps[0:64, 0:nb], w[0:K_, :], rhs[0:K_, b0:b0 + nb], True, True, [w, rhs], [ps])
                k.ts(y[:, 0:nb], ps[0:64, 0:nb], bias[:, 0:1], None, ALU.add, reads=[ps, bias], writes=[y])
                k.ts(kq[:, 0:nb], y[:, 0:nb], 1.0 / TWO_PI, None, ALU.mult, reads=[y], writes=[kq])
                k.copy(ti[:, 0:nb], kq[:, 0:nb], [kq], [ti])
                k.copy(kq[:, 0:nb], ti[:, 0:nb], [ti], [kq])
                k.stt(y[:, 0:nb], kq[:, 0:nb], -TWO_PI, y[:, 0:nb], ALU.mult, ALU.add, [kq, y], [y])
                k.ts(y[:, 0:nb], y[:, 0:nb], -3.141592, 3.141592, ALU.max, ALU.min, reads=[y], writes=[y])
                k.activation(out[:, b0:b0 + nb], y[:, 0:nb], AF.Sin, [y], [out])

        sin_layer(w1, 17, ft, b1, h1)
        sin_layer(w2, 64, h1, b2, h2)
        accs = [g.PS[4], g.PS[5]]
        for tt in range(ntl):
            for hf in range(2):
                ps = g.PS[2 + hf]
                fl = fls[hf]
                k.mm(ps[:, :], h2[:, tt * 128:(tt + 1) * 128], w3[:, hf * 512:(hf + 1) * 512], True, True, [h2, w3], [ps])
                k.activation(dec[:], eld[:, hf * 512:(hf + 1) * 512], AF.Exp, [eld, ntn], [dec], scale=ntn[:, tt:tt + 1])
                k.tt(fl[:], ps[:, :], dec[:], ALU.mult, [ps, dec], [fl])
                if hf == 1 and tt == 0:
                    k.ts(fl[:], fl[:], g.hmsk[:, 1:2], None, ALU.mult, reads=[fl, g.hmsk], writes=[fl])
                k.activation(ab[:], fl[:], AF.Abs, [fl], [ab])
                k.mm(accs[hf][:, :], g.onesF[:, :], ab[:], tt == 0, tt == ntl - 1, [ab, g.onesF], [accs[hf]])
            k.tt(FBS[:, tt * 512:(tt + 1) * 512], fls[0][:], fls[1][:], ALU.add, [fls[0], fls[1]], [FBS], E=k.pool)
            k.tt(FBD[:, tt * 512:(tt + 1) * 512], fls[0][:], fls[1][:], ALU.subtract, [fls[0], fls[1]], [FBD])
        k.copy(rn[:], accs[0][:, :], [accs[0]], [rn], E=k.act)
        k.tt(rn[:], rn[:], accs[1][:, :], ALU.add, [rn, accs[1]], [rn])
        k.ts(rn[:], rn[:], EPS, None, ALU.add, reads=[rn], writes=[rn])
        k.op(k.dve, lambda: nc.vector.reciprocal(out=rn[:], in_=rn[:]), [rn], [rn])
        Fq = [k.sb(es, "Fq%d" % i, [128, n], BF16) for i in range(2)]
        for i in range(2):
            k.dma(k.sp, Fq[i][:], D["Fq" + sfx][i], (), [Fq[i]])
        for i in range(nfc):
            ip = i % npair
            isim = i >= npair
            ps = g.PS[i % 2]
            if i == npair:
                for kk in range(ntl):
                    k.mm(ps[:, :], Fq[0][:, kk * 128:(kk + 1) * 128], FBD[:, kk * 512:(kk + 1) * 512], kk == 0, False,
                         [Fq[0], FBD], [ps])
                for kk in range(ntl):
                    k.mm(ps[:, :], Fq[1][:, kk * 128:(kk + 1) * 128], FBS[:, kk * 512:(kk + 1) * 512], False, kk == ntl - 1,
                         [Fq[1], FBS], [ps])
            else:
                F_ = Fi[i % 3]
                k.dma(k.sp, F_[:], D["Fr" + sfx][i], (), [F_])
                FB_ = FBD if isim else FBS
                for kk in range(ntl):
                    k.mm(ps[:, :], F_[:, kk * 128:(kk + 1) * 128], FB_[:, kk * 512:(kk + 1) * 512], kk == 0, kk == ntl - 1,
                         [F_, FB_], [ps])
            Re_ = ReH[:, ip * 512:(ip + 1) * 512]
            if not isim:
                k.tt(Re_, ps[:, :], rn[:], ALU.mult, [ps, rn], [ReH])
            else:
                B_ = Bt[ip % 2]
                k.tt(B_[:], ps[:, :], rn[:], ALU.mult, [ps, rn], [B_])
                hs_ = D["HS" + sfx][ip]
                hr = g.HSR[sfx][ip]
                if ip > 0:
                    k.dma(k.pool, hs_[0], Re_, [ReH], [hr])
                    k.dma(k.pool, hs_[1], B_[:], [B_], [hr])
                    k.dma(k.pool, hs_[2], Re_, [ReH], [hr])
                else:
                    k.ts(Ct[:], Re_, g.hmsk[:, 1:2], None, ALU.mult, reads=[ReH, g.hmsk], writes=[Ct])
                    k.stt(Ct[:], B_[:], g.hmsk[:, 2:3], Ct[:], ALU.mult, ALU.add, [B_, Ct, g.hmsk], [Ct])
                    k.ts(B_[:], B_[:], g.hmsk[:, 1:2], None, ALU.mult, reads=[B_, g.hmsk], writes=[B_])
                    k.dma(k.pool, hs_[0], Re_, [ReH], [hr])
                    k.dma(k.pool, hs_[1], B_[:], [B_], [hr])
                    k.dma(k.pool, hs_[2], Ct[:], [Ct], [hr])
        k.barrier()


def hyena_convs(g, l, n, sfx, seqs):
    k, nc, D = g.k, g.nc, g.D
    ntl = n // 128
    npair = n // 128
    nfc = 2 * npair
    tbw = min(512, n)
    ntb = n // tbw
    with ExitStack() as es:
        hs = k.sb(es, "hsT", [128, 18], F32)
        hb = k.sb(es, "hbT", [128, 4], F32)
        k.dma(k.sp, hs[:], D["hsT"][l], (), [hs])
        k.dma(k.sp, hb[:], D["hbT"][l], (), [hb])
        U = k.sb(es, "hU", [128, 2 * n], F32)
        X = k.sb(es, "hX", [128, 2 * n], F32)
        xin = k.sb(es, "hxin", [128, n], F32)
        ubf = k.sb(es, "hubf", [128, 2 * n], BF16)
        utm = k.sb(es, "hutm", [128, ntl * 256], BF16)
        Yt = k.sb(es, "hYt", [128, nfc * 256], BF16)
        Gts = [k.sb(es, "hGt%d" % i, [128, nfc * tbw], BF16) for i in range(2)]
        Fi = [k.sb(es, "hFi%d" % i, [128, n], BF16) for i in range(4)]
        Hs = [k.sb(es, "hHs%d" % i, [128, 768], F32) for i in range(2)]
        vre = k.sb(es, "hvre", [128, 256], F32)
        vim = k.sb(es, "hvim", [128, 256], F32)
        t1 = k.sb(es, "ht1", [128, 256], F32)
        t2 = k.sb(es, "ht2", [128, 256], F32)
        ob = [k.sb(es, "hob%d" % i, [128, 512], BF16) for i in range(2)]

        def short_conv(dst, ch0, t0):
            for cc in range(2):
                ap_, rr = pj_rows(g, R_CV + (ch0 + cc) * 128, 128, t0, n)
                k.dma(k.sp, xin[:], ap_, rr, [xin])
                c3 = (ch0 + cc) * 3
                d_ = dst[:, cc * n:(cc + 1) * n]
                k.ts(d_, xin[:], hs[:, c3 + 1:c3 + 2], None, ALU.mult, reads=[xin, hs], writes=[dst])
                k.stt(d_[:, 1:n], xin[:, 0:n - 1], hs[:, c3:c3 + 1], d_[:, 1:n], ALU.mult, ALU.add, [xin, hs, dst], [dst])
                k.stt(d_[:, 0:n - 1], xin[:, 1:n], hs[:, c3 + 2:c3 + 3], d_[:, 0:n - 1], ALU.mult, ALU.add, [xin, hs, dst], [dst])

        def long_conv(o, t0, last):
            k.copy(ubf[:], U[:], [U], [ubf], E=k.act)
            for tt in range(ntl):
                for cc in range(2):
                    pT = g.PS[(tt * 2 + cc) % 2]
                    k.mm(pT[:, 0:128], ubf[:, cc * n + tt * 128: cc * n + (tt + 1) * 128], g.ident128[:, :], True, True,
                         [ubf, g.ident128], [pT])
                    k.copy(utm[:, tt * 256 + cc * 128: tt * 256 + (cc + 1) * 128], pT[:, 0:128], [pT], [utm],
                           E=(k.act if cc else k.dve))
            for ip in range(npair):
                Fre = Fi[(ip % 2) * 2]
                Fim = Fi[(ip % 2) * 2 + 1]
                k.dma(k.sp, Fre[:], D["Fr" + sfx][ip], (), [Fre])
                k.dma(k.sp, Fim[:], D["Fr" + sfx][npair + ip], (), [Fim])
                H_ = Hs[ip % 2]
                k.dma(k.sp, H_[:].rearrange("p (a c) -> p a c", a=3),
                      D["HS" + sfx][ip].rearrange("a p c -> p a c")[:, :, o * 256:(o + 1) * 256], [g.HSR[sfx][ip]], [H_])
                pre = g.PS[2 + (ip % 2) * 2]
                pim = g.PS[3 + (ip % 2) * 2]
                for kk in range(ntl):
                    k.mm(pre[:, 0:256], Fre[:, kk * 128:(kk + 1) * 128], utm[:, kk * 256:(kk + 1) * 256], kk == 0, kk == ntl - 1,
                         [Fre, utm], [pre])
                for kk in range(ntl):
                    k.mm(pim[:, 0:256], Fim[:, kk * 128:(kk + 1) * 128], utm[:, kk * 256:(kk + 1) * 256], kk == 0, kk == ntl - 1,
                         [Fim, utm], [pim])
                k.copy(vre[:], pre[:, 0:256], [pre], [vre], E=k.act)
                k.copy(vim[:], pim[:, 0:256], [pim], [vim], E=k.act)
                A_, B_, C_ = H_[:, 0:256], H_[:, 256:512], H_[:, 512:768]
                k.tt(t1[:], vre[:], A_, ALU.mult, [vre, H_], [t1])
                k.tt(t2[:], vim[:], B_, ALU.mult, [vim, H_], [t2], E=k.pool)
                k.tt(Yt[:, ip * 256:(ip + 1) * 256], t1[:], t2[:], ALU.subtract, [t1, t2], [Yt])
                k.tt(t1[:], vre[:], B_, ALU.mult, [vre, H_], [t1])
                k.tt(t2[:], vim[:], C_, ALU.mult, [vim, H_], [t2], E=k.pool)
                k.tt(Yt[:, (npair + ip) * 256:(npair + ip + 1) * 256], t1[:], t2[:], ALU.add, [t1, t2], [Yt])
            oi = 0
            for tb in range(ntb):
                Gt = Gts[tb % 2]
                for q in range(4):
                    w_ = nfc * tbw // 4
                    k.dma(k.sp, Gt[:, q * w_:(q + 1) * w_], D["Gr" + sfx][tb][:, q * w_:(q + 1) * w_], (), [Gt])
                for cc in range(2):
                    py = g.PS[6 + cc]
                    for fc in range(nfc):
                        k.mm(py[:, 0:tbw], Yt[:, fc * 256 + cc * 128: fc * 256 + (cc + 1) * 128], Gt[:, fc * tbw:(fc + 1) * tbw],
                             fc == 0, fc == nfc - 1, [Yt, Gt], [py])
                    sl = slice(cc * n + tb * tbw, cc * n + (tb + 1) * tbw)
                    k.stt(U[:, sl], U[:, sl], hb[:, o * 2 + cc:o * 2 + cc + 1], py[:, 0:tbw], ALU.mult, ALU.add, [U, hb, py], [U])
                    if not last:
                        k.tt(U[:, sl], U[:, sl], X[:, sl], ALU.mult, [U, X], [U])
                    else:
                        o_ = ob[oi % 2]
                        oi += 1
                        k.tt(o_[:, 0:tbw], U[:, sl], X[:, sl], ALU.mult, [U, X], [o_])
                        k.dma(k.pool, D["O"][4 + cc, :, t0 + tb * tbw:t0 + (tb + 1) * tbw], o_[:, 0:tbw], [o_], [g.OR[4 + cc]])

        for (t0, _, ctx) in seqs:
            short_conv(U, 0, t0)
            short_conv(X, 2, t0)
            long_conv(0, t0, False)
            short_conv(X, 4, t0)
            long_conv(1, t0, True)
        k.barrier()


def phase_hyena(g, l):
    for n, sfx, seqs in HYN:
        hyena_filters(g, l, n, sfx)
        hyena_convs(g, l, n, sfx, seqs)

def build(depth=DEPTH, mixers=("a", "b", "c", "d"), debug=False):
    nc = bass.Bass("TRN2", target_bir_lowering=False)
    D = {}

    def din(name, shape, dt=F32):
        D[name] = nc.dram_tensor(name, list(shape), dt, kind="ExternalInput").ap()

    def dout(name, shape, dt=F32):
        D[name] = nc.dram_tensor(name, list(shape), dt, kind="ExternalOutput").ap()

    def dscr(name, shape, dt=F32):
        kind = "ExternalOutput" if debug else "Internal"
        D[name] = nc.dram_tensor(name, list(shape), dt, kind=kind).ap()

    for name, shape in input_shapes().items():
        din(name, shape, BF16 if name[:2] in ("Fr", "Gr", "Fq") else F32)
    dout("yT", [NTB, 128, 8, 512])
    dout("mlaT", [DEPTH, 160, NPR])
    dout("dkT", [DEPTH, 256, NPR])
    dout("dvT", [DEPTH, 256, NPR])
    dout("hg", [DEPTH, 4, 2, 4, 64, 64])
    dscr("PJ", [NCC, 128, NT])
    dscr("PT", [NT, 1024])
    dscr("O", [8, 128, NT], BF16)
    for nm, shp, dt in extra_scratch():
        dscr(nm, shp, dt)

    with ExitStack() as es:
        k = K(nc, es)
        g = Ctx()
        g.k, g.nc, g.D = k, nc, D
        g.PS = [k.psum(es, "ps%d" % i, [128, 512]) for i in range(8)]
        g.modv = [k.sb(es, "modv%d" % l, [128, 144], F32) for l in range(DEPTH)]
        g.XR = [[Res() for _ in range(NTB)] for _ in range(8)]
        g.PJR = [Res() for _ in range(NCC)]
        g.PTR = Res()
        g.OR = [Res() for _ in range(8)]
        g.xwritten = set()
        g.ones_ms = k.sb(es, "ones_ms", [128, 128], BF16)
        k.memset(g.ones_ms[:], 1.0 / 1024.0, [g.ones_ms])
        g.eps_t = k.sb(es, "eps_t", [128, 1], F32)
        k.memset(g.eps_t[:], EPS, [g.eps_t])
        setup_consts(g, es)
        phase_mod(g)
        if debug == "mod":
            for l in range(DEPTH):
                k.dma(k.sp, D["PT"][l * 128:(l + 1) * 128, 0:144], g.modv[l][:], [g.modv[l]], [])
            depth = 0
        for l in range(depth):
            phase_ffn(g, l, 0)
            phase_proj(g, l)
            phase_mixers(g, l, mixers)
            phase_out(g, l)
            phase_ffn(g, l, 1)
        k.finish()
        g.stats = (k.n_inst, k.n_wait)
    return nc, g


def input_shapes():
    s = {
        "xT": [NTB, 128, 8, 512], "cT": [128, 16], "w_mod": [DEPTH, 1024, 9216], "bmodT": [DEPTH, 128, 72],
        "normgT": [DEPTH, 128, 24], "wgu": [DEPTH, 2, NJ, 128, 2048], "wd": [DEPTH, 2, 8, 128, NJ * 128],
        "win": [DEPTH, NCC, 128, 1024], "wtm": [DEPTH, 128, 8192], "w_out": [DEPTH, 1024, 1024],
    }
    s.update(mixer_input_shapes())
    return s


def host_shared(inp):
    f = np.float32
    S = {}
    S["w_mod"] = np.ascontiguousarray(inp["w_mod"], dtype=f)
    S["bmodT"] = np.ascontiguousarray(inp["b_mod"].reshape(DEPTH, 72, 128).transpose(0, 2, 1), dtype=f)
    S["normgT"] = np.ascontiguousarray(inp["norm_g"].reshape(DEPTH, 24, 128).transpose(0, 2, 1), dtype=f)
    wgu = inp["ffn_w_gu"].reshape(DEPTH, 2, 8, 128, 2, NJ, 128)
    S["wgu"] = np.ascontiguousarray(wgu.transpose(0, 1, 5, 3, 4, 2, 6).reshape(DEPTH, 2, NJ, 128, 2048), dtype=f)
    wd = inp["ffn_w_down"].reshape(DEPTH, 2, NJ, 128, 8, 128)
    S["wd"] = np.ascontiguousarray(wd.transpose(0, 1, 4, 3, 2, 5).reshape(DEPTH, 2, 8, 128, NJ * 128), dtype=f)
    win = np.zeros((DEPTH, 1024, NCC * 128), f)
    win[:, :, :INW] = inp["w_in"]
    win = win.reshape(DEPTH, 8, 128, NCC, 128)
    S["win"] = np.ascontiguousarray(win.transpose(0, 3, 2, 1, 4).reshape(DEPTH, NCC, 128, 1024))
    wi = inp["w_in"]
    tm = np.concatenate([wi[:, :, R_AI:R_AI + 256], wi[:, :, R_AFF:R_AFF + 256], wi[:, :, R_AFB:R_AFB + 256],
                         wi[:, :, R_DV:R_DV + 256]], axis=2)
    tm = tm.reshape(DEPTH, 8, 128, 1024).transpose(0, 2, 1, 3)
    S["wtm"] = np.ascontiguousarray(tm.reshape(DEPTH, 128, 8192), dtype=f)
    S["w_out"] = np.ascontiguousarray(inp["w_out"], dtype=f)
    S.update(mixer_host_shared(inp))
    return S


def host_core(inp, core):
    f = np.float32
    C = {}
    xp = inp["x_prompt"][core * 4:(core + 1) * 4].reshape(NPR, 1024)
    xs = inp["x_sample"][core]
    x = np.concatenate([xp, xs], axis=0)
    C["xT"] = np.ascontiguousarray(x.T.reshape(8, 128, NTB, 512).transpose(2, 1, 0, 3), dtype=f)
    cc = np.stack([inp["c_ctx"], inp["c"][core]], axis=1)
    C["cT"] = np.ascontiguousarray(cc.reshape(8, 128, 2).transpose(1, 0, 2).reshape(128, 16), dtype=f)
    C.update(mixer_host_core(inp, core))
    return C


_CACHE = {}


def kernel(**inp):
    inp = {k_: np.asarray(v) for k_, v in inp.items()}
    if "nc" not in _CACHE:
        _CACHE["nc"] = build()[0]
    nc = _CACHE["nc"]
    S = host_shared(inp)
    in_maps = []
    for core in range(8):
        m = dict(S)
        m.update(host_core(inp, core))
        in_maps.append(m)
    res = run_bass_kernel_spmd(nc, in_maps, core_ids=list(range(8)))
    R = res.results
    f = np.float32
    yp = np.zeros((32, 256, 1024), f)
    ys = np.zeros((8, 2048, 1024), f)
    mla = np.zeros((32, DEPTH, 256, 160), f)
    dk = np.zeros((32, DEPTH, 256, 4, 2, 32), f)
    dv = np.zeros((32, DEPTH, 256, 4, 64), f)
    hg = np.zeros((32, DEPTH, 2, 4, 64, 64), f)
    for core in range(8):
        r = R[core]
        y = np.asarray(r["yT"]).reshape(NTB, 128, 8, 512).transpose(2, 1, 0, 3).reshape(1024, NT).T
        yp[core * 4:(core + 1) * 4] = y[:NPR].reshape(4, 256, 1024)
        ys[core] = y[NPR:]
        m_ = np.asarray(r["mlaT"]).reshape(DEPTH, 160, 4, 256)
        mla[core * 4:(core + 1) * 4] = m_.transpose(2, 0, 3, 1)
        k_ = np.asarray(r["dkT"]).reshape(DEPTH, 256, 4, 256)
        dk[core * 4:(core + 1) * 4] = k_.transpose(2, 0, 3, 1).reshape(4, DEPTH, 256, 4, 2, 32)
        v_ = np.asarray(r["dvT"]).reshape(DEPTH, 256, 4, 256)
        dv[core * 4:(core + 1) * 4] = v_.transpose(2, 0, 3, 1).reshape(4, DEPTH, 256, 4, 64)
        h_ = np.asarray(r["hg"])
        hg[core * 4:(core + 1) * 4] = h_.transpose(1, 0, 2, 3, 4, 5)
    return (yp, ys, mla, dk, dv, hg)
```
